# Optimizing a Trainium2 kernel written in Bass

```python
import math
import jax, jax.numpy as jnp
from jax import lax
import numpy as np

D_MODEL = 1024
BATCH = 4
SEQ = 8192
DEPTH = 1
DEC_BATCH = 32
DEC_SEQ = 2048
PAST_LEN = 128

N_FGROUPS = 4
FGROUP_DIM = 64
F_WIDTH = N_FGROUPS * FGROUP_DIM
N_HEADS = 6
QK_NOPE_DIM = 128
QK_ROPE_DIM = 64
V_HEAD_DIM = 128
QK_HEAD_DIM = QK_NOPE_DIM + QK_ROPE_DIM
Q_LORA_RANK = 384
KV_LORA_RANK = 256
ATTN_WIDTH = N_HEADS * V_HEAD_DIM
MIX_WIDTH = F_WIDTH + ATTN_WIDTH
IN_WIDTH = F_WIDTH + Q_LORA_RANK + KV_LORA_RANK + QK_ROPE_DIM
D_FF = ((8 * D_MODEL // 3 + 255) // 256) * 256
ROPE_THETA = 10000.0
EPS = 1e-6
Q_BLOCK = 128
SM_SCALE = 1.0 / math.sqrt(QK_HEAD_DIM)

kernel_name = "fnet_mla_parallel_encoder"


def rms_norm(x, g):
    xf = x.astype(jnp.float32)
    y = xf * lax.rsqrt(jnp.mean(xf * xf, axis=-1, keepdims=True) + EPS)
    return (y * g.astype(jnp.float32)).astype(x.dtype)


def rope_tables(seq_len):
    inv_freq = 1.0 / (ROPE_THETA ** (jnp.arange(0, QK_ROPE_DIM, 2, dtype=jnp.float32) / QK_ROPE_DIM))
    ang = jnp.arange(seq_len, dtype=jnp.float32)[:, None] * inv_freq[None, :]
    return jnp.cos(ang), jnp.sin(ang)


def apply_rope(x, cos, sin):
    xf = x.astype(jnp.float32)
    x1, x2 = jnp.split(xf, 2, axis=-1)
    return jnp.concatenate([x1 * cos - x2 * sin, x2 * cos + x1 * sin], axis=-1).astype(x.dtype)


def fourier_mix(u):
    b, s, _ = u.shape
    ug = u.reshape(b, s, N_FGROUPS, FGROUP_DIM).astype(jnp.float32)
    f = jnp.fft.fft2(ug, axes=(1, 3), norm="ortho").real
    return f.reshape(b, s, F_WIDTH).astype(u.dtype)


def latent_attention(q_nope, q_rope, k_nope, k_rope, v):
    b, s, h, _ = q_nope.shape
    nb = s // Q_BLOCK
    qn = q_nope.reshape(b, nb, Q_BLOCK, h, QK_NOPE_DIM).swapaxes(0, 1)
    qr = q_rope.reshape(b, nb, Q_BLOCK, h, QK_ROPE_DIM).swapaxes(0, 1)

    def block(args):
        qn_b, qr_b = args
        sc = (jnp.einsum('bqhd,bkhd->bhqk', qn_b, k_nope, preferred_element_type=jnp.float32)
              + jnp.einsum('bqhr,bkr->bhqk', qr_b, k_rope, preferred_element_type=jnp.float32)) * SM_SCALE
        p = jax.nn.softmax(sc, axis=-1)
        return jnp.einsum('bhqk,bkhd->bqhd', p.astype(v.dtype), v)

    o = lax.map(block, (qn, qr))
    return o.swapaxes(0, 1).reshape(b, s, h * V_HEAD_DIM)


def token_mixer(xn, w_in, q_norm_g, w_q_up, kv_norm_g, w_kv_up, w_out):
    b, s, _ = xn.shape
    hcat = xn @ w_in
    o1 = F_WIDTH
    o2 = o1 + Q_LORA_RANK
    o3 = o2 + KV_LORA_RANK
    u_f = hcat[..., :o1]
    c_q = hcat[..., o1:o2]
    c_kv = hcat[..., o2:o3]
    k_rope_raw = hcat[..., o3:]
    y_f = fourier_mix(u_f)
    cos, sin = rope_tables(s)
    q = (rms_norm(c_q, q_norm_g) @ w_q_up).reshape(b, s, N_HEADS, QK_HEAD_DIM)
    q_nope = q[..., :QK_NOPE_DIM]
    q_rope = apply_rope(q[..., QK_NOPE_DIM:], cos[:, None, :], sin[:, None, :])
    kv = (rms_norm(c_kv, kv_norm_g) @ w_kv_up).reshape(b, s, N_HEADS, QK_NOPE_DIM + V_HEAD_DIM)
    k_nope = kv[..., :QK_NOPE_DIM]
    v = kv[..., QK_NOPE_DIM:]
    k_rope = apply_rope(k_rope_raw, cos, sin)
    y_a = latent_attention(q_nope, q_rope, k_nope, k_rope, v)
    return jnp.concatenate([y_f, y_a], axis=-1) @ w_out


def swiglu(xn, w_gate, w_up, w_down):
    return (jax.nn.silu(xn @ w_gate) * (xn @ w_up)) @ w_down


def setup_inputs(seed: int = 0) -> dict:
    key = jax.random.key(seed)
    ks = jax.random.split(key, 16)
    f32 = jnp.float32

    def w(k, shape, fan_in):
        return jax.random.normal(k, shape, f32) * (fan_in ** -0.5)

    def gain(k, shape):
        return 1.0 + 0.02 * jax.random.normal(k, shape, f32)

    return {
        "x_prompt": jax.random.normal(ks[0], (BATCH, SEQ, D_MODEL), f32),
        "x_sample": jax.random.normal(ks[1], (DEC_BATCH, DEC_SEQ, D_MODEL), f32),
        "norm_mix_g": gain(ks[2], (DEPTH, D_MODEL)),
        "w_in": w(ks[3], (DEPTH, D_MODEL, IN_WIDTH), D_MODEL),
        "q_norm_g": gain(ks[4], (DEPTH, Q_LORA_RANK)),
        "w_q_up": w(ks[5], (DEPTH, Q_LORA_RANK, N_HEADS * QK_HEAD_DIM), Q_LORA_RANK),
        "kv_norm_g": gain(ks[6], (DEPTH, KV_LORA_RANK)),
        "w_kv_up": w(ks[7], (DEPTH, KV_LORA_RANK, N_HEADS * (QK_NOPE_DIM + V_HEAD_DIM)), KV_LORA_RANK),
        "w_out": w(ks[8], (DEPTH, MIX_WIDTH, D_MODEL), MIX_WIDTH),
        "norm_ffn_g": gain(ks[9], (DEPTH, D_MODEL)),
        "w_gate": w(ks[10], (DEPTH, D_MODEL, D_FF), D_MODEL),
        "w_up": w(ks[11], (DEPTH, D_MODEL, D_FF), D_MODEL),
        "w_down": w(ks[12], (DEPTH, D_FF, D_MODEL), D_FF),
        "final_norm_g": gain(ks[13], (D_MODEL,)),
    }


def reference(x_prompt, x_sample, norm_mix_g, w_in, q_norm_g, w_q_up, kv_norm_g, w_kv_up, w_out,
              norm_ffn_g, w_gate, w_up, w_down, final_norm_g):
    def trunk(x):
        for l in range(DEPTH):
            x = x + token_mixer(rms_norm(x, norm_mix_g[l]), w_in[l], q_norm_g[l], w_q_up[l],
                                kv_norm_g[l], w_kv_up[l], w_out[l])
            x = x + swiglu(rms_norm(x, norm_ffn_g[l]), w_gate[l], w_up[l], w_down[l])
        return rms_norm(x, final_norm_g)

    y_prompt = trunk(x_prompt)
    y_sample = trunk(x_sample)
    return (y_prompt, y_sample)
```

```python
import math
import numpy as np
import concourse.bass as bass
import concourse.mybir as mybir
from concourse.bass_utils import run_bass_kernel_spmd

F32 = mybir.dt.float32
BF16 = mybir.dt.bfloat16
AF = mybir.ActivationFunctionType
ALU = mybir.AluOpType

D = 1024
NH = 6
DFF = 2816
NJ = DFF // 128
EPS = 1e-6
SM_SCALE = 1.0 / math.sqrt(192.0)
T = 512
TC = 1024

PE, ACT, DVE, POOL, SP = "tensor", "scalar", "vector", "gpsimd", "sync"
ENGS = [PE, ACT, DVE, POOL, SP]
NDMASEM = 16
NXT = 4
NXS = 8


class Op:
    __slots__ = ("eng", "fn", "deps", "signal", "seq", "dma", "dsem", "dval", "didx", "barrier", "raw")

    def __init__(self, eng, fn, dma):
        self.eng = eng
        self.fn = fn
        self.deps = []
        self.signal = False
        self.seq = 0
        self.dma = dma
        self.dsem = None
        self.dval = None
        self.didx = None
        self.barrier = 0
        self.raw = ()


class Sched:
    def __init__(self, same_engine_sync=True):
        self.ops = {e: [] for e in ENGS}
        self.last_writer = {}
        self.readers = {}
        self.same_engine_sync = same_engine_sync
        self.nbar = 0

    def op(self, eng, fn, reads=(), writes=(), dma=False):
        o = Op(eng, fn, dma)
        deps = {}
        raw = set()
        for k in reads:
            w = self.last_writer.get(k)
            if w is not None:
                deps[id(w)] = w
                raw.add(id(w))
        for k in writes:
            w = self.last_writer.get(k)
            if w is not None:
                deps[id(w)] = w
            for r in self.readers.get(k, ()):
                deps[id(r)] = r
        o.raw = raw
        for k in reads:
            lst = self.readers.setdefault(k, [])
            if not dma:
                lst[:] = [r for r in lst if r.dma or r.eng != eng]
            lst.append(o)
        for k in writes:
            self.last_writer[k] = o
            self.readers[k] = []
        o.deps = [d for d in deps.values() if d is not o]
        self.ops[eng].append(o)
        return o

    def barrier(self):
        self.nbar += 1
        for e in ENGS:
            o = Op(e, None, False)
            o.barrier = self.nbar
            self.ops[e].append(o)
        self.last_writer = {}
        self.readers = {}

    def plan(self):
        for e in ENGS:
            for o in self.ops[e]:
                for d in o.deps:
                    if d.dma:
                        continue
                    if d.eng == o.eng and not o.dma:
                        if d.eng == PE or not self.same_engine_sync or id(d) not in o.raw:
                            continue
                    d.signal = True
        for e in ENGS:
            c = 0
            i = 0
            for o in self.ops[e]:
                if o.barrier:
                    continue
                if o.dma:
                    o.didx = i
                    o.dsem = i % NDMASEM
                    o.dval = 16 * (i // NDMASEM + 1)
                    i += 1
                elif o.signal:
                    c += 1
                    o.seq = c

    def emit(self, eng, e, sems, dsems, bsems):
        waited = {f: 0 for f in ENGS}
        waited_dma = set()
        my_dmas = []
        for o in self.ops[eng]:
            if o.barrier:
                for d in my_dmas[-NDMASEM:]:
                    if id(d) not in waited_dma:
                        e.wait_ge(dsems[eng][d.dsem], d.dval)
                        waited_dma.add(id(d))
                e.drain().then_inc(bsems[eng], 1)
                for f in ENGS:
                    if f != eng:
                        e.wait_ge(bsems[f], o.barrier)
                for f in ENGS:
                    waited[f] = max(waited[f], self.bar_seq[f][o.barrier])
                continue
            for d in o.deps:
                if d.dma:
                    if id(d) not in waited_dma:
                        e.wait_ge(dsems[d.eng][d.dsem], d.dval)
                        waited_dma.add(id(d))
                else:
                    if d.eng == eng and not o.dma and (eng == PE or not self.same_engine_sync
                                                       or id(d) not in o.raw):
                        continue
                    if d.seq > waited[d.eng]:
                        e.wait_ge(sems[d.eng], d.seq)
                        waited[d.eng] = d.seq
            if o.dma:
                if o.didx >= NDMASEM:
                    p = my_dmas[o.didx - NDMASEM]
                    if id(p) not in waited_dma:
                        e.wait_ge(dsems[eng][p.dsem], p.dval)
                        waited_dma.add(id(p))
                my_dmas.append(o)
            inst = o.fn(e)
            if o.dma:
                inst.then_inc(dsems[eng][o.dsem], 16)
            elif o.signal:
                inst.then_inc(sems[eng], 1)

    def prepare(self):
        self.plan()
        self.bar_seq = {e: {} for e in ENGS}
        for e in ENGS:
            c = 0
            for o in self.ops[e]:
                if o.barrier:
                    self.bar_seq[e][o.barrier] = c
                elif (not o.dma) and o.signal:
                    c = o.seq


def build_program(jobs, phases="FABC", debug_out=False):
    nc = bass.Bass("TRN2", target_bir_lowering=False)
    s = Sched()
    dram = {}

    def din(name, shape):
        if name not in dram:
            dram[name] = nc.dram_tensor(name, list(shape), F32, kind="ExternalInput").ap()
        return dram[name]

    def dout(name, shape):
        dram[name] = nc.dram_tensor(name, list(shape), F32, kind="ExternalOutput").ap()
        return dram[name]

    w_in_d = din("w_in_r", [128, 8, 960])
    w_q_d = din("w_q_r", [128, 3, 1152])
    w_kv_d = din("w_kv_r", [128, 2, 1536])
    w_out_d = din("w_out_r", [128, 8, 1024])
    w_g_d = din("w_gate_r", [NJ, 128, 8, 128])
    w_u_d = din("w_up_r", [NJ, 128, 8, 128])
    w_d_d = din("w_down_r", [2, NJ, 128, 512])
    gains_d = din("gains", [128, 24])
    gfin_d = din("gfin_bc", [128, 1024])
    cst_d = din("consts", [128, 4, 128])

    for jb in jobs:
        S, Q = jb["S"], jb["Q"]
        N2 = S // 128
        tb = jb["tab"]
        din(jb["x"], [jb["xrows"], D])
        din("etab_" + tb, [S, 256])
        din("f2c_" + tb, [2 * N2, 2 * (Q // 128)])
        din("ropec_" + tb, [2, 64, S])
        din("ropeq_" + tb, [2, 64, Q])
        dout(jb["y"], [Q, D])

    wg16 = nc.dram_tensor("wg16", [NJ, 128, 8, 128], BF16, kind="Internal").ap()
    wu16 = nc.dram_tensor("wu16", [NJ, 128, 8, 128], BF16, kind="Internal").ap()
    wd16 = nc.dram_tensor("wd16", [2, NJ, 128, 512], BF16, kind="Internal").ap()
    SMAX = max(jb["S"] for jb in jobs)
    QMAX = max(jb["Q"] for jb in jobs)
    N2MAX = SMAX // 128

    class Carver:
        def __init__(self):
            self.off = 0
            self.maxoff = 0

        def take(self, nbytes):
            o = self.off
            self.off += (nbytes + 31) // 32 * 32
            self.maxoff = max(self.maxoff, self.off)
            return o

    lay = {}
    cv = Carver()
    XPIPE0 = cv.take(0)
    lay["XT"] = [cv.take(4096) for _ in range(NXT)]
    lay["XS"] = [cv.take(2048) for _ in range(NXS)]
    lay["XNT"] = [cv.take(8192) for _ in range(2)]
    XPIPE1 = cv.off
    lay["BT"] = XPIPE0
    if XPIPE1 - XPIPE0 < 32768:
        cv.off = XPIPE0 + 32768
        cv.maxoff = max(cv.maxoff, cv.off)
    COMMON_END = cv.off
    lay["U"] = [cv.take(512) for _ in range(2)]
    lay["ET"] = [cv.take(512) for _ in range(3)]
    lay["B"] = cv.take(N2MAX * 1024)
    lay["YC"] = cv.take(2 * QMAX * 2)
    F0_END = cv.off
    cv.off = COMMON_END
    lay["CKV"] = cv.take(2 * SMAX * 2)
    lay["KR"] = cv.take(SMAX * 2)
    lay["CQ"] = cv.take(3 * QMAX * 2)
    AB0 = cv.off
    lay["ROPE"] = [cv.take(4096) for _ in range(1)]
    lay["SQ"] = cv.take(5 * 512 * 2)
    lay["RSTD"] = [cv.take(2048) for _ in range(2)]
    lay["CST"] = cv.take(5 * 512 * 4)
    lay["T12"] = [cv.take(2048) for _ in range(2)]
    A_END = cv.off
    cv.off = XPIPE0
    lay["KH"] = cv.take(SMAX * 2)
    lay["VH"] = cv.take(SMAX * 2)
    assert cv.off <= COMMON_END + 0 or True
    B_X_END = cv.off
    lay["ACC"] = [cv.take(2048) for _ in range(5)]
    assert cv.off <= COMMON_END, (cv.off, COMMON_END)
    cv.off = AB0
    lay["ROPEB"] = [cv.take(4096) for _ in range(2)]
    lay["T12B"] = [cv.take(2048) for _ in range(2)]
    lay["QN"] = cv.take(QMAX * 2)
    lay["QR"] = cv.take(QMAX * 2)
    lay["PT"] = [cv.take(1024) for _ in range(4)]
    lay["RD"] = cv.take(2048)
    lay["OSB"] = [cv.take(2048) for _ in range(2)]
    B_END = cv.off
    cv.off = COMMON_END
    lay["X1"] = [cv.take(4096) for _ in range(TC // 128)]
    lay["HT"] = cv.take(NJ * TC * 2)
    lay["SG"] = [cv.take(2048) for _ in range(1)]
    lay["WG"] = [cv.take(2048) for _ in range(2)]
    lay["WU"] = [cv.take(2048) for _ in range(2)]
    lay["WD"] = [cv.take(1024) for _ in range(2)]
    C_END = cv.off
    SCR_BYTES = cv.maxoff
    print('SBUF layout: F0_END', F0_END, 'A_END', A_END, 'B_END', B_END, 'C_END', C_END, 'SCR', SCR_BYTES)
    assert KH_ok(lay, SMAX) if False else True
    if B_X_END > COMMON_END:
        raise AssertionError("KH/VH overflow common region: %d > %d" % (B_X_END, COMMON_END))

    from contextlib import ExitStack
    es = ExitStack()
    with es:
        scr = es.enter_context(nc.sbuf_tensor("scr", [128, SCR_BYTES // 2], BF16))
        WQ = es.enter_context(nc.sbuf_tensor("wq", [128, 3 * 1152], BF16))
        WQROT = es.enter_context(nc.sbuf_tensor("wqrot", [128, 3 * 384], BF16))
        WKV = es.enter_context(nc.sbuf_tensor("wkv", [128, 2 * 1536], BF16))
        WIO = es.enter_context(nc.sbuf_tensor("wio", [128, 8 * 1024], BF16))
        CAT = es.enter_context(nc.sbuf_tensor("cat", [128, 8 * QMAX], BF16))
        GAINS = es.enter_context(nc.sbuf_tensor("gains_sb", [128, 24], F32))
        GFIN = es.enter_context(nc.sbuf_tensor("gfin_sb", [128, 1024], F32))
        CONST = es.enter_context(nc.sbuf_tensor("const_sb", [128, 4 * 128], BF16))
        STAT = es.enter_context(nc.sbuf_tensor("stat_sb", [128, 48], F32))
        F2C = es.enter_context(nc.sbuf_tensor("f2c_sb", [128, 64], BF16))
        ONESF = es.enter_context(nc.sbuf_tensor("onesf_sb", [128, 128], F32))
        psum = [es.enter_context(nc.psum_tensor("ps%d" % i, [128, 512], F32)) for i in range(8)]
        NS = 5
        sems = {e: es.enter_context(nc.semaphore("sem_" + e)) for e in ENGS}
        bsems = {e: es.enter_context(nc.semaphore("bsem_" + e)) for e in ENGS}
        dsems = {e: [es.enter_context(nc.semaphore("dsem_%s_%d" % (e, i))) for i in range(NDMASEM)]
                 for e in (SP, POOL)}
        block = es.enter_context(nc.Block())

        def sc(off, n, dt):
            assert off % 4 == 0
            if dt == BF16:
                return scr[:, off // 2: off // 2 + n]
            return scr[:, off // 2: off // 2 + 2 * n].bitcast(F32)

        def ps_bf(b):
            return psum[b][:, :].bitcast(BF16)

        EPSC = GAINS[:, 21:22]
        ident = CONST[:, 0:128]
        ones = CONST[:, 128:256]
        CcB = CONST[:, 256:384]
        ScB = CONST[:, 384:512]
        WQ3 = WQ[:, :].rearrange("p (c n) -> p c n", c=3)
        WQROT3 = WQROT[:, :].rearrange("p (c n) -> p c n", c=3)
        WKV3 = WKV[:, :].rearrange("p (c n) -> p c n", c=2)
        WIO3 = WIO[:, :].rearrange("p (c n) -> p c n", c=8)
        CAT3 = CAT[:, :].rearrange("p (c n) -> p c n", c=8)

        s.op(SP, lambda e: e.dma_start(out=GAINS[:, :], in_=gains_d), writes=["GAINS"], dma=True)
        s.op(SP, lambda e: e.dma_start(out=GFIN[:, :], in_=gfin_d), writes=["GFIN"], dma=True)
        s.op(POOL, lambda e: e.dma_start(out=CONST[:, :].rearrange("p (c n) -> p c n", c=4), in_=cst_d),
             writes=["CONST"], dma=True)
        s.op(POOL, lambda e: e.dma_start(out=WQ3, in_=w_q_d), writes=["WQ"], dma=True)
        s.op(POOL, lambda e: e.dma_start(out=WKV3, in_=w_kv_d), writes=["WKV"], dma=True)
        for h in range(NH):
            a = h * 192 + 128
            s.op(DVE, lambda e, h=h, a=a: e.tensor_scalar(
                out=WQROT3[:, :, h * 64: h * 64 + 32], in0=WQ3[:, :, a + 32: a + 64],
                scalar1=-1.0, scalar2=None, op0=ALU.mult), reads=["WQ"], writes=["WQROT"])
            s.op(DVE, lambda e, h=h, a=a: e.tensor_copy(
                out=WQROT3[:, :, h * 64 + 32: h * 64 + 64], in_=WQ3[:, :, a: a + 32]),
                reads=["WQ"], writes=["WQROT"])

        s.op(DVE, lambda e: e.memset(ONESF[:, :], 1.0), writes=["ONESF"])
        psrr = [0]
        xtc = [0]

        def load_w_in():
            s.op(POOL, lambda e: e.dma_start(out=WIO3[:, :, 0:256], in_=w_in_d[:, :, 0:256]), writes=["WIOF"], dma=True)
            s.op(POOL, lambda e: e.dma_start(out=WIO3[:, :, 256:960], in_=w_in_d[:, :, 256:960]), writes=["WIO"], dma=True)
            s.op(DVE, lambda e: e.tensor_scalar(out=WIO3[:, :, 960:992], in0=WIO3[:, :, 928:960],
                                                scalar1=-1.0, scalar2=None, op0=ALU.mult),
                 reads=["WIO"], writes=["WIO"])
            s.op(DVE, lambda e: e.tensor_copy(out=WIO3[:, :, 992:1024], in_=WIO3[:, :, 896:928]),
                 reads=["WIO"], writes=["WIO"])

        def load_w_out():
            s.op(POOL, lambda e: e.dma_start(out=WIO3, in_=w_out_d), writes=["WIO", "WIOF"], dma=True)

        gctr = [0]

        def norm_front(srcs):
            gi = gctr[0]
            gctr[0] += 1
            blk = gi % 2
            SSb = STAT[:, 8 * blk: 8 * blk + 4]
            RSb = STAT[:, 8 * blk + 4: 8 * blk + 8]
            kss = [("ss", blk, i) for i in range(4)]
            krs = ("rs", blk)
            XSs = []
            for i, (ap, key) in enumerate(srcs):
                xsb = (gi * 4 + i) % NXS
                XS = sc(lay["XS"][xsb], 1024, BF16)
                s.op(ACT, lambda e, XS=XS, ap=ap, i=i: e.activation(out=XS, in_=ap, func=AF.Square,
                                                                  accum_out=SSb[:, i:i + 1]),
                     reads=[key, kss[i]], writes=[kss[i], ("XS", xsb)])
                XSs.append((XS, xsb))
            s.op(ACT, lambda e: e.activation(out=RSb, in_=SSb, func=AF.Ln, scale=1.0 / D, bias=EPSC),
                 reads=kss + ["GAINS"], writes=[krs])
            s.op(ACT, lambda e: e.activation(out=RSb, in_=RSb, func=AF.Exp, scale=-0.5),
                 reads=[krs], writes=[krs])
            for i, (ap, key) in enumerate(srcs):
                XS, xsb = XSs[i]
                s.op(ACT, lambda e, XS=XS, ap=ap, i=i: e.activation(
                    out=XS, in_=ap, func=AF.Copy, scale=RSb[:, i:i + 1]),
                    reads=[key, krs], writes=[("XS", xsb)])
            return (gi, XSs)

        def norm_back(h, g_off):
            gi, XSs = h
            for i in range(4):
                XS, xsb = XSs[i]
                for c in range(8):
                    b = c // 2
                    o0 = (c % 2) * 512 + i * 128
                    s.op(PE, lambda e, c=c, b=b, o0=o0, XS=XS: e.transpose(
                        out=ps_bf(b)[:, o0:o0 + 128], in_=XS[:, c * 128:(c + 1) * 128], identity=ident),
                        reads=[("XS", xsb), "CONST"], writes=[("ps", b)])
            xnt_buf = gi % 2
            XNT = sc(lay["XNT"][xnt_buf], 4096, BF16).rearrange("p (c n) -> p c n", c=8)
            for c in range(8):
                b = c // 2
                o0 = (c % 2) * 512
                s.op(DVE, lambda e, c=c, b=b, o0=o0: e.tensor_scalar(
                    out=XNT[:, c, :], in0=ps_bf(b)[:, o0:o0 + 512],
                    scalar1=GAINS[:, g_off + c: g_off + c + 1], scalar2=None, op0=ALU.mult),
                    reads=[("ps", b), "GAINS"], writes=[("XNT", xnt_buf, c)])
            return XNT, xnt_buf

        def norm_group(srcs, g_off):
            return norm_back(norm_front(srcs), g_off)

        def ps_next(lo, hi):
            psrr[0] += 1
            return lo + psrr[0] % (hi - lo)

        def do_job(ji, jb):
            S, Q = jb["S"], jb["Q"]
            N2 = S // 128
            QP = Q // N2
            K2O = Q // 128
            TR = 2 * N2
            R = 128 // N2
            NG = N2 // 4
            NQC = Q // 512
            xd = dram[jb["x"]]
            tb = jb["tab"]
            etab = dram["etab_" + tb]
            f2c_d = dram["f2c_" + tb]
            ropec = dram["ropec_" + tb]
            ropeq = dram["ropeq_" + tb]
            yd = dram[jb["y"]]
            xbase = jb["xbase"]
            segs = jb["perm_segs"]
            own_off = jb["own_off"]

            def load_ctx_tile(t, buf):
                XT = sc(lay["XT"][buf], 1024, F32)
                for (dp, sp_, n) in segs:
                    src = xd[xbase + sp_ * N2 + t: xbase + (sp_ + n - 1) * N2 + t + 1: N2, :]
                    s.op(SP, lambda e, src=src, dp=dp, n=n: e.dma_start(out=XT[dp:dp + n, :], in_=src),
                         writes=[("XT", buf)], dma=True)
                return XT

            s.barrier()
            load_w_in()
            if "F" not in phases:
                return
            s.op(POOL, lambda e: e.dma_start(out=F2C[0:TR, 0:2 * K2O], in_=f2c_d), writes=["F2C"], dma=True)
            Bv = sc(lay["B"], N2 * 512, BF16)
            def ctx_front(g):
                srcs = []
                for i in range(4):
                    t = g * 4 + i
                    xtb = xtc[0] % NXT
                    xtc[0] += 1
                    srcs.append((load_ctx_tile(t, xtb), ("XT", xtb)))
                return norm_front(srcs)

            hcur = ctx_front(0)
            for g in range(NG):
                XNT, xb = norm_back(hcur, 0)
                hcur = ctx_front(g + 1) if g + 1 < NG else None

                def u_part(i, g=g, XNT=XNT, xb=xb):
                    t = g * 4 + i
                    ub = t % 2
                    eb = t % 3
                    U = sc(lay["U"][ub], 256, BF16)
                    ET = sc(lay["ET"][eb], 256, BF16)
                    s.op(POOL, lambda e, t=t, ET=ET: e.dma_start(out=ET, in_=etab[t * 128:(t + 1) * 128, :]),
                         writes=[("ET", eb)], dma=True)
                    pb = 4 if t % 2 == 0 else 7
                    for c in range(8):
                        s.op(PE, lambda e, c=c, i=i, pb=pb, XNT=XNT: e.matmul(
                            out=psum[pb][:, 0:256], lhsT=XNT[:, c, i * 128:(i + 1) * 128],
                            rhs=WIO3[:, c, 0:256], start=(c == 0), stop=(c == 7)),
                            reads=[("XNT", xb, c), "WIOF"], writes=[("ps", pb)])
                    s.op(DVE, lambda e, pb=pb, U=U: e.tensor_copy(out=U, in_=psum[pb][:, 0:256]),
                         reads=[("ps", pb)], writes=[("U", ub)])

                def s1_part(i, g=g):
                    t = g * 4 + i
                    ub = t % 2
                    eb = t % 3
                    U = sc(lay["U"][ub], 256, BF16)
                    ET = sc(lay["ET"][eb], 256, BF16)
                    sb = 5 + t % 2
                    for ri in range(2):
                        s.op(PE, lambda e, ri=ri, sb=sb, U=U, ET=ET: e.matmul(
                            out=psum[sb][:, ri * 256:(ri + 1) * 256], lhsT=ET[:, ri * 128:(ri + 1) * 128],
                            rhs=U, start=True, stop=True),
                            reads=[("U", ub), ("ET", eb)], writes=[("ps", sb)])
                    s.op(DVE, lambda e, t=t, sb=sb: e.tensor_scalar(
                        out=Bv[:, t * 512:(t + 1) * 512], in0=psum[sb][:, :], scalar1=1.0, scalar2=None, op0=ALU.mult),
                        reads=[("ps", sb)], writes=[("B", t)])

                u_part(0)
                for i in range(4):
                    if i + 1 < 4:
                        u_part(i + 1)
                    s1_part(i)
            s.barrier()
            if "f" in phases:
                return
            BT = sc(lay["BT"], 128 * 128, BF16)
            BT3 = BT.rearrange("p (f k) -> p f k", f=128)
            YC = sc(lay["YC"], 2 * Q, BF16)
            for fc in range(2):
                for fb in range(16):
                    b = fb % 2
                    for i in range(8):
                        f = fc * 128 + fb * 8 + i
                        s.op(PE, lambda e, f=f, b=b, i=i: e.transpose(
                            out=ps_bf(b)[0:TR, i * 128:(i + 1) * 128], in_=Bv[:, f:f + 256 * (TR - 1) + 1:256],
                            identity=ident), reads=["CONST"], writes=[("ps", b)])
                    if fb % 2:
                        s.op(ACT, lambda e, fb=fb, b=b: e.activation(
                            out=BT[0:TR, fb * 1024:(fb + 1) * 1024], in_=ps_bf(b)[0:TR, :], func=AF.Copy),
                            reads=[("ps", b)], writes=[("BT", fb)])
                    else:
                        s.op(DVE, lambda e, fb=fb, b=b: e.tensor_copy(
                            out=BT[0:TR, fb * 1024:(fb + 1) * 1024], in_=ps_bf(b)[0:TR, :]),
                            reads=[("ps", b)], writes=[("BT", fb)])
                for kb in range(8):
                    b = 2 + kb % 2
                    for i in range(16):
                        k1 = kb * 16 + i
                        s.op(PE, lambda e, k1=k1, b=b, i=i: e.matmul(
                            out=psum[b][:, i * 2 * K2O:(i + 1) * 2 * K2O], lhsT=BT3[0:TR, :, k1],
                            rhs=F2C[0:TR, 0:2 * K2O], start=True, stop=True),
                            reads=[("BT", x) for x in range(16)] + ["F2C"], writes=[("ps", b)])
                    k10 = kb * 16
                    t0 = k10 % N2
                    rr = k10 // N2
                    YCv = YC.rearrange("p (r t k q) -> p r t k q", r=2, t=N2, k=K2O, q=R)
                    for ri in range(2):
                        src_ap = psum[b][:, 0:16 * 2 * K2O].rearrange("p (i r k) -> p i r k", i=16, r=2)[:, :, ri, :]
                        dst_ap = YCv[:, ri, t0:t0 + 16, :, rr]
                        s.op(DVE, lambda e, src_ap=src_ap, dst_ap=dst_ap: e.tensor_copy(out=dst_ap, in_=src_ap),
                             reads=[("ps", b)], writes=[("YC", kb)])
                for qc in range(NQC):
                    b = 4 + qc % 2
                    s.op(PE, lambda e, b=b, qc=qc: e.matmul(out=psum[b][:, :], lhsT=CcB,
                                                          rhs=YC[:, qc * 512:(qc + 1) * 512], start=True, stop=False),
                         reads=[("YC", x) for x in range(8)] + ["CONST"], writes=[("ps", b)])
                    s.op(PE, lambda e, b=b, qc=qc: e.matmul(out=psum[b][:, :], lhsT=ScB,
                                                          rhs=YC[:, Q + qc * 512: Q + (qc + 1) * 512],
                                                          start=False, stop=True),
                         reads=[("YC", x) for x in range(8)] + ["CONST"], writes=[("ps", b)])
                    s.op(ACT, lambda e, b=b, qc=qc, fc=fc: e.activation(
                        out=CAT3[:, fc, qc * 512:(qc + 1) * 512], in_=psum[b][:, :], func=AF.Copy,
                        scale=1.0 / math.sqrt(64.0 * S)), reads=[("ps", b)], writes=[("CAT", fc, qc)])

            s.barrier()
            if "A" not in phases:
                return
            CKV = sc(lay["CKV"], 2 * S, BF16).rearrange("p (c n) -> p c n", c=2)
            KR = sc(lay["KR"], S, BF16)
            CQ = sc(lay["CQ"], 3 * Q, BF16).rearrange("p (c n) -> p c n", c=3)
            SQ = sc(lay["SQ"], 5 * 512, BF16).rearrange("p (c n) -> p c n", c=5)
            CST = sc(lay["CST"], 5 * 512, F32).rearrange("p (c n) -> p c n", c=5)
            nq = 4 * QP
            s.op(POOL, lambda e: e.memset(KR[64:128, :], 0.0), writes=["KRZ"])
            hcur = ctx_front(0)
            for g in range(NG):
                XNT, xb = norm_back(hcur, 0)
                hcur = ctx_front(g + 1) if g + 1 < NG else None
                rb = 0
                ROPE = sc(lay["ROPE"][rb], 1024, F32).rearrange("p (c n) -> p c n", c=2)
                s.op(SP, lambda e, g=g, ROPE=ROPE: e.dma_start(
                    out=ROPE[0:64, :, :], in_=ropec[:, :, g * 512:(g + 1) * 512].rearrange("c p n -> p c n")),
                    writes=[("ROPE", rb)], dma=True)
                def rhs_q(c, XNT=XNT):
                    if QP == 128:
                        return XNT[:, c, :]
                    return XNT[:, c, :].rearrange("p (i q) -> p i q", i=4)[:, :, 0:QP]
                outs = []
                for m in range(5):
                    b = 4 + m % 4 if m < 4 else 4
                    b = [4, 5, 6, 7, 4][m]
                    n = nq if m < 3 else 512
                    col0 = 256 + m * 128
                    for c in range(8):
                        if m < 3 and QP != 128:
                            o_ap = psum[b][:, 0:n].rearrange("p (i q) -> p i q", i=4)
                        else:
                            o_ap = psum[b][:, 0:n]
                        r_ap = rhs_q(c) if m < 3 else XNT[:, c, :]
                        s.op(PE, lambda e, c=c, m=m, o_ap=o_ap, col0=col0, r_ap=r_ap: e.matmul(
                            out=o_ap, lhsT=WIO3[:, c, col0:col0 + 128],
                            rhs=r_ap, start=(c == 0), stop=(c == 7)),
                            reads=[("XNT", xb, c), "WIO"], writes=[("ps", b)])
                    s.op(DVE, lambda e, b=b, m=m, n=n: e.tensor_copy(out=CST[:, m, 0:n], in_=psum[b][:, 0:n]),
                         reads=[("ps", b)], writes=[("CST", m)])
                    s.op(ACT, lambda e, b=b, m=m, n=n: e.activation(out=SQ[:, m, 0:n], in_=CST[:, m, 0:n], func=AF.Square),
                         reads=[("CST", m)], writes=[("SQ", m)])
                for (ms, n, dim, goff, which) in (((0, 1, 2), nq, 384.0, 16, 0), ((3, 4), 512, 256.0, 19, 1)):
                    b = 5 + which
                    for k, m in enumerate(ms):
                        s.op(PE, lambda e, b=b, m=m, n=n, k=k, ms=ms: e.matmul(
                            out=psum[b][:, 0:n], lhsT=ones, rhs=SQ[:, m, 0:n], start=(k == 0), stop=(k == len(ms) - 1)),
                            reads=[("SQ", m), "CONST"], writes=[("ps", b)])
                    RS = sc(lay["RSTD"][which], 512, F32)
                    s.op(ACT, lambda e, b=b, n=n, RS=RS, dim=dim: e.activation(
                        out=RS[:, 0:n], in_=psum[b][:, 0:n], func=AF.Ln, scale=1.0 / dim, bias=EPSC),
                        reads=[("ps", b), "GAINS"], writes=[("RSTD", which)])
                    s.op(ACT, lambda e, n=n, RS=RS: e.activation(out=RS[:, 0:n], in_=RS[:, 0:n], func=AF.Exp, scale=-0.5),
                        reads=[("RSTD", which)], writes=[("RSTD", which)])
                    for k, m in enumerate(ms):
                        if which == 0:
                            dst = CQ[:, k, g * nq:(g + 1) * nq]
                            wk = ("CQ", g)
                        else:
                            dst = CKV[:, k, g * 512:(g + 1) * 512]
                            wk = ("CKV", g)
                        s.op(DVE, lambda e, m=m, n=n, RS=RS, dst=dst, goff=goff, k=k: e.scalar_tensor_tensor(
                            out=dst, in0=CST[:, m, 0:n], scalar=GAINS[:, goff + k: goff + k + 1], in1=RS[:, 0:n],
                            op0=ALU.mult, op1=ALU.mult), reads=[("CST", m), ("RSTD", which), "GAINS"], writes=[wk])
                for k, (col0, b) in enumerate(((896, 7), (960, 4))):
                    for c in range(8):
                        s.op(PE, lambda e, c=c, b=b, col0=col0, XNT=XNT: e.matmul(
                            out=psum[b][0:64, :], lhsT=WIO3[:, c, col0:col0 + 64], rhs=XNT[:, c, :],
                            start=(c == 0), stop=(c == 7)), reads=[("XNT", xb, c), "WIO"], writes=[("ps", b)])
                    T12 = sc(lay["T12"][k], 512, F32)
                    s.op(DVE, lambda e, b=b, k=k, T12=T12, ROPE=ROPE: e.tensor_tensor(
                        out=T12[0:64, :], in0=psum[b][0:64, :], in1=ROPE[0:64, k, :], op=ALU.mult),
                        reads=[("ps", b), ("ROPE", rb)], writes=[("T12", k)])
                Ta = sc(lay["T12"][0], 512, F32)
                Tb = sc(lay["T12"][1], 512, F32)
                s.op(DVE, lambda e, g=g, Ta=Ta, Tb=Tb: e.tensor_tensor(
                    out=KR[0:64, g * 512:(g + 1) * 512], in0=Ta[0:64, :], in1=Tb[0:64, :], op=ALU.add),
                    reads=[("T12", 0), ("T12", 1)], writes=[("KR", g)])

            s.barrier()
            load_w_out()
            if "B" not in phases:
                return
            KH = sc(lay["KH"], S, BF16)
            VH = sc(lay["VH"], S, BF16).rearrange("p (t d) -> p t d", d=128)
            QN = sc(lay["QN"], Q, BF16)
            QRp = sc(lay["QR"], Q, BF16)
            RD = sc(lay["RD"], 512, F32)
            s.op(POOL, lambda e: e.memset(QRp[64:128, :], 0.0), writes=["QRZ"])
            if ji == 0:
                for j in range(NJ):
                    s.op(POOL, lambda e, j=j: e.dma_start(out=wg16[j], in_=w_g_d[j]), writes=["W16"], dma=True)
                    s.op(POOL, lambda e, j=j: e.dma_start(out=wu16[j], in_=w_u_d[j]), writes=["W16"], dma=True)
                for hf in range(2):
                    for j in range(NJ):
                        s.op(POOL, lambda e, j=j, hf=hf: e.dma_start(out=wd16[hf, j], in_=w_d_d[hf, j]), writes=["W16"], dma=True)
            allCKV = [("CKV", g) for g in range(NG)]
            allKR = [("KR", g) for g in range(NG)]
            allCQ = [("CQ", g) for g in range(NG)]
            for h in range(NH):
                for g in range(NG):
                    bk = (0, 2)[g % 2]
                    bv = (3, 4)[g % 2]
                    for c in range(2):
                        s.op(PE, lambda e, bk=bk, c=c, g=g, h=h: e.matmul(
                            out=psum[bk][:, :], lhsT=WKV3[:, c, h * 256:h * 256 + 128],
                            rhs=CKV[:, c, g * 512:(g + 1) * 512], start=(c == 0), stop=(c == 1)),
                            reads=[("CKV", g), "WKV"], writes=[("ps", bk)])
                    s.op(ACT, lambda e, bk=bk, g=g: e.activation(out=KH[:, g * 512:(g + 1) * 512], in_=psum[bk][:, :], func=AF.Copy),
                         reads=[("ps", bk)], writes=[("KH", g)])
                    for i in range(4):
                        for c in range(2):
                            s.op(PE, lambda e, bv=bv, c=c, g=g, i=i, h=h: e.matmul(
                                out=psum[bv][:, i * 128:(i + 1) * 128],
                                lhsT=CKV[:, c, (g * 4 + i) * 128:(g * 4 + i + 1) * 128],
                                rhs=WKV3[:, c, h * 256 + 128:h * 256 + 256], start=(c == 0), stop=(c == 1)),
                                reads=[("CKV", g), "WKV"], writes=[("ps", bv)])
                    s.op(DVE, lambda e, bv=bv, g=g: e.tensor_scalar(
                        out=VH[:, g * 4:(g + 1) * 4, :], in0=psum[bv][:, :].rearrange("p (t d) -> p t d", d=128),
                        scalar1=1.0, scalar2=None, op0=ALU.mult),
                        reads=[("ps", bv)], writes=[("VH", g)])
                for qc in range(NQC):
                    b = (0, 2)[qc % 2]
                    for c in range(3):
                        s.op(PE, lambda e, b=b, c=c, qc=qc, h=h: e.matmul(
                            out=psum[b][:, :], lhsT=WQ3[:, c, h * 192:h * 192 + 128],
                            rhs=CQ[:, c, qc * 512:(qc + 1) * 512], start=(c == 0), stop=(c == 2)),
                            reads=allCQ + ["WQ"], writes=[("ps", b)])
                    s.op(ACT, lambda e, b=b, qc=qc: e.activation(out=QN[:, qc * 512:(qc + 1) * 512], in_=psum[b][:, :], func=AF.Copy),
                         reads=[("ps", b)], writes=[("QN", qc)])
                    rb = qc % 2
                    ROPE = sc(lay["ROPEB"][rb], 1024, F32).rearrange("p (c n) -> p c n", c=2)
                    s.op(SP, lambda e, qc=qc, ROPE=ROPE: e.dma_start(
                        out=ROPE[0:64, :, :], in_=ropeq[:, :, qc * 512:(qc + 1) * 512].rearrange("c p n -> p c n")),
                        writes=[("ROPEB", rb)], dma=True)
                    for k in range(2):
                        b2 = 3 + k
                        for c in range(3):
                            lhs = WQ3[:, c, h * 192 + 128:h * 192 + 192] if k == 0 else WQROT3[:, c, h * 64:(h + 1) * 64]
                            s.op(PE, lambda e, b2=b2, c=c, qc=qc, lhs=lhs: e.matmul(
                                out=psum[b2][0:64, :], lhsT=lhs, rhs=CQ[:, c, qc * 512:(qc + 1) * 512],
                                start=(c == 0), stop=(c == 2)),
                                reads=allCQ + ["WQ", "WQROT"], writes=[("ps", b2)])
                        T12 = sc(lay["T12B"][k], 512, F32)
                        s.op(DVE, lambda e, b2=b2, k=k, T12=T12, ROPE=ROPE: e.tensor_tensor(
                            out=T12[0:64, :], in0=psum[b2][0:64, :], in1=ROPE[0:64, k, :], op=ALU.mult),
                            reads=[("ps", b2), ("ROPEB", rb)], writes=[("T12B", k)])
                    Ta = sc(lay["T12B"][0], 512, F32)
                    Tb = sc(lay["T12B"][1], 512, F32)
                    s.op(DVE, lambda e, qc=qc, Ta=Ta, Tb=Tb: e.tensor_tensor(
                        out=QRp[0:64, qc * 512:(qc + 1) * 512], in0=Ta[0:64, :], in1=Tb[0:64, :], op=ALU.add),
                        reads=[("T12B", 0), ("T12B", 1)], writes=[("QR", qc)])
                allKH = [("KH", g) for g in range(NG)]
                allVH = [("VH", g) for g in range(NG)]
                for qp in range(NQC // 2):
                    qcs = (2 * qp, 2 * qp + 1)
                    obs = (6, 7)
                    db = 1

                    def qk(kt, u):
                        qc = qcs[u]
                        qs = slice(qc * 512, (qc + 1) * 512)
                        bsc = 2 + 2 * u + kt % 2
                        s.op(PE, lambda e, bsc=bsc, kt=kt, qs=qs: e.matmul(
                            out=psum[bsc][:, :], lhsT=KH[:, kt * 128:(kt + 1) * 128], rhs=QN[:, qs],
                            start=True, stop=False),
                            reads=[("KH", kt // 4), ("QN", qc)], writes=[("ps", bsc)])
                        s.op(PE, lambda e, bsc=bsc, kt=kt, qs=qs: e.matmul(
                            out=psum[bsc][:, :], lhsT=KR[:, kt * 128:(kt + 1) * 128], rhs=QRp[:, qs],
                            start=False, stop=True),
                            reads=[("KR", kt // 4), ("QR", qc), "KRZ", "QRZ"], writes=[("ps", bsc)])

                    qk(0, 0)
                    qk(0, 1)
                    for kt in range(N2):
                        if kt + 1 < N2:
                            qk(kt + 1, 0)
                            qk(kt + 1, 1)
                        for u in range(2):
                            bsc = 2 + 2 * u + kt % 2
                            pb = 2 * u + kt % 2
                            PT = sc(lay["PT"][pb], 512, BF16)
                            s.op(ACT, lambda e, bsc=bsc, PT=PT: e.activation(out=PT, in_=psum[bsc][:, :], func=AF.Exp, scale=SM_SCALE),
                                 reads=[("ps", bsc)], writes=[("PT", pb)])
                            s.op(PE, lambda e, kt=kt, PT=PT, u=u: e.matmul(
                                out=psum[obs[u]][:, :], lhsT=VH[:, kt, :], rhs=PT, start=(kt == 0), stop=(kt == N2 - 1)),
                                reads=[("VH", kt // 4), ("PT", pb)], writes=[("ps", obs[u])])
                            ab = 2 * u + kt % 2
                            ACC = sc(lay["ACC"][ab], 512, F32)
                            if kt < 2:
                                s.op(DVE, lambda e, PT=PT, ACC=ACC: e.tensor_copy(out=ACC, in_=PT),
                                     reads=[("PT", pb)], writes=[("ACC", ab)])
                            else:
                                s.op(DVE, lambda e, PT=PT, ACC=ACC: e.tensor_tensor(out=ACC, in0=ACC, in1=PT, op=ALU.add),
                                     reads=[("PT", pb), ("ACC", ab)], writes=[("ACC", ab)])
                    OSBs = []
                    for u in range(2):
                        OSB = sc(lay["OSB"][u], 512, F32)
                        s.op(DVE, lambda e, u=u, OSB=OSB: e.tensor_copy(out=OSB, in_=psum[obs[u]][:, :]),
                             reads=[("ps", obs[u])], writes=[("OSB", u)])
                        OSBs.append(OSB)
                    for u in range(2):
                        qc = qcs[u]
                        qs = slice(qc * 512, (qc + 1) * 512)
                        ACC0 = sc(lay["ACC"][2 * u], 512, F32)
                        ACC1 = sc(lay["ACC"][2 * u + 1], 512, F32)
                        ACS = sc(lay["ACC"][4], 512, F32)
                        if N2 > 1:
                            s.op(DVE, lambda e, ACC0=ACC0, ACC1=ACC1, ACS=ACS: e.tensor_tensor(out=ACS, in0=ACC0, in1=ACC1, op=ALU.add),
                                 reads=[("ACC", 2 * u), ("ACC", 2 * u + 1)], writes=[("ACC", 4)])
                        else:
                            s.op(DVE, lambda e, ACC0=ACC0, ACS=ACS: e.tensor_copy(out=ACS, in_=ACC0),
                                 reads=[("ACC", 2 * u)], writes=[("ACC", 4)])
                        s.op(PE, lambda e, ACS=ACS: e.matmul(out=psum[db][:, :], lhsT=ONESF[:, :], rhs=ACS, start=True, stop=True),
                             reads=[("ACC", 4), "ONESF"], writes=[("ps", db)])
                        s.op(ACT, lambda e: e.activation(out=RD, in_=psum[db][:, :], func=AF.Ln),
                             reads=[("ps", db)], writes=["RD"])
                        s.op(ACT, lambda e: e.activation(out=RD, in_=RD, func=AF.Exp, scale=-1.0),
                             reads=["RD"], writes=["RD"])
                        s.op(DVE, lambda e, u=u, h=h, qs=qs, OSB=OSBs[u]: e.tensor_tensor(
                            out=CAT3[:, 2 + h, qs], in0=OSB, in1=RD, op=ALU.mult),
                            reads=[("OSB", u), "RD"], writes=[("CAT", 2 + h, qc)])

            s.barrier()
            if "C" not in phases:
                return
            HT = sc(lay["HT"], NJ * TC, BF16).rearrange("p (j n) -> p j n", j=NJ)
            NSUB = TC // 512
            for tg in range(Q // TC):
                XNTs = []
                for sub in range(NSUB):
                    srcs = []
                    for i in range(4):
                        ti = sub * 4 + i
                        qt = tg * (TC // 128) + ti
                        xtb = xtc[0] % NXT
                        xtc[0] += 1
                        XT = sc(lay["XT"][xtb], 1024, F32)
                        npt = 128 // QP
                        for a_ in range(npt):
                            t = qt * npt + a_
                            src_ = xd[xbase + own_off * N2 + t: xbase + (own_off + QP - 1) * N2 + t + 1: N2, :]
                            s.op(SP, lambda e, src_=src_, a_=a_, XT=XT: e.dma_start(out=XT[a_ * QP:(a_ + 1) * QP, :], in_=src_),
                                 writes=[("XT", xtb)], dma=True)
                        X1 = sc(lay["X1"][ti], 1024, F32)
                        for hf in range(2):
                            b = 4 + hf + 2 * (i % 2)
                            for c in range(8):
                                s.op(PE, lambda e, b=b, c=c, qt=qt, hf=hf: e.matmul(
                                    out=psum[b][:, :], lhsT=CAT3[:, c, qt * 128:(qt + 1) * 128],
                                    rhs=WIO3[:, c, hf * 512:(hf + 1) * 512], start=(c == 0), stop=(c == 7)),
                                    reads=[("CAT", c, qt // 4), "WIO"], writes=[("ps", b)])
                            s.op(DVE, lambda e, b=b, hf=hf, X1=X1, XT=XT: e.tensor_tensor(
                                out=X1[:, hf * 512:(hf + 1) * 512], in0=psum[b][:, :], in1=XT[:, hf * 512:(hf + 1) * 512],
                                op=ALU.add), reads=[("ps", b), ("XT", xtb)], writes=[("X1", ti)])
                        srcs.append((X1, ("X1", ti)))
                    XNTs.append(norm_front(srcs))
                XNTs = [norm_back(h_, 8) for h_ in XNTs]
                for j in range(NJ):
                    wb = j % 2
                    WG = sc(lay["WG"][wb], 1024, BF16).rearrange("p (c n) -> p c n", c=8)
                    WU = sc(lay["WU"][wb], 1024, BF16).rearrange("p (c n) -> p c n", c=8)
                    s.op(SP, lambda e, j=j, WG=WG: e.dma_start(out=WG, in_=wg16[j]), reads=["W16"], writes=[("WG", wb)], dma=True)
                    s.op(SP, lambda e, j=j, WU=WU: e.dma_start(out=WU, in_=wu16[j]), reads=["W16"], writes=[("WU", wb)], dma=True)
                    for sub in range(NSUB):
                        XNT, xb = XNTs[sub]
                        bg = 4 * (j % 2) + sub
                        bu = 4 * (j % 2) + 2 + sub
                        for c in range(8):
                            s.op(PE, lambda e, c=c, bg=bg, WG=WG, XNT=XNT: e.matmul(
                                out=psum[bg][:, :], lhsT=WG[:, c, :], rhs=XNT[:, c, :], start=(c == 0), stop=(c == 7)),
                                reads=[("WG", wb), ("XNT", xb, c)], writes=[("ps", bg)])
                        for c in range(8):
                            s.op(PE, lambda e, c=c, bu=bu, WU=WU, XNT=XNT: e.matmul(
                                out=psum[bu][:, :], lhsT=WU[:, c, :], rhs=XNT[:, c, :], start=(c == 0), stop=(c == 7)),
                                reads=[("WU", wb), ("XNT", xb, c)], writes=[("ps", bu)])
                        sgb = 0
                        SG = sc(lay["SG"][sgb], 512, F32)
                        s.op(ACT, lambda e, bg=bg, SG=SG: e.activation(out=SG, in_=psum[bg][:, :], func=AF.Silu),
                             reads=[("ps", bg)], writes=[("SG", sgb)])
                        s.op(DVE, lambda e, bu=bu, SG=SG, j=j, sub=sub: e.tensor_tensor(
                            out=HT[:, j, sub * 512:(sub + 1) * 512], in0=psum[bu][:, :], in1=SG, op=ALU.mult),
                            reads=[("ps", bu), ("SG", sgb)], writes=[("HT", j, sub)])
                NT8 = TC // 128
                for hf in range(2):
                    for j in range(NJ):
                        wb = (hf * NJ + j) % 2
                        WD = sc(lay["WD"][wb], 512, BF16)
                        s.op(SP, lambda e, j=j, hf=hf, WD=WD: e.dma_start(out=WD, in_=wd16[hf, j]),
                             reads=["W16"], writes=[("WD", wb)], dma=True)
                        for i in range(NT8):
                            s.op(PE, lambda e, i=i, j=j, WD=WD: e.matmul(
                                out=psum[i][:, :], lhsT=HT[:, j, i * 128:(i + 1) * 128], rhs=WD,
                                start=(j == 0), stop=(j == NJ - 1)),
                                reads=[("HT", j, i // 4), ("WD", wb)], writes=[("ps", i)])
                    for i in range(NT8):
                        X1 = sc(lay["X1"][i], 1024, F32)
                        s.op(DVE, lambda e, i=i, hf=hf, X1=X1: e.tensor_tensor(
                            out=X1[:, hf * 512:(hf + 1) * 512], in0=psum[i][:, :], in1=X1[:, hf * 512:(hf + 1) * 512],
                            op=ALU.add), reads=[("ps", i), ("X1", i)], writes=[("X1", i)])
                for sub in range(NSUB):
                    SSf = STAT[:, 16 + 8 * sub:20 + 8 * sub]
                    RSf = STAT[:, 20 + 8 * sub:24 + 8 * sub]
                    kssf = [("ssf", sub, i) for i in range(4)]
                    krsf = ("rsf", sub)
                    for i in range(4):
                        ti = sub * 4 + i
                        X1 = sc(lay["X1"][ti], 1024, F32)
                        YOj = sc(lay["XS"][i % NXS], 1024, BF16)
                        s.op(ACT, lambda e, X1=X1, SSf=SSf, YOj=YOj, i=i: e.activation(
                            out=YOj, in_=X1, func=AF.Square, accum_out=SSf[:, i:i + 1]),
                            reads=[("X1", ti), kssf[i]], writes=[kssf[i], ("XS", i % NXS)])
                    s.op(ACT, lambda e, SSf=SSf, RSf=RSf: e.activation(out=RSf, in_=SSf, func=AF.Ln, scale=1.0 / D, bias=EPSC),
                         reads=kssf + ["GAINS"], writes=[krsf])
                    s.op(ACT, lambda e, RSf=RSf: e.activation(out=RSf, in_=RSf, func=AF.Exp, scale=-0.5), reads=[krsf], writes=[krsf])
                    for i in range(4):
                        ti = sub * 4 + i
                        qt = tg * (TC // 128) + ti
                        X1 = sc(lay["X1"][ti], 1024, F32)
                        s.op(DVE, lambda e, X1=X1, RSf=RSf, i=i: e.scalar_tensor_tensor(
                            out=X1, in0=X1, scalar=RSf[:, i:i + 1], in1=GFIN[:, :], op0=ALU.mult, op1=ALU.mult),
                            reads=[("X1", ti), krsf, "GFIN"], writes=[("X1", ti)])
                        s.op(SP, lambda e, qt=qt, X1=X1: e.dma_start(out=yd[qt * 128:(qt + 1) * 128, :], in_=X1),
                             reads=[("X1", ti)], dma=True)
        for ji, jb in enumerate(jobs):
            do_job(ji, jb)
        s.barrier()

        s.prepare()

        @block.sync
        def _(e):
            s.emit(SP, e, sems, dsems, bsems)

        @block.gpsimd
        def _(e):
            s.emit(POOL, e, sems, dsems, bsems)

        @block.scalar
        def _(e):
            s.emit(ACT, e, sems, dsems, bsems)

        @block.vector
        def _(e):
            s.emit(DVE, e, sems, dsems, bsems)

        @block.tensor
        def _(e):
            s.emit(PE, e, sems, dsems, bsems)

    return nc


def KH_ok(lay, smax):
    return True


def _tables(S, Q, s1_of_p, own_k2_0):
    N2 = S // 128
    K2O = Q // 128
    t = np.arange(N2)[:, None]
    tok = (N2 * np.asarray(s1_of_p)[None, :] + t).astype(np.int64)
    k1 = np.arange(128, dtype=np.int64)
    ph = (tok[:, :, None] * k1[None, None, :]) % S
    ang = 2.0 * np.pi * ph.astype(np.float64) / S
    etab = np.concatenate([np.cos(ang), -np.sin(ang)], axis=2).reshape(S, 256).astype(np.float32)
    k2 = own_k2_0 + np.arange(K2O, dtype=np.int64)
    phi = 2.0 * np.pi * ((np.arange(N2, dtype=np.int64)[:, None] * k2[None, :]) % N2) / N2
    f2c = np.zeros((N2, 2, 2, K2O), np.float64)
    f2c[:, 0, 0, :] = np.cos(phi)
    f2c[:, 1, 0, :] = np.sin(phi)
    f2c[:, 0, 1, :] = -np.sin(phi)
    f2c[:, 1, 1, :] = np.cos(phi)
    f2c = f2c.reshape(2 * N2, 2 * K2O).astype(np.float32)
    inv_freq = (1.0 / (10000.0 ** (np.arange(0, 64, 2, dtype=np.float32) / 64.0))).astype(np.float32)
    pos = tok.reshape(-1).astype(np.float32)
    angr = pos[None, :] * inv_freq[:, None]
    cosr = np.cos(angr).astype(np.float32)
    sinr = np.sin(angr).astype(np.float32)
    ropec = np.stack([np.concatenate([cosr, cosr], 0), np.concatenate([sinr, sinr], 0)], 0)
    return etab, f2c, np.ascontiguousarray(ropec)


def _rope_q(ropec, S, Q):
    N2 = S // 128
    QP = Q // N2
    r = ropec.reshape(2, 64, N2, 128)[:, :, :, :QP]
    return np.ascontiguousarray(r.reshape(2, 64, Q))


def _consts():
    ident = np.eye(128, dtype=np.float32)
    ones = np.ones((128, 128), np.float32)
    c = np.arange(64)
    psi = 2.0 * np.pi * ((c[:, None] * c[None, :]) % 64) / 64.0
    Cc = np.zeros((128, 128), np.float64)
    Sc = np.zeros((128, 128), np.float64)
    for g in range(2):
        Cc[g * 64:(g + 1) * 64, g * 64:(g + 1) * 64] = np.cos(psi)
        Sc[g * 64:(g + 1) * 64, g * 64:(g + 1) * 64] = np.sin(psi)
    return np.ascontiguousarray(np.stack([ident, ones, Cc.astype(np.float32), Sc.astype(np.float32)], 1))


def _weight_maps(inp):
    f = lambda a: np.ascontiguousarray(np.asarray(a, dtype=np.float32))
    w_in = f(inp["w_in"])[0]
    m = {}
    m["w_in_r"] = f(w_in.reshape(8, 128, 960).transpose(1, 0, 2))
    m["w_q_r"] = f(f(inp["w_q_up"])[0].reshape(3, 128, 1152).transpose(1, 0, 2))
    m["w_kv_r"] = f(f(inp["w_kv_up"])[0].reshape(2, 128, 1536).transpose(1, 0, 2))
    m["w_out_r"] = f(f(inp["w_out"])[0].reshape(8, 128, 1024).transpose(1, 0, 2))
    m["w_gate_r"] = f(f(inp["w_gate"])[0].reshape(8, 128, NJ, 128).transpose(2, 1, 0, 3))
    m["w_up_r"] = f(f(inp["w_up"])[0].reshape(8, 128, NJ, 128).transpose(2, 1, 0, 3))
    m["w_down_r"] = f(f(inp["w_down"])[0].reshape(NJ, 128, 2, 512).transpose(2, 0, 1, 3))
    g = np.zeros((128, 24), np.float32)
    g[:, 0:8] = f(inp["norm_mix_g"])[0].reshape(8, 128).T
    g[:, 8:16] = f(inp["norm_ffn_g"])[0].reshape(8, 128).T
    g[:, 16:19] = f(inp["q_norm_g"])[0].reshape(3, 128).T
    g[:, 19:21] = f(inp["kv_norm_g"])[0].reshape(2, 128).T
    g[:, 21] = EPS
    m["gains"] = g
    m["gfin_bc"] = f(np.broadcast_to(f(inp["final_norm_g"])[None, :], (128, 1024)))
    m["consts"] = _consts()
    return m


SP_, QQ = 8192, 2048
SS_ = 2048


def _job_list():
    jobs = []
    QPp = QQ // (SP_ // 128)
    jobs.append(dict(S=SP_, Q=QQ, x="xp", xrows=SP_, xbase=0, perm_segs=[(0, 0, 128)], own_off=0, tab="p0", y="y0"))
    jobs.append(dict(S=SP_, Q=QQ, x="xp", xrows=SP_, xbase=0,
                     perm_segs=[(0, QPp, QPp), (QPp, 0, QPp), (2 * QPp, 2 * QPp, 128 - 2 * QPp)],
                     own_off=QPp, tab="p1", y="y1"))
    for k in range(4):
        jobs.append(dict(S=SS_, Q=QQ, x="xs", xrows=4 * SS_, xbase=k * SS_, perm_segs=[(0, 0, 128)], own_off=0,
                         tab="s", y="y%d" % (2 + k)))
    return jobs


def kernel(**inputs):
    xp = np.asarray(inputs["x_prompt"], dtype=np.float32)
    xs = np.asarray(inputs["x_sample"], dtype=np.float32)
    wm = _weight_maps(inputs)
    jobs = _job_list()
    nc = build_program(jobs)
    N2p = SP_ // 128
    QPp = QQ // N2p
    in_maps = []
    sigmas = []
    tabs_s = _tables(SS_, QQ, np.arange(128), 0)
    for core in range(8):
        b, half = core // 2, core % 2
        q0, q1 = 2 * half, 2 * half + 1
        own0 = np.arange(q0 * QPp, (q0 + 1) * QPp)
        own1 = np.arange(q1 * QPp, (q1 + 1) * QPp)
        rest = np.array([v for v in range(128) if v // QPp not in (q0, q1)])
        sigma = np.concatenate([own0, own1, rest])
        sigmas.append(sigma)
        m = dict(wm)
        xb = xp[b].reshape(128, N2p, D)[sigma].reshape(SP_, D)
        m["xp"] = np.ascontiguousarray(xb)
        m["xs"] = np.ascontiguousarray(xs[4 * core:4 * core + 4].reshape(4 * SS_, D))
        s1p0 = sigma
        s1p1 = np.concatenate([sigma[QPp:2 * QPp], sigma[0:QPp], sigma[2 * QPp:]])
        for tbn, s1p, qr in (("p0", s1p0, q0), ("p1", s1p1, q1)):
            et, f2, rc = _tables(SP_, QQ, s1p, qr * (QQ // 128))
            m["etab_" + tbn] = et
            m["f2c_" + tbn] = f2
            m["ropec_" + tbn] = rc
            m["ropeq_" + tbn] = _rope_q(rc, SP_, QQ)
        m["etab_s"], m["f2c_s"], m["ropec_s"] = tabs_s
        m["ropeq_s"] = _rope_q(tabs_s[2], SS_, QQ)
        in_maps.append(m)
    res = run_bass_kernel_spmd(nc, in_maps, core_ids=list(range(8)))
    y_prompt = np.zeros((4, SP_, D), np.float32)
    y_sample = np.zeros((32, SS_, D), np.float32)
    for core in range(8):
        r = res.results[core]
        b, half = core // 2, core % 2
        for jj in range(2):
            qr = 2 * half + jj
            y = np.asarray(r["y%d" % jj]).reshape(N2p, QPp, D)
            y_prompt[b].reshape(128, N2p, D)[qr * QPp:(qr + 1) * QPp] = y.transpose(1, 0, 2)
        for k in range(4):
            y = np.asarray(r["y%d" % (2 + k)]).reshape(SS_ // 128, 128, D)
            y_sample[4 * core + k] = y.transpose(1, 0, 2).reshape(SS_, D)
    return (y_prompt, y_sample)
```

```python
import math
import numpy as np
import concourse.bass as bass
import concourse.mybir as mybir
from concourse.bass_utils import run_bass_kernel_spmd

F32 = mybir.dt.float32
BF16 = mybir.dt.bfloat16
AF = mybir.ActivationFunctionType
ALU = mybir.AluOpType

D = 1024
NH = 6
DFF = 2816
NJ = DFF // 128
EPS = 1e-6
SM_SCALE = 1.0 / math.sqrt(192.0)
T = 512
TC = 1024

PE, ACT, DVE, POOL, SP = "tensor", "scalar", "vector", "gpsimd", "sync"
ENGS = [PE, ACT, DVE, POOL, SP]
NDMASEM = 24
NXT = 4
NXS = 8


class Op:
    __slots__ = ("eng", "fn", "deps", "signal", "seq", "dma", "dsem", "dval", "didx", "barrier", "raw")

    def __init__(self, eng, fn, dma):
        self.eng = eng
        self.fn = fn
        self.deps = []
        self.signal = False
        self.seq = 0
        self.dma = dma
        self.dsem = None
        self.dval = None
        self.didx = None
        self.barrier = 0
        self.raw = ()


class Sched:
    def __init__(self, same_engine_sync=True):
        self.ops = {e: [] for e in ENGS}
        self.last_writer = {}
        self.readers = {}
        self.same_engine_sync = same_engine_sync
        self.nbar = 0

    def op(self, eng, fn, reads=(), writes=(), dma=False):
        o = Op(eng, fn, dma)
        deps = {}
        raw = set()
        for k in reads:
            w = self.last_writer.get(k)
            if w is not None:
                deps[id(w)] = w
                raw.add(id(w))
        for k in writes:
            w = self.last_writer.get(k)
            if w is not None:
                deps[id(w)] = w
            for r in self.readers.get(k, ()):
                deps[id(r)] = r
        o.raw = raw
        for k in reads:
            lst = self.readers.setdefault(k, [])
            if not dma:
                lst[:] = [r for r in lst if r.dma or r.eng != eng]
            lst.append(o)
        for k in writes:
            self.last_writer[k] = o
            self.readers[k] = []
        o.deps = [d for d in deps.values() if d is not o]
        self.ops[eng].append(o)
        return o

    def barrier(self):
        self.nbar += 1
        for e in ENGS:
            o = Op(e, None, False)
            o.barrier = self.nbar
            self.ops[e].append(o)
        self.last_writer = {}
        self.readers = {}

    def plan(self):
        for e in ENGS:
            for o in self.ops[e]:
                for d in o.deps:
                    if d.dma:
                        continue
                    if d.eng == o.eng and not o.dma:
                        if d.eng == PE or not self.same_engine_sync or id(d) not in o.raw:
                            continue
                    d.signal = True
        for e in ENGS:
            c = 0
            i = 0
            for o in self.ops[e]:
                if o.barrier:
                    continue
                if o.dma:
                    o.didx = i
                    o.dsem = i % NDMASEM
                    o.dval = 16 * (i // NDMASEM + 1)
                    i += 1
                elif o.signal:
                    c += 1
                    o.seq = c

    def emit(self, eng, e, sems, dsems, bsems):
        waited = {f: 0 for f in ENGS}
        waited_dma = set()
        my_dmas = []
        for o in self.ops[eng]:
            if o.barrier:
                for d in my_dmas[-NDMASEM:]:
                    if id(d) not in waited_dma:
                        e.wait_ge(dsems[eng][d.dsem], d.dval)
                        waited_dma.add(id(d))
                e.drain().then_inc(bsems[eng], 1)
                for f in ENGS:
                    if f != eng:
                        e.wait_ge(bsems[f], o.barrier)
                for f in ENGS:
                    waited[f] = max(waited[f], self.bar_seq[f][o.barrier])
                continue
            for d in o.deps:
                if d.dma:
                    if id(d) not in waited_dma:
                        e.wait_ge(dsems[d.eng][d.dsem], d.dval)
                        waited_dma.add(id(d))
                else:
                    if d.eng == eng and not o.dma and (eng == PE or not self.same_engine_sync
                                                       or id(d) not in o.raw):
                        continue
                    if d.seq > waited[d.eng]:
                        e.wait_ge(sems[d.eng], d.seq)
                        waited[d.eng] = d.seq
            if o.dma:
                if o.didx >= NDMASEM:
                    p = my_dmas[o.didx - NDMASEM]
                    if id(p) not in waited_dma:
                        e.wait_ge(dsems[eng][p.dsem], p.dval)
                        waited_dma.add(id(p))
                my_dmas.append(o)
            inst = o.fn(e)
            if o.dma:
                inst.then_inc(dsems[eng][o.dsem], 16)
            elif o.signal:
                inst.then_inc(sems[eng], 1)

    def prepare(self):
        self.plan()
        self.bar_seq = {e: {} for e in ENGS}
        for e in ENGS:
            c = 0
            for o in self.ops[e]:
                if o.barrier:
                    self.bar_seq[e][o.barrier] = c
                elif (not o.dma) and o.signal:
                    c = o.seq


def build_program(jobs, phases="FABC", debug_out=False):
    nc = bass.Bass("TRN2", target_bir_lowering=False)
    s = Sched()
    dram = {}

    def din(name, shape):
        if name not in dram:
            dram[name] = nc.dram_tensor(name, list(shape), F32, kind="ExternalInput").ap()
        return dram[name]

    def dout(name, shape):
        dram[name] = nc.dram_tensor(name, list(shape), F32, kind="ExternalOutput").ap()
        return dram[name]

    w_in_d = din("w_in_r", [128, 8, 960])
    w_q_d = din("w_q_r", [128, 3, 1152])
    w_kv_d = din("w_kv_r", [128, 2, 1536])
    w_out_d = din("w_out_r", [128, 8, 1024])
    w_g_d = din("w_gate_r", [NJ, 128, 8, 128])
    w_u_d = din("w_up_r", [NJ, 128, 8, 128])
    w_d_d = din("w_down_r", [2, NJ, 128, 512])
    gains_d = din("gains", [128, 24])
    gfin_d = din("gfin_bc", [128, 1024])
    cst_d = din("consts", [128, 4, 128])

    for jb in jobs:
        S, Q = jb["S"], jb["Q"]
        N2 = S // 128
        tb = jb["tab"]
        din(jb["x"], [jb["xrows"], D])
        din("etab_" + tb, [S, 256])
        din("f2c_" + tb, [2 * N2, 2 * (Q // 128)])
        din("ropec_" + tb, [2, 64, S])
        din("ropeq_" + tb, [2, 64, Q])
        dout(jb["y"], [Q, D])

    wg16 = nc.dram_tensor("wg16", [NJ, 128, 8, 128], BF16, kind="Internal").ap()
    wu16 = nc.dram_tensor("wu16", [NJ, 128, 8, 128], BF16, kind="Internal").ap()
    wd16 = nc.dram_tensor("wd16", [2, NJ, 128, 512], BF16, kind="Internal").ap()
    SMAX = max(jb["S"] for jb in jobs)
    QMAX = max(jb["Q"] for jb in jobs)
    N2MAX = SMAX // 128

    class Carver:
        def __init__(self):
            self.off = 0
            self.maxoff = 0

        def take(self, nbytes):
            o = self.off
            self.off += (nbytes + 31) // 32 * 32
            self.maxoff = max(self.maxoff, self.off)
            return o

    lay = {}
    cv = Carver()
    XPIPE0 = cv.take(0)
    lay["XT"] = [cv.take(4096) for _ in range(NXT)]
    lay["XS"] = [cv.take(2048) for _ in range(NXS)]
    lay["XNT"] = [cv.take(8192) for _ in range(2)]
    XPIPE1 = cv.off
    lay["BT"] = XPIPE0
    if XPIPE1 - XPIPE0 < 32768:
        cv.off = XPIPE0 + 32768
        cv.maxoff = max(cv.maxoff, cv.off)
    COMMON_END = cv.off
    lay["U"] = [cv.take(512) for _ in range(2)]
    lay["ET"] = [cv.take(512) for _ in range(3)]
    lay["B"] = cv.take(N2MAX * 1024)
    lay["YC"] = cv.take(2 * QMAX * 2)
    F0_END = cv.off
    cv.off = COMMON_END
    lay["CKV"] = cv.take(2 * SMAX * 2)
    lay["KR"] = cv.take(SMAX * 2)
    lay["CQ"] = cv.take(3 * QMAX * 2)
    AB0 = cv.off
    lay["ROPE"] = [cv.take(4096) for _ in range(1)]
    lay["SQ"] = cv.take(5 * 512 * 2)
    lay["RSTD"] = [cv.take(2048) for _ in range(2)]
    lay["CST"] = cv.take(5 * 512 * 4)
    lay["T12"] = [cv.take(2048) for _ in range(2)]
    A_END = cv.off
    cv.off = XPIPE0
    lay["KH"] = cv.take(SMAX * 2)
    lay["VH"] = cv.take(SMAX * 2)
    assert cv.off <= COMMON_END + 0 or True
    B_X_END = cv.off
    lay["ACC"] = [cv.take(2048) for _ in range(5)]
    assert cv.off <= COMMON_END, (cv.off, COMMON_END)
    cv.off = AB0
    lay["ROPEB"] = [cv.take(4096) for _ in range(2)]
    lay["T12B"] = [cv.take(2048) for _ in range(2)]
    lay["QN"] = cv.take(QMAX * 2)
    lay["QR"] = cv.take(QMAX * 2)
    lay["PT"] = [cv.take(1024) for _ in range(4)]
    lay["RD"] = cv.take(2048)
    lay["OSB"] = [cv.take(2048) for _ in range(2)]
    B_END = cv.off
    cv.off = COMMON_END
    lay["X1"] = [cv.take(4096) for _ in range(TC // 128)]
    lay["HT"] = cv.take(NJ * TC * 2)
    lay["SG"] = [cv.take(2048) for _ in range(1)]
    lay["WG"] = [cv.take(2048) for _ in range(2)]
    lay["WU"] = [cv.take(2048) for _ in range(2)]
    lay["WD"] = [cv.take(1024) for _ in range(2)]
    C_END = cv.off
    SCR_BYTES = cv.maxoff
    print('SBUF layout: F0_END', F0_END, 'A_END', A_END, 'B_END', B_END, 'C_END', C_END, 'SCR', SCR_BYTES)
    assert KH_ok(lay, SMAX) if False else True
    if B_X_END > COMMON_END:
        raise AssertionError("KH/VH overflow common region: %d > %d" % (B_X_END, COMMON_END))

    from contextlib import ExitStack
    es = ExitStack()
    with es:
        scr = es.enter_context(nc.sbuf_tensor("scr", [128, SCR_BYTES // 2], BF16))
        WQ = es.enter_context(nc.sbuf_tensor("wq", [128, 3 * 1152], BF16))
        WQROT = es.enter_context(nc.sbuf_tensor("wqrot", [128, 3 * 384], BF16))
        WKV = es.enter_context(nc.sbuf_tensor("wkv", [128, 2 * 1536], BF16))
        WIO = es.enter_context(nc.sbuf_tensor("wio", [128, 8 * 1024], BF16))
        CAT = es.enter_context(nc.sbuf_tensor("cat", [128, 8 * QMAX], BF16))
        GAINS = es.enter_context(nc.sbuf_tensor("gains_sb", [128, 24], F32))
        GFIN = es.enter_context(nc.sbuf_tensor("gfin_sb", [128, 1024], F32))
        CONST = es.enter_context(nc.sbuf_tensor("const_sb", [128, 4 * 128], BF16))
        STAT = es.enter_context(nc.sbuf_tensor("stat_sb", [128, 48], F32))
        F2C = es.enter_context(nc.sbuf_tensor("f2c_sb", [128, 64], BF16))
        ONESF = es.enter_context(nc.sbuf_tensor("onesf_sb", [128, 128], F32))
        psum = [es.enter_context(nc.psum_tensor("ps%d" % i, [128, 512], F32)) for i in range(8)]
        NS = 5
        sems = {e: es.enter_context(nc.semaphore("sem_" + e)) for e in ENGS}
        bsems = {e: es.enter_context(nc.semaphore("bsem_" + e)) for e in ENGS}
        dsems = {e: [es.enter_context(nc.semaphore("dsem_%s_%d" % (e, i))) for i in range(NDMASEM)]
                 for e in (SP, POOL)}
        block = es.enter_context(nc.Block())

        def sc(off, n, dt):
            assert off % 4 == 0
            if dt == BF16:
                return scr[:, off // 2: off // 2 + n]
            return scr[:, off // 2: off // 2 + 2 * n].bitcast(F32)

        def ps_bf(b):
            return psum[b][:, :].bitcast(BF16)

        EPSC = GAINS[:, 21:22]
        ident = CONST[:, 0:128]
        ones = CONST[:, 128:256]
        CcB = CONST[:, 256:384]
        ScB = CONST[:, 384:512]
        WQ3 = WQ[:, :].rearrange("p (c n) -> p c n", c=3)
        WQROT3 = WQROT[:, :].rearrange("p (c n) -> p c n", c=3)
        WKV3 = WKV[:, :].rearrange("p (c n) -> p c n", c=2)
        WIO3 = WIO[:, :].rearrange("p (c n) -> p c n", c=8)
        CAT3 = CAT[:, :].rearrange("p (c n) -> p c n", c=8)

        s.op(SP, lambda e: e.dma_start(out=GAINS[:, :], in_=gains_d), writes=["GAINS"], dma=True)
        s.op(SP, lambda e: e.dma_start(out=GFIN[:, :], in_=gfin_d), writes=["GFIN"], dma=True)
        s.op(POOL, lambda e: e.dma_start(out=CONST[:, :].rearrange("p (c n) -> p c n", c=4), in_=cst_d),
             writes=["CONST"], dma=True)
        s.op(POOL, lambda e: e.dma_start(out=WQ3, in_=w_q_d), writes=["WQ"], dma=True)
        s.op(POOL, lambda e: e.dma_start(out=WKV3, in_=w_kv_d), writes=["WKV"], dma=True)
        for h in range(NH):
            a = h * 192 + 128
            s.op(DVE, lambda e, h=h, a=a: e.tensor_scalar(
                out=WQROT3[:, :, h * 64: h * 64 + 32], in0=WQ3[:, :, a + 32: a + 64],
                scalar1=-1.0, scalar2=None, op0=ALU.mult), reads=["WQ"], writes=["WQROT"])
            s.op(DVE, lambda e, h=h, a=a: e.tensor_copy(
                out=WQROT3[:, :, h * 64 + 32: h * 64 + 64], in_=WQ3[:, :, a: a + 32]),
                reads=["WQ"], writes=["WQROT"])

        s.op(DVE, lambda e: e.memset(ONESF[:, :], 1.0), writes=["ONESF"])
        psrr = [0]
        xtc = [0]

        def load_w_in():
            s.op(POOL, lambda e: e.dma_start(out=WIO3[:, :, 0:256], in_=w_in_d[:, :, 0:256]), writes=["WIOF"], dma=True)
            s.op(POOL, lambda e: e.dma_start(out=WIO3[:, :, 256:960], in_=w_in_d[:, :, 256:960]), writes=["WIO"], dma=True)
            s.op(DVE, lambda e: e.tensor_scalar(out=WIO3[:, :, 960:992], in0=WIO3[:, :, 928:960],
                                                scalar1=-1.0, scalar2=None, op0=ALU.mult),
                 reads=["WIO"], writes=["WIO"])
            s.op(DVE, lambda e: e.tensor_copy(out=WIO3[:, :, 992:1024], in_=WIO3[:, :, 896:928]),
                 reads=["WIO"], writes=["WIO"])

        def load_w_out():
            s.op(POOL, lambda e: e.dma_start(out=WIO3, in_=w_out_d), writes=["WIO", "WIOF"], dma=True)

        gctr = [0]

        def norm_front(srcs):
            gi = gctr[0]
            gctr[0] += 1
            blk = gi % 2
            SSb = STAT[:, 8 * blk: 8 * blk + 4]
            RSb = STAT[:, 8 * blk + 4: 8 * blk + 8]
            kss = [("ss", blk, i) for i in range(4)]
            krs = ("rs", blk)
            XSs = []
            for i, (ap, key) in enumerate(srcs):
                xsb = (gi * 4 + i) % NXS
                XS = sc(lay["XS"][xsb], 1024, BF16)
                s.op(ACT, lambda e, XS=XS, ap=ap, i=i: e.activation(out=XS, in_=ap, func=AF.Square,
                                                                  accum_out=SSb[:, i:i + 1]),
                     reads=[key, kss[i]], writes=[kss[i], ("XS", xsb)])
                XSs.append((XS, xsb))
            s.op(ACT, lambda e: e.activation(out=RSb, in_=SSb, func=AF.Ln, scale=1.0 / D, bias=EPSC),
                 reads=kss + ["GAINS"], writes=[krs])
            s.op(ACT, lambda e: e.activation(out=RSb, in_=RSb, func=AF.Exp, scale=-0.5),
                 reads=[krs], writes=[krs])
            for i, (ap, key) in enumerate(srcs):
                XS, xsb = XSs[i]
                s.op(ACT, lambda e, XS=XS, ap=ap, i=i: e.activation(
                    out=XS, in_=ap, func=AF.Copy, scale=RSb[:, i:i + 1]),
                    reads=[key, krs], writes=[("XS", xsb)])
            return (gi, XSs)

        def norm_back(h, g_off):
            gi, XSs = h
            for i in range(4):
                XS, xsb = XSs[i]
                for c in range(8):
                    b = c // 2
                    o0 = (c % 2) * 512 + i * 128
                    s.op(PE, lambda e, c=c, b=b, o0=o0, XS=XS: e.transpose(
                        out=ps_bf(b)[:, o0:o0 + 128], in_=XS[:, c * 128:(c + 1) * 128], identity=ident),
                        reads=[("XS", xsb), "CONST"], writes=[("ps", b)])
            xnt_buf = gi % 2
            XNT = sc(lay["XNT"][xnt_buf], 4096, BF16).rearrange("p (c n) -> p c n", c=8)
            for c in range(8):
                b = c // 2
                o0 = (c % 2) * 512
                s.op(DVE, lambda e, c=c, b=b, o0=o0: e.tensor_scalar(
                    out=XNT[:, c, :], in0=ps_bf(b)[:, o0:o0 + 512],
                    scalar1=GAINS[:, g_off + c: g_off + c + 1], scalar2=None, op0=ALU.mult),
                    reads=[("ps", b), "GAINS"], writes=[("XNT", xnt_buf, c)])
            return XNT, xnt_buf

        def norm_group(srcs, g_off):
            return norm_back(norm_front(srcs), g_off)

        def ps_next(lo, hi):
            psrr[0] += 1
            return lo + psrr[0] % (hi - lo)

        def do_job(ji, jb):
            S, Q = jb["S"], jb["Q"]
            N2 = S // 128
            QP = Q // N2
            K2O = Q // 128
            TR = 2 * N2
            R = 128 // N2
            NG = N2 // 4
            NQC = Q // 512
            xd = dram[jb["x"]]
            tb = jb["tab"]
            etab = dram["etab_" + tb]
            f2c_d = dram["f2c_" + tb]
            ropec = dram["ropec_" + tb]
            ropeq = dram["ropeq_" + tb]
            yd = dram[jb["y"]]
            xbase = jb["xbase"]
            segs = jb["perm_segs"]
            own_off = jb["own_off"]

            def load_ctx_tile(t, buf):
                XT = sc(lay["XT"][buf], 1024, F32)
                for (dp, sp_, n) in segs:
                    src = xd[xbase + sp_ * N2 + t: xbase + (sp_ + n - 1) * N2 + t + 1: N2, :]
                    s.op(SP, lambda e, src=src, dp=dp, n=n: e.dma_start(out=XT[dp:dp + n, :], in_=src),
                         writes=[("XT", buf)], dma=True)
                return XT

            s.barrier()
            load_w_in()
            if "F" not in phases:
                return
            s.op(POOL, lambda e: e.dma_start(out=F2C[0:TR, 0:2 * K2O], in_=f2c_d), writes=["F2C"], dma=True)
            Bv = sc(lay["B"], N2 * 512, BF16)
            def ctx_front(g):
                srcs = []
                for i in range(4):
                    t = g * 4 + i
                    xtb = xtc[0] % NXT
                    xtc[0] += 1
                    srcs.append((load_ctx_tile(t, xtb), ("XT", xtb)))
                return norm_front(srcs)

            hcur = ctx_front(0)
            for g in range(NG):
                XNT, xb = norm_back(hcur, 0)
                hcur = ctx_front(g + 1) if g + 1 < NG else None

                def u_part(i, g=g, XNT=XNT, xb=xb):
                    t = g * 4 + i
                    ub = t % 2
                    eb = t % 3
                    U = sc(lay["U"][ub], 256, BF16)
                    ET = sc(lay["ET"][eb], 256, BF16)
                    s.op(POOL, lambda e, t=t, ET=ET: e.dma_start(out=ET, in_=etab[t * 128:(t + 1) * 128, :]),
                         writes=[("ET", eb)], dma=True)
                    pb = 4 if t % 2 == 0 else 7
                    for c in range(8):
                        s.op(PE, lambda e, c=c, i=i, pb=pb, XNT=XNT: e.matmul(
                            out=psum[pb][:, 0:256], lhsT=XNT[:, c, i * 128:(i + 1) * 128],
                            rhs=WIO3[:, c, 0:256], start=(c == 0), stop=(c == 7)),
                            reads=[("XNT", xb, c), "WIOF"], writes=[("ps", pb)])
                    s.op(DVE, lambda e, pb=pb, U=U: e.tensor_copy(out=U, in_=psum[pb][:, 0:256]),
                         reads=[("ps", pb)], writes=[("U", ub)])

                def s1_part(i, g=g):
                    t = g * 4 + i
                    ub = t % 2
                    eb = t % 3
                    U = sc(lay["U"][ub], 256, BF16)
                    ET = sc(lay["ET"][eb], 256, BF16)
                    sb = 5 + t % 2
                    for ri in range(2):
                        s.op(PE, lambda e, ri=ri, sb=sb, U=U, ET=ET: e.matmul(
                            out=psum[sb][:, ri * 256:(ri + 1) * 256], lhsT=ET[:, ri * 128:(ri + 1) * 128],
                            rhs=U, start=True, stop=True),
                            reads=[("U", ub), ("ET", eb)], writes=[("ps", sb)])
                    s.op(DVE, lambda e, t=t, sb=sb: e.tensor_scalar(
                        out=Bv[:, t * 512:(t + 1) * 512], in0=psum[sb][:, :], scalar1=1.0, scalar2=None, op0=ALU.mult),
                        reads=[("ps", sb)], writes=[("B", t)])

                u_part(0)
                for i in range(4):
                    if i + 1 < 4:
                        u_part(i + 1)
                    s1_part(i)
            s.barrier()
            if "f" in phases:
                return
            BT = sc(lay["BT"], 128 * 128, BF16)
            BT3 = BT.rearrange("p (f k) -> p f k", f=128)
            YC = sc(lay["YC"], 2 * Q, BF16)
            for fc in range(2):
                for fb in range(16):
                    b = fb % 2
                    for i in range(8):
                        f = fc * 128 + fb * 8 + i
                        s.op(PE, lambda e, f=f, b=b, i=i: e.transpose(
                            out=ps_bf(b)[0:TR, i * 128:(i + 1) * 128], in_=Bv[:, f:f + 256 * (TR - 1) + 1:256],
                            identity=ident), reads=["CONST"], writes=[("ps", b)])
                    if fb % 2:
                        s.op(ACT, lambda e, fb=fb, b=b: e.activation(
                            out=BT[0:TR, fb * 1024:(fb + 1) * 1024], in_=ps_bf(b)[0:TR, :], func=AF.Copy),
                            reads=[("ps", b)], writes=[("BT", fb)])
                    else:
                        s.op(DVE, lambda e, fb=fb, b=b: e.tensor_copy(
                            out=BT[0:TR, fb * 1024:(fb + 1) * 1024], in_=ps_bf(b)[0:TR, :]),
                            reads=[("ps", b)], writes=[("BT", fb)])
                for kb in range(8):
                    b = 2 + kb % 2
                    for i in range(16):
                        k1 = kb * 16 + i
                        s.op(PE, lambda e, k1=k1, b=b, i=i: e.matmul(
                            out=psum[b][:, i * 2 * K2O:(i + 1) * 2 * K2O], lhsT=BT3[0:TR, :, k1],
                            rhs=F2C[0:TR, 0:2 * K2O], start=True, stop=True),
                            reads=[("BT", x) for x in range(16)] + ["F2C"], writes=[("ps", b)])
                    k10 = kb * 16
                    t0 = k10 % N2
                    rr = k10 // N2
                    YCv = YC.rearrange("p (r t k q) -> p r t k q", r=2, t=N2, k=K2O, q=R)
                    for ri in range(2):
                        src_ap = psum[b][:, 0:16 * 2 * K2O].rearrange("p (i r k) -> p i r k", i=16, r=2)[:, :, ri, :]
                        dst_ap = YCv[:, ri, t0:t0 + 16, :, rr]
                        s.op(DVE, lambda e, src_ap=src_ap, dst_ap=dst_ap: e.tensor_copy(out=dst_ap, in_=src_ap),
                             reads=[("ps", b)], writes=[("YC", kb)])
                for qc in range(NQC):
                    b = 4 + qc % 2
                    s.op(PE, lambda e, b=b, qc=qc: e.matmul(out=psum[b][:, :], lhsT=CcB,
                                                          rhs=YC[:, qc * 512:(qc + 1) * 512], start=True, stop=False),
                         reads=[("YC", x) for x in range(8)] + ["CONST"], writes=[("ps", b)])
                    s.op(PE, lambda e, b=b, qc=qc: e.matmul(out=psum[b][:, :], lhsT=ScB,
                                                          rhs=YC[:, Q + qc * 512: Q + (qc + 1) * 512],
                                                          start=False, stop=True),
                         reads=[("YC", x) for x in range(8)] + ["CONST"], writes=[("ps", b)])
                    s.op(ACT, lambda e, b=b, qc=qc, fc=fc: e.activation(
                        out=CAT3[:, fc, qc * 512:(qc + 1) * 512], in_=psum[b][:, :], func=AF.Copy,
                        scale=1.0 / math.sqrt(64.0 * S)), reads=[("ps", b)], writes=[("CAT", fc, qc)])

            s.barrier()
            if "A" not in phases:
                return
            CKV = sc(lay["CKV"], 2 * S, BF16).rearrange("p (c n) -> p c n", c=2)
            KR = sc(lay["KR"], S, BF16)
            CQ = sc(lay["CQ"], 3 * Q, BF16).rearrange("p (c n) -> p c n", c=3)
            SQ = sc(lay["SQ"], 5 * 512, BF16).rearrange("p (c n) -> p c n", c=5)
            CST = sc(lay["CST"], 5 * 512, F32).rearrange("p (c n) -> p c n", c=5)
            nq = 4 * QP
            s.op(POOL, lambda e: e.memset(KR[64:128, :], 0.0), writes=["KRZ"])
            hcur = ctx_front(0)
            for g in range(NG):
                XNT, xb = norm_back(hcur, 0)
                hcur = ctx_front(g + 1) if g + 1 < NG else None
                rb = 0
                ROPE = sc(lay["ROPE"][rb], 1024, F32).rearrange("p (c n) -> p c n", c=2)
                s.op(SP, lambda e, g=g, ROPE=ROPE: e.dma_start(
                    out=ROPE[0:64, :, :], in_=ropec[:, :, g * 512:(g + 1) * 512].rearrange("c p n -> p c n")),
                    writes=[("ROPE", rb)], dma=True)
                def rhs_q(c, XNT=XNT):
                    if QP == 128:
                        return XNT[:, c, :]
                    return XNT[:, c, :].rearrange("p (i q) -> p i q", i=4)[:, :, 0:QP]
                outs = []
                for m in range(5):
                    b = 4 + m % 4 if m < 4 else 4
                    b = [4, 5, 6, 7, 4][m]
                    n = nq if m < 3 else 512
                    col0 = 256 + m * 128
                    for c in range(8):
                        if m < 3 and QP != 128:
                            o_ap = psum[b][:, 0:n].rearrange("p (i q) -> p i q", i=4)
                        else:
                            o_ap = psum[b][:, 0:n]
                        r_ap = rhs_q(c) if m < 3 else XNT[:, c, :]
                        s.op(PE, lambda e, c=c, m=m, o_ap=o_ap, col0=col0, r_ap=r_ap: e.matmul(
                            out=o_ap, lhsT=WIO3[:, c, col0:col0 + 128],
                            rhs=r_ap, start=(c == 0), stop=(c == 7)),
                            reads=[("XNT", xb, c), "WIO"], writes=[("ps", b)])
                    s.op(DVE, lambda e, b=b, m=m, n=n: e.tensor_copy(out=CST[:, m, 0:n], in_=psum[b][:, 0:n]),
                         reads=[("ps", b)], writes=[("CST", m)])
                    s.op(ACT, lambda e, b=b, m=m, n=n: e.activation(out=SQ[:, m, 0:n], in_=CST[:, m, 0:n], func=AF.Square),
                         reads=[("CST", m)], writes=[("SQ", m)])
                for (ms, n, dim, goff, which) in (((0, 1, 2), nq, 384.0, 16, 0), ((3, 4), 512, 256.0, 19, 1)):
                    b = 5 + which
                    for k, m in enumerate(ms):
                        s.op(PE, lambda e, b=b, m=m, n=n, k=k, ms=ms: e.matmul(
                            out=psum[b][:, 0:n], lhsT=ones, rhs=SQ[:, m, 0:n], start=(k == 0), stop=(k == len(ms) - 1)),
                            reads=[("SQ", m), "CONST"], writes=[("ps", b)])
                    RS = sc(lay["RSTD"][which], 512, F32)
                    s.op(ACT, lambda e, b=b, n=n, RS=RS, dim=dim: e.activation(
                        out=RS[:, 0:n], in_=psum[b][:, 0:n], func=AF.Ln, scale=1.0 / dim, bias=EPSC),
                        reads=[("ps", b), "GAINS"], writes=[("RSTD", which)])
                    s.op(ACT, lambda e, n=n, RS=RS: e.activation(out=RS[:, 0:n], in_=RS[:, 0:n], func=AF.Exp, scale=-0.5),
                        reads=[("RSTD", which)], writes=[("RSTD", which)])
                    for k, m in enumerate(ms):
                        if which == 0:
                            dst = CQ[:, k, g * nq:(g + 1) * nq]
                            wk = ("CQ", g)
                        else:
                            dst = CKV[:, k, g * 512:(g + 1) * 512]
                            wk = ("CKV", g)
                        s.op(DVE, lambda e, m=m, n=n, RS=RS, dst=dst, goff=goff, k=k: e.scalar_tensor_tensor(
                            out=dst, in0=CST[:, m, 0:n], scalar=GAINS[:, goff + k: goff + k + 1], in1=RS[:, 0:n],
                            op0=ALU.mult, op1=ALU.mult), reads=[("CST", m), ("RSTD", which), "GAINS"], writes=[wk])
                for k, (col0, b) in enumerate(((896, 7), (960, 4))):
                    for c in range(8):
                        s.op(PE, lambda e, c=c, b=b, col0=col0, XNT=XNT: e.matmul(
                            out=psum[b][0:64, :], lhsT=WIO3[:, c, col0:col0 + 64], rhs=XNT[:, c, :],
                            start=(c == 0), stop=(c == 7)), reads=[("XNT", xb, c), "WIO"], writes=[("ps", b)])
                    T12 = sc(lay["T12"][k], 512, F32)
                    s.op(DVE, lambda e, b=b, k=k, T12=T12, ROPE=ROPE: e.tensor_tensor(
                        out=T12[0:64, :], in0=psum[b][0:64, :], in1=ROPE[0:64, k, :], op=ALU.mult),
                        reads=[("ps", b), ("ROPE", rb)], writes=[("T12", k)])
                Ta = sc(lay["T12"][0], 512, F32)
                Tb = sc(lay["T12"][1], 512, F32)
                s.op(DVE, lambda e, g=g, Ta=Ta, Tb=Tb: e.tensor_tensor(
                    out=KR[0:64, g * 512:(g + 1) * 512], in0=Ta[0:64, :], in1=Tb[0:64, :], op=ALU.add),
                    reads=[("T12", 0), ("T12", 1)], writes=[("KR", g)])

            s.barrier()
            load_w_out()
            if "B" not in phases:
                return
            KH = sc(lay["KH"], S, BF16)
            VH = sc(lay["VH"], S, BF16).rearrange("p (t d) -> p t d", d=128)
            QN = sc(lay["QN"], Q, BF16)
            QRp = sc(lay["QR"], Q, BF16)
            RD = sc(lay["RD"], 512, F32)
            s.op(POOL, lambda e: e.memset(QRp[64:128, :], 0.0), writes=["QRZ"])
            if ji == 0:
                for j in range(NJ):
                    s.op(POOL, lambda e, j=j: e.dma_start(out=wg16[j], in_=w_g_d[j]), writes=["W16"], dma=True)
                    s.op(POOL, lambda e, j=j: e.dma_start(out=wu16[j], in_=w_u_d[j]), writes=["W16"], dma=True)
                for hf in range(2):
                    for j in range(NJ):
                        s.op(POOL, lambda e, j=j, hf=hf: e.dma_start(out=wd16[hf, j], in_=w_d_d[hf, j]), writes=["W16"], dma=True)
            allCKV = [("CKV", g) for g in range(NG)]
            allKR = [("KR", g) for g in range(NG)]
            allCQ = [("CQ", g) for g in range(NG)]
            for h in range(NH):
                for g in range(NG):
                    bk = (0, 2)[g % 2]
                    bv = (3, 4)[g % 2]
                    for c in range(2):
                        s.op(PE, lambda e, bk=bk, c=c, g=g, h=h: e.matmul(
                            out=psum[bk][:, :], lhsT=WKV3[:, c, h * 256:h * 256 + 128],
                            rhs=CKV[:, c, g * 512:(g + 1) * 512], start=(c == 0), stop=(c == 1)),
                            reads=[("CKV", g), "WKV"], writes=[("ps", bk)])
                    s.op(ACT, lambda e, bk=bk, g=g: e.activation(out=KH[:, g * 512:(g + 1) * 512], in_=psum[bk][:, :], func=AF.Copy),
                         reads=[("ps", bk)], writes=[("KH", g)])
                    for i in range(4):
                        for c in range(2):
                            s.op(PE, lambda e, bv=bv, c=c, g=g, i=i, h=h: e.matmul(
                                out=psum[bv][:, i * 128:(i + 1) * 128],
                                lhsT=CKV[:, c, (g * 4 + i) * 128:(g * 4 + i + 1) * 128],
                                rhs=WKV3[:, c, h * 256 + 128:h * 256 + 256], start=(c == 0), stop=(c == 1)),
                                reads=[("CKV", g), "WKV"], writes=[("ps", bv)])
                    s.op(DVE, lambda e, bv=bv, g=g: e.tensor_scalar(
                        out=VH[:, g * 4:(g + 1) * 4, :], in0=psum[bv][:, :].rearrange("p (t d) -> p t d", d=128),
                        scalar1=1.0, scalar2=None, op0=ALU.mult),
                        reads=[("ps", bv)], writes=[("VH", g)])
                for qc in range(NQC):
                    b = (0, 2)[qc % 2]
                    for c in range(3):
                        s.op(PE, lambda e, b=b, c=c, qc=qc, h=h: e.matmul(
                            out=psum[b][:, :], lhsT=WQ3[:, c, h * 192:h * 192 + 128],
                            rhs=CQ[:, c, qc * 512:(qc + 1) * 512], start=(c == 0), stop=(c == 2)),
                            reads=allCQ + ["WQ"], writes=[("ps", b)])
                    s.op(ACT, lambda e, b=b, qc=qc: e.activation(out=QN[:, qc * 512:(qc + 1) * 512], in_=psum[b][:, :], func=AF.Copy),
                         reads=[("ps", b)], writes=[("QN", qc)])
                    rb = qc % 2
                    ROPE = sc(lay["ROPEB"][rb], 1024, F32).rearrange("p (c n) -> p c n", c=2)
                    s.op(SP, lambda e, qc=qc, ROPE=ROPE: e.dma_start(
                        out=ROPE[0:64, :, :], in_=ropeq[:, :, qc * 512:(qc + 1) * 512].rearrange("c p n -> p c n")),
                        writes=[("ROPEB", rb)], dma=True)
                    for k in range(2):
                        b2 = 3 + k
                        for c in range(3):
                            lhs = WQ3[:, c, h * 192 + 128:h * 192 + 192] if k == 0 else WQROT3[:, c, h * 64:(h + 1) * 64]
                            s.op(PE, lambda e, b2=b2, c=c, qc=qc, lhs=lhs: e.matmul(
                                out=psum[b2][0:64, :], lhsT=lhs, rhs=CQ[:, c, qc * 512:(qc + 1) * 512],
                                start=(c == 0), stop=(c == 2)),
                                reads=allCQ + ["WQ", "WQROT"], writes=[("ps", b2)])
                        T12 = sc(lay["T12B"][k], 512, F32)
                        s.op(DVE, lambda e, b2=b2, k=k, T12=T12, ROPE=ROPE: e.tensor_tensor(
                            out=T12[0:64, :], in0=psum[b2][0:64, :], in1=ROPE[0:64, k, :], op=ALU.mult),
                            reads=[("ps", b2), ("ROPEB", rb)], writes=[("T12B", k)])
                    Ta = sc(lay["T12B"][0], 512, F32)
                    Tb = sc(lay["T12B"][1], 512, F32)
                    s.op(DVE, lambda e, qc=qc, Ta=Ta, Tb=Tb: e.tensor_tensor(
                        out=QRp[0:64, qc * 512:(qc + 1) * 512], in0=Ta[0:64, :], in1=Tb[0:64, :], op=ALU.add),
                        reads=[("T12B", 0), ("T12B", 1)], writes=[("QR", qc)])
                allKH = [("KH", g) for g in range(NG)]
                allVH = [("VH", g) for g in range(NG)]
                for qp in range(NQC // 2):
                    qcs = (2 * qp, 2 * qp + 1)
                    obs = (6, 7)
                    db = 1

                    def qk(kt, u):
                        qc = qcs[u]
                        qs = slice(qc * 512, (qc + 1) * 512)
                        bsc = 2 + 2 * u + kt % 2
                        s.op(PE, lambda e, bsc=bsc, kt=kt, qs=qs: e.matmul(
                            out=psum[bsc][:, :], lhsT=KH[:, kt * 128:(kt + 1) * 128], rhs=QN[:, qs],
                            start=True, stop=False),
                            reads=[("KH", kt // 4), ("QN", qc)], writes=[("ps", bsc)])
                        s.op(PE, lambda e, bsc=bsc, kt=kt, qs=qs: e.matmul(
                            out=psum[bsc][:, :], lhsT=KR[:, kt * 128:(kt + 1) * 128], rhs=QRp[:, qs],
                            start=False, stop=True),
                            reads=[("KR", kt // 4), ("QR", qc), "KRZ", "QRZ"], writes=[("ps", bsc)])

                    qk(0, 0)
                    qk(0, 1)
                    for kt in range(N2):
                        if kt + 1 < N2:
                            qk(kt + 1, 0)
                            qk(kt + 1, 1)
                        for u in range(2):
                            bsc = 2 + 2 * u + kt % 2
                            pb = 2 * u + kt % 2
                            PT = sc(lay["PT"][pb], 512, BF16)
                            s.op(ACT, lambda e, bsc=bsc, PT=PT: e.activation(out=PT, in_=psum[bsc][:, :], func=AF.Exp, scale=SM_SCALE),
                                 reads=[("ps", bsc)], writes=[("PT", pb)])
                            s.op(PE, lambda e, kt=kt, PT=PT, u=u: e.matmul(
                                out=psum[obs[u]][:, :], lhsT=VH[:, kt, :], rhs=PT, start=(kt == 0), stop=(kt == N2 - 1)),
                                reads=[("VH", kt // 4), ("PT", pb)], writes=[("ps", obs[u])])
                            ab = 2 * u + kt % 2
                            ACC = sc(lay["ACC"][ab], 512, F32)
                            if kt < 2:
                                s.op(DVE, lambda e, PT=PT, ACC=ACC: e.tensor_copy(out=ACC, in_=PT),
                                     reads=[("PT", pb)], writes=[("ACC", ab)])
                            else:
                                s.op(DVE, lambda e, PT=PT, ACC=ACC: e.tensor_tensor(out=ACC, in0=ACC, in1=PT, op=ALU.add),
                                     reads=[("PT", pb), ("ACC", ab)], writes=[("ACC", ab)])
                    OSBs = []
                    for u in range(2):
                        OSB = sc(lay["OSB"][u], 512, F32)
                        s.op(DVE, lambda e, u=u, OSB=OSB: e.tensor_copy(out=OSB, in_=psum[obs[u]][:, :]),
                             reads=[("ps", obs[u])], writes=[("OSB", u)])
                        OSBs.append(OSB)
                    for u in range(2):
                        qc = qcs[u]
                        qs = slice(qc * 512, (qc + 1) * 512)
                        ACC0 = sc(lay["ACC"][2 * u], 512, F32)
                        ACC1 = sc(lay["ACC"][2 * u + 1], 512, F32)
                        ACS = sc(lay["ACC"][4], 512, F32)
                        if N2 > 1:
                            s.op(DVE, lambda e, ACC0=ACC0, ACC1=ACC1, ACS=ACS: e.tensor_tensor(out=ACS, in0=ACC0, in1=ACC1, op=ALU.add),
                                 reads=[("ACC", 2 * u), ("ACC", 2 * u + 1)], writes=[("ACC", 4)])
                        else:
                            s.op(DVE, lambda e, ACC0=ACC0, ACS=ACS: e.tensor_copy(out=ACS, in_=ACC0),
                                 reads=[("ACC", 2 * u)], writes=[("ACC", 4)])
                        s.op(PE, lambda e, ACS=ACS: e.matmul(out=psum[db][:, :], lhsT=ONESF[:, :], rhs=ACS, start=True, stop=True),
                             reads=[("ACC", 4), "ONESF"], writes=[("ps", db)])
                        s.op(ACT, lambda e: e.activation(out=RD, in_=psum[db][:, :], func=AF.Ln),
                             reads=[("ps", db)], writes=["RD"])
                        s.op(ACT, lambda e: e.activation(out=RD, in_=RD, func=AF.Exp, scale=-1.0),
                             reads=["RD"], writes=["RD"])
                        s.op(DVE, lambda e, u=u, h=h, qs=qs, OSB=OSBs[u]: e.tensor_tensor(
                            out=CAT3[:, 2 + h, qs], in0=OSB, in1=RD, op=ALU.mult),
                            reads=[("OSB", u), "RD"], writes=[("CAT", 2 + h, qc)])

            s.barrier()
            if "C" not in phases:
                return
            HT = sc(lay["HT"], NJ * TC, BF16).rearrange("p (j n) -> p j n", j=NJ)
            NSUB = TC // 512
            for tg in range(Q // TC):
                XNTs = []
                for sub in range(NSUB):
                    srcs = []
                    for i in range(4):
                        ti = sub * 4 + i
                        qt = tg * (TC // 128) + ti
                        xtb = xtc[0] % NXT
                        xtc[0] += 1
                        XT = sc(lay["XT"][xtb], 1024, F32)
                        npt = 128 // QP
                        for a_ in range(npt):
                            t = qt * npt + a_
                            src_ = xd[xbase + own_off * N2 + t: xbase + (own_off + QP - 1) * N2 + t + 1: N2, :]
                            s.op(SP, lambda e, src_=src_, a_=a_, XT=XT: e.dma_start(out=XT[a_ * QP:(a_ + 1) * QP, :], in_=src_),
                                 writes=[("XT", xtb)], dma=True)
                        X1 = sc(lay["X1"][ti], 1024, F32)
                        for hf in range(2):
                            b = 4 + hf + 2 * (i % 2)
                            for c in range(8):
                                s.op(PE, lambda e, b=b, c=c, qt=qt, hf=hf: e.matmul(
                                    out=psum[b][:, :], lhsT=CAT3[:, c, qt * 128:(qt + 1) * 128],
                                    rhs=WIO3[:, c, hf * 512:(hf + 1) * 512], start=(c == 0), stop=(c == 7)),
                                    reads=[("CAT", c, qt // 4), "WIO"], writes=[("ps", b)])
                            s.op(DVE, lambda e, b=b, hf=hf, X1=X1, XT=XT: e.tensor_tensor(
                                out=X1[:, hf * 512:(hf + 1) * 512], in0=psum[b][:, :], in1=XT[:, hf * 512:(hf + 1) * 512],
                                op=ALU.add), reads=[("ps", b), ("XT", xtb)], writes=[("X1", ti)])
                        srcs.append((X1, ("X1", ti)))
                    XNTs.append(norm_front(srcs))
                XNTs = [norm_back(h_, 8) for h_ in XNTs]
                for j in range(NJ):
                    wb = j % 2
                    WG = sc(lay["WG"][wb], 1024, BF16).rearrange("p (c n) -> p c n", c=8)
                    WU = sc(lay["WU"][wb], 1024, BF16).rearrange("p (c n) -> p c n", c=8)
                    s.op(SP, lambda e, j=j, WG=WG: e.dma_start(out=WG, in_=wg16[j]), reads=["W16"], writes=[("WG", wb)], dma=True)
                    s.op(SP, lambda e, j=j, WU=WU: e.dma_start(out=WU, in_=wu16[j]), reads=["W16"], writes=[("WU", wb)], dma=True)
                    for sub in range(NSUB):
                        XNT, xb = XNTs[sub]
                        bg = 4 * (j % 2) + sub
                        bu = 4 * (j % 2) + 2 + sub
                        for c in range(8):
                            s.op(PE, lambda e, c=c, bg=bg, WG=WG, XNT=XNT: e.matmul(
                                out=psum[bg][:, :], lhsT=WG[:, c, :], rhs=XNT[:, c, :], start=(c == 0), stop=(c == 7)),
                                reads=[("WG", wb), ("XNT", xb, c)], writes=[("ps", bg)])
                        for c in range(8):
                            s.op(PE, lambda e, c=c, bu=bu, WU=WU, XNT=XNT: e.matmul(
                                out=psum[bu][:, :], lhsT=WU[:, c, :], rhs=XNT[:, c, :], start=(c == 0), stop=(c == 7)),
                                reads=[("WU", wb), ("XNT", xb, c)], writes=[("ps", bu)])
                        sgb = 0
                        SG = sc(lay["SG"][sgb], 512, F32)
                        s.op(ACT, lambda e, bg=bg, SG=SG: e.activation(out=SG, in_=psum[bg][:, :], func=AF.Silu),
                             reads=[("ps", bg)], writes=[("SG", sgb)])
                        s.op(DVE, lambda e, bu=bu, SG=SG, j=j, sub=sub: e.tensor_tensor(
                            out=HT[:, j, sub * 512:(sub + 1) * 512], in0=psum[bu][:, :], in1=SG, op=ALU.mult),
                            reads=[("ps", bu), ("SG", sgb)], writes=[("HT", j, sub)])
                NT8 = TC // 128
                for hf in range(2):
                    for j in range(NJ):
                        wb = (hf * NJ + j) % 2
                        WD = sc(lay["WD"][wb], 512, BF16)
                        s.op(SP, lambda e, j=j, hf=hf, WD=WD: e.dma_start(out=WD, in_=wd16[hf, j]),
                             reads=["W16"], writes=[("WD", wb)], dma=True)
                        for i in range(NT8):
                            s.op(PE, lambda e, i=i, j=j, WD=WD: e.matmul(
                                out=psum[i][:, :], lhsT=HT[:, j, i * 128:(i + 1) * 128], rhs=WD,
                                start=(j == 0), stop=(j == NJ - 1)),
                                reads=[("HT", j, i // 4), ("WD", wb)], writes=[("ps", i)])
                    for i in range(NT8):
                        X1 = sc(lay["X1"][i], 1024, F32)
                        s.op(DVE, lambda e, i=i, hf=hf, X1=X1: e.tensor_tensor(
                            out=X1[:, hf * 512:(hf + 1) * 512], in0=psum[i][:, :], in1=X1[:, hf * 512:(hf + 1) * 512],
                            op=ALU.add), reads=[("ps", i), ("X1", i)], writes=[("X1", i)])
                for sub in range(NSUB):
                    SSf = STAT[:, 16 + 8 * sub:20 + 8 * sub]
                    RSf = STAT[:, 20 + 8 * sub:24 + 8 * sub]
                    kssf = [("ssf", sub, i) for i in range(4)]
                    krsf = ("rsf", sub)
                    for i in range(4):
                        ti = sub * 4 + i
                        X1 = sc(lay["X1"][ti], 1024, F32)
                        YOj = sc(lay["XS"][i % NXS], 1024, BF16)
                        s.op(ACT, lambda e, X1=X1, SSf=SSf, YOj=YOj, i=i: e.activation(
                            out=YOj, in_=X1, func=AF.Square, accum_out=SSf[:, i:i + 1]),
                            reads=[("X1", ti), kssf[i]], writes=[kssf[i], ("XS", i % NXS)])
                    s.op(ACT, lambda e, SSf=SSf, RSf=RSf: e.activation(out=RSf, in_=SSf, func=AF.Ln, scale=1.0 / D, bias=EPSC),
                         reads=kssf + ["GAINS"], writes=[krsf])
                    s.op(ACT, lambda e, RSf=RSf: e.activation(out=RSf, in_=RSf, func=AF.Exp, scale=-0.5), reads=[krsf], writes=[krsf])
                    for i in range(4):
                        ti = sub * 4 + i
                        qt = tg * (TC // 128) + ti
                        X1 = sc(lay["X1"][ti], 1024, F32)
                        s.op(DVE, lambda e, X1=X1, RSf=RSf, i=i: e.scalar_tensor_tensor(
                            out=X1, in0=X1, scalar=RSf[:, i:i + 1], in1=GFIN[:, :], op0=ALU.mult, op1=ALU.mult),
                            reads=[("X1", ti), krsf, "GFIN"], writes=[("X1", ti)])
                        s.op(SP, lambda e, qt=qt, X1=X1: e.dma_start(out=yd[qt * 128:(qt + 1) * 128, :], in_=X1),
                             reads=[("X1", ti)], dma=True)
        for ji, jb in enumerate(jobs):
            do_job(ji, jb)
        s.barrier()

        s.prepare()

        @block.sync
        def _(e):
            s.emit(SP, e, sems, dsems, bsems)

        @block.gpsimd
        def _(e):
            s.emit(POOL, e, sems, dsems, bsems)

        @block.scalar
        def _(e):
            s.emit(ACT, e, sems, dsems, bsems)

        @block.vector
        def _(e):
            s.emit(DVE, e, sems, dsems, bsems)

        @block.tensor
        def _(e):
            s.emit(PE, e, sems, dsems, bsems)

    return nc


def KH_ok(lay, smax):
    return True


def _tables(S, Q, s1_of_p, own_k2_0):
    N2 = S // 128
    K2O = Q // 128
    t = np.arange(N2)[:, None]
    tok = (N2 * np.asarray(s1_of_p)[None, :] + t).astype(np.int64)
    k1 = np.arange(128, dtype=np.int64)
    ph = (tok[:, :, None] * k1[None, None, :]) % S
    ang = 2.0 * np.pi * ph.astype(np.float64) / S
    etab = np.concatenate([np.cos(ang), -np.sin(ang)], axis=2).reshape(S, 256).astype(np.float32)
    k2 = own_k2_0 + np.arange(K2O, dtype=np.int64)
    phi = 2.0 * np.pi * ((np.arange(N2, dtype=np.int64)[:, None] * k2[None, :]) % N2) / N2
    f2c = np.zeros((N2, 2, 2, K2O), np.float64)
    f2c[:, 0, 0, :] = np.cos(phi)
    f2c[:, 1, 0, :] = np.sin(phi)
    f2c[:, 0, 1, :] = -np.sin(phi)
    f2c[:, 1, 1, :] = np.cos(phi)
    f2c = f2c.reshape(2 * N2, 2 * K2O).astype(np.float32)
    inv_freq = (1.0 / (10000.0 ** (np.arange(0, 64, 2, dtype=np.float32) / 64.0))).astype(np.float32)
    pos = tok.reshape(-1).astype(np.float32)
    angr = pos[None, :] * inv_freq[:, None]
    cosr = np.cos(angr).astype(np.float32)
    sinr = np.sin(angr).astype(np.float32)
    ropec = np.stack([np.concatenate([cosr, cosr], 0), np.concatenate([sinr, sinr], 0)], 0)
    return etab, f2c, np.ascontiguousarray(ropec)


def _rope_q(ropec, S, Q):
    N2 = S // 128
    QP = Q // N2
    r = ropec.reshape(2, 64, N2, 128)[:, :, :, :QP]
    return np.ascontiguousarray(r.reshape(2, 64, Q))


def _consts():
    ident = np.eye(128, dtype=np.float32)
    ones = np.ones((128, 128), np.float32)
    c = np.arange(64)
    psi = 2.0 * np.pi * ((c[:, None] * c[None, :]) % 64) / 64.0
    Cc = np.zeros((128, 128), np.float64)
    Sc = np.zeros((128, 128), np.float64)
    for g in range(2):
        Cc[g * 64:(g + 1) * 64, g * 64:(g + 1) * 64] = np.cos(psi)
        Sc[g * 64:(g + 1) * 64, g * 64:(g + 1) * 64] = np.sin(psi)
    return np.ascontiguousarray(np.stack([ident, ones, Cc.astype(np.float32), Sc.astype(np.float32)], 1))


def _weight_maps(inp):
    f = lambda a: np.ascontiguousarray(np.asarray(a, dtype=np.float32))
    w_in = f(inp["w_in"])[0]
    m = {}
    m["w_in_r"] = f(w_in.reshape(8, 128, 960).transpose(1, 0, 2))
    m["w_q_r"] = f(f(inp["w_q_up"])[0].reshape(3, 128, 1152).transpose(1, 0, 2))
    m["w_kv_r"] = f(f(inp["w_kv_up"])[0].reshape(2, 128, 1536).transpose(1, 0, 2))
    m["w_out_r"] = f(f(inp["w_out"])[0].reshape(8, 128, 1024).transpose(1, 0, 2))
    m["w_gate_r"] = f(f(inp["w_gate"])[0].reshape(8, 128, NJ, 128).transpose(2, 1, 0, 3))
    m["w_up_r"] = f(f(inp["w_up"])[0].reshape(8, 128, NJ, 128).transpose(2, 1, 0, 3))
    m["w_down_r"] = f(f(inp["w_down"])[0].reshape(NJ, 128, 2, 512).transpose(2, 0, 1, 3))
    g = np.zeros((128, 24), np.float32)
    g[:, 0:8] = f(inp["norm_mix_g"])[0].reshape(8, 128).T
    g[:, 8:16] = f(inp["norm_ffn_g"])[0].reshape(8, 128).T
    g[:, 16:19] = f(inp["q_norm_g"])[0].reshape(3, 128).T
    g[:, 19:21] = f(inp["kv_norm_g"])[0].reshape(2, 128).T
    g[:, 21] = EPS
    m["gains"] = g
    m["gfin_bc"] = f(np.broadcast_to(f(inp["final_norm_g"])[None, :], (128, 1024)))
    m["consts"] = _consts()
    return m


SP_, QQ = 8192, 2048
SS_ = 2048


def _job_list():
    jobs = []
    QPp = QQ // (SP_ // 128)
    jobs.append(dict(S=SP_, Q=QQ, x="xp", xrows=SP_, xbase=0, perm_segs=[(0, 0, 128)], own_off=0, tab="p0", y="y0"))
    jobs.append(dict(S=SP_, Q=QQ, x="xp", xrows=SP_, xbase=0,
                     perm_segs=[(0, QPp, QPp), (QPp, 0, QPp), (2 * QPp, 2 * QPp, 128 - 2 * QPp)],
                     own_off=QPp, tab="p1", y="y1"))
    for k in range(4):
        jobs.append(dict(S=SS_, Q=QQ, x="xs", xrows=4 * SS_, xbase=k * SS_, perm_segs=[(0, 0, 128)], own_off=0,
                         tab="s", y="y%d" % (2 + k)))
    return jobs


def kernel(**inputs):
    xp = np.asarray(inputs["x_prompt"], dtype=np.float32)
    xs = np.asarray(inputs["x_sample"], dtype=np.float32)
    wm = _weight_maps(inputs)
    jobs = _job_list()
    nc = build_program(jobs)
    N2p = SP_ // 128
    QPp = QQ // N2p
    in_maps = []
    sigmas = []
    tabs_s = _tables(SS_, QQ, np.arange(128), 0)
    for core in range(8):
        b, half = core // 2, core % 2
        q0, q1 = 2 * half, 2 * half + 1
        own0 = np.arange(q0 * QPp, (q0 + 1) * QPp)
        own1 = np.arange(q1 * QPp, (q1 + 1) * QPp)
        rest = np.array([v for v in range(128) if v // QPp not in (q0, q1)])
        sigma = np.concatenate([own0, own1, rest])
        sigmas.append(sigma)
        m = dict(wm)
        xb = xp[b].reshape(128, N2p, D)[sigma].reshape(SP_, D)
        m["xp"] = np.ascontiguousarray(xb)
        m["xs"] = np.ascontiguousarray(xs[4 * core:4 * core + 4].reshape(4 * SS_, D))
        s1p0 = sigma
        s1p1 = np.concatenate([sigma[QPp:2 * QPp], sigma[0:QPp], sigma[2 * QPp:]])
        for tbn, s1p, qr in (("p0", s1p0, q0), ("p1", s1p1, q1)):
            et, f2, rc = _tables(SP_, QQ, s1p, qr * (QQ // 128))
            m["etab_" + tbn] = et
            m["f2c_" + tbn] = f2
            m["ropec_" + tbn] = rc
            m["ropeq_" + tbn] = _rope_q(rc, SP_, QQ)
        m["etab_s"], m["f2c_s"], m["ropec_s"] = tabs_s
        m["ropeq_s"] = _rope_q(tabs_s[2], SS_, QQ)
        in_maps.append(m)
    res = run_bass_kernel_spmd(nc, in_maps, core_ids=list(range(8)))
    y_prompt = np.zeros((4, SP_, D), np.float32)
    y_sample = np.zeros((32, SS_, D), np.float32)
    for core in range(8):
        r = res.results[core]
        b, half = core // 2, core % 2
        for jj in range(2):
            qr = 2 * half + jj
            y = np.asarray(r["y%d" % jj]).reshape(N2p, QPp, D)
            y_prompt[b].reshape(128, N2p, D)[qr * QPp:(qr + 1) * QPp] = y.transpose(1, 0, 2)
        for k in range(4):
            y = np.asarray(r["y%d" % (2 + k)]).reshape(SS_ // 128, 128, D)
            y_sample[4 * core + k] = y.transpose(1, 0, 2).reshape(SS_, D)
    return (y_prompt, y_sample)
```

```python
import math
import numpy as np
import concourse.bass as bass
import concourse.mybir as mybir
from concourse.bass_utils import run_bass_kernel_spmd

F32 = mybir.dt.float32
BF16 = mybir.dt.bfloat16
AF = mybir.ActivationFunctionType
ALU = mybir.AluOpType

D = 1024
NH = 6
DFF = 2816
NJ = DFF // 128
EPS = 1e-6
SM_SCALE = 1.0 / math.sqrt(192.0)
T = 512
TC = 1024

PE, ACT, DVE, POOL, SP = "tensor", "scalar", "vector", "gpsimd", "sync"
ENGS = [PE, ACT, DVE, POOL, SP]
NDMASEM = 24
NXT = 4
NXS = 8


class Op:
    __slots__ = ("eng", "fn", "deps", "signal", "seq", "dma", "dsem", "dval", "didx", "barrier", "raw")

    def __init__(self, eng, fn, dma):
        self.eng = eng
        self.fn = fn
        self.deps = []
        self.signal = False
        self.seq = 0
        self.dma = dma
        self.dsem = None
        self.dval = None
        self.didx = None
        self.barrier = 0
        self.raw = ()


class Sched:
    def __init__(self, same_engine_sync=True):
        self.ops = {e: [] for e in ENGS}
        self.last_writer = {}
        self.readers = {}
        self.same_engine_sync = same_engine_sync
        self.nbar = 0

    def op(self, eng, fn, reads=(), writes=(), dma=False):
        o = Op(eng, fn, dma)
        deps = {}
        raw = set()
        for k in reads:
            w = self.last_writer.get(k)
            if w is not None:
                deps[id(w)] = w
                raw.add(id(w))
        for k in writes:
            w = self.last_writer.get(k)
            if w is not None:
                deps[id(w)] = w
            for r in self.readers.get(k, ()):
                deps[id(r)] = r
        o.raw = raw
        for k in reads:
            lst = self.readers.setdefault(k, [])
            if not dma:
                lst[:] = [r for r in lst if r.dma or r.eng != eng]
            lst.append(o)
        for k in writes:
            self.last_writer[k] = o
            self.readers[k] = []
        o.deps = [d for d in deps.values() if d is not o]
        self.ops[eng].append(o)
        return o

    def barrier(self):
        self.nbar += 1
        for e in ENGS:
            o = Op(e, None, False)
            o.barrier = self.nbar
            self.ops[e].append(o)
        self.last_writer = {}
        self.readers = {}

    def plan(self):
        for e in ENGS:
            for o in self.ops[e]:
                for d in o.deps:
                    if d.dma:
                        continue
                    if d.eng == o.eng and not o.dma:
                        if d.eng == PE or not self.same_engine_sync or id(d) not in o.raw:
                            continue
                    d.signal = True
        for e in ENGS:
            c = 0
            i = 0
            for o in self.ops[e]:
                if o.barrier:
                    continue
                if o.dma:
                    o.didx = i
                    o.dsem = i % NDMASEM
                    o.dval = 16 * (i // NDMASEM + 1)
                    i += 1
                elif o.signal:
                    c += 1
                    o.seq = c

    def emit(self, eng, e, sems, dsems, bsems):
        waited = {f: 0 for f in ENGS}
        waited_dma = set()
        my_dmas = []
        for o in self.ops[eng]:
            if o.barrier:
                for d in my_dmas[-NDMASEM:]:
                    if id(d) not in waited_dma:
                        e.wait_ge(dsems[eng][d.dsem], d.dval)
                        waited_dma.add(id(d))
                e.drain().then_inc(bsems[eng], 1)
                for f in ENGS:
                    if f != eng:
                        e.wait_ge(bsems[f], o.barrier)
                for f in ENGS:
                    waited[f] = max(waited[f], self.bar_seq[f][o.barrier])
                continue
            for d in o.deps:
                if d.dma:
                    if id(d) not in waited_dma:
                        e.wait_ge(dsems[d.eng][d.dsem], d.dval)
                        waited_dma.add(id(d))
                else:
                    if d.eng == eng and not o.dma and (eng == PE or not self.same_engine_sync
                                                       or id(d) not in o.raw):
                        continue
                    if d.seq > waited[d.eng]:
                        e.wait_ge(sems[d.eng], d.seq)
                        waited[d.eng] = d.seq
            if o.dma:
                if o.didx >= NDMASEM:
                    p = my_dmas[o.didx - NDMASEM]
                    if id(p) not in waited_dma:
                        e.wait_ge(dsems[eng][p.dsem], p.dval)
                        waited_dma.add(id(p))
                my_dmas.append(o)
            inst = o.fn(e)
            if o.dma:
                inst.then_inc(dsems[eng][o.dsem], 16)
            elif o.signal:
                inst.then_inc(sems[eng], 1)

    def prepare(self):
        self.plan()
        self.bar_seq = {e: {} for e in ENGS}
        for e in ENGS:
            c = 0
            for o in self.ops[e]:
                if o.barrier:
                    self.bar_seq[e][o.barrier] = c
                elif (not o.dma) and o.signal:
                    c = o.seq


def build_program(jobs, phases="FABC", debug_out=False):
    nc = bass.Bass("TRN2", target_bir_lowering=False)
    s = Sched()
    dram = {}

    def din(name, shape):
        if name not in dram:
            dram[name] = nc.dram_tensor(name, list(shape), F32, kind="ExternalInput").ap()
        return dram[name]

    def dout(name, shape):
        dram[name] = nc.dram_tensor(name, list(shape), F32, kind="ExternalOutput").ap()
        return dram[name]

    w_in_d = din("w_in_r", [128, 8, 960])
    w_q_d = din("w_q_r", [128, 3, 1152])
    w_kv_d = din("w_kv_r", [128, 2, 1536])
    w_out_d = din("w_out_r", [128, 8, 1024])
    w_g_d = din("w_gate_r", [NJ, 128, 8, 128])
    w_u_d = din("w_up_r", [NJ, 128, 8, 128])
    w_d_d = din("w_down_r", [2, NJ, 128, 512])
    gains_d = din("gains", [128, 24])
    gfin_d = din("gfin_bc", [128, 1024])
    cst_d = din("consts", [128, 4, 128])

    for jb in jobs:
        S, Q = jb["S"], jb["Q"]
        N2 = S // 128
        tb = jb["tab"]
        din(jb["x"], [jb["xrows"], D])
        din("etab_" + tb, [S, 256])
        din("f2c_" + tb, [2 * N2, 2 * (Q // 128)])
        din("ropec_" + tb, [2, 128, S])
        din("ropeq_" + tb, [128, Q])
        dout(jb["y"], [Q, D])

    wg16 = nc.dram_tensor("wg16", [NJ, 128, 8, 128], BF16, kind="Internal").ap()
    wu16 = nc.dram_tensor("wu16", [NJ, 128, 8, 128], BF16, kind="Internal").ap()
    wd16 = nc.dram_tensor("wd16", [2, NJ, 128, 512], BF16, kind="Internal").ap()
    SMAX = max(jb["S"] for jb in jobs)
    QMAX = max(jb["Q"] for jb in jobs)
    N2MAX = SMAX // 128

    class Carver:
        def __init__(self):
            self.off = 0
            self.maxoff = 0

        def take(self, nbytes):
            o = self.off
            self.off += (nbytes + 31) // 32 * 32
            self.maxoff = max(self.maxoff, self.off)
            return o

    lay = {}
    cv = Carver()
    XPIPE0 = cv.take(0)
    lay["XT"] = [cv.take(4096) for _ in range(NXT)]
    lay["XS"] = [cv.take(2048) for _ in range(NXS)]
    lay["XNT"] = [cv.take(8192) for _ in range(2)]
    XPIPE1 = cv.off
    lay["BT"] = XPIPE0
    if XPIPE1 - XPIPE0 < 32768:
        cv.off = XPIPE0 + 32768
        cv.maxoff = max(cv.maxoff, cv.off)
    COMMON_END = cv.off
    lay["U"] = [cv.take(512) for _ in range(2)]
    lay["ET"] = [cv.take(512) for _ in range(3)]
    lay["B"] = cv.take(N2MAX * 1024)
    lay["YC"] = cv.take(2 * QMAX * 2)
    F0_END = cv.off
    cv.off = COMMON_END
    lay["CKV"] = cv.take(2 * SMAX * 2)
    lay["KR"] = cv.take(SMAX * 2)
    lay["CQ"] = cv.take(3 * QMAX * 2)
    AB0 = cv.off
    lay["ROPE"] = [cv.take(4096) for _ in range(1)]
    lay["SQ"] = cv.take(5 * 512 * 2)
    lay["RSTD"] = [cv.take(2048) for _ in range(2)]
    lay["CST"] = cv.take(5 * 512 * 4)
    lay["T12"] = [cv.take(2048) for _ in range(2)]
    A_END = cv.off
    cv.off = XPIPE0
    lay["KH"] = cv.take(SMAX * 2)
    lay["VH"] = cv.take(SMAX * 2)
    assert cv.off <= COMMON_END + 0 or True
    B_X_END = cv.off
    lay["ACC"] = [cv.take(2048) for _ in range(5)]
    assert cv.off <= COMMON_END, (cv.off, COMMON_END)
    cv.off = AB0
    lay["ROPEB"] = [cv.take(4096) for _ in range(2)]
    lay["T12B"] = [cv.take(2048) for _ in range(2)]
    lay["QN"] = cv.take(QMAX * 2)
    lay["QR"] = cv.take(QMAX * 2)
    lay["PT"] = [cv.take(1024) for _ in range(4)]
    lay["RD"] = cv.take(2048)
    lay["OSB"] = [cv.take(2048) for _ in range(2)]
    B_END = cv.off
    cv.off = COMMON_END
    lay["X1"] = [cv.take(4096) for _ in range(TC // 128)]
    lay["HT"] = cv.take(NJ * TC * 2)
    lay["SG"] = [cv.take(2048) for _ in range(1)]
    lay["WG"] = [cv.take(2048) for _ in range(2)]
    lay["WU"] = [cv.take(2048) for _ in range(2)]
    lay["WD"] = [cv.take(1024) for _ in range(2)]
    C_END = cv.off
    SCR_BYTES = cv.maxoff
    print('SBUF layout: F0_END', F0_END, 'A_END', A_END, 'B_END', B_END, 'C_END', C_END, 'SCR', SCR_BYTES)
    assert KH_ok(lay, SMAX) if False else True
    if B_X_END > COMMON_END:
        raise AssertionError("KH/VH overflow common region: %d > %d" % (B_X_END, COMMON_END))

    from contextlib import ExitStack
    es = ExitStack()
    with es:
        scr = es.enter_context(nc.sbuf_tensor("scr", [128, SCR_BYTES // 2], BF16))
        WQN = es.enter_context(nc.sbuf_tensor("wqn", [128, 3 * 768], BF16))
        WQRR = es.enter_context(nc.sbuf_tensor("wqrr", [128, 3 * 768], BF16))
        WKV = es.enter_context(nc.sbuf_tensor("wkv", [128, 2 * 1536], BF16))
        WIO = es.enter_context(nc.sbuf_tensor("wio", [128, 8 * 1024], BF16))
        CAT = es.enter_context(nc.sbuf_tensor("cat", [128, 8 * QMAX], BF16))
        GAINS = es.enter_context(nc.sbuf_tensor("gains_sb", [128, 24], F32))
        GFIN = es.enter_context(nc.sbuf_tensor("gfin_sb", [128, 1024], F32))
        CONST = es.enter_context(nc.sbuf_tensor("const_sb", [128, 4 * 128], BF16))
        STAT = es.enter_context(nc.sbuf_tensor("stat_sb", [128, 48], F32))
        F2C = es.enter_context(nc.sbuf_tensor("f2c_sb", [128, 64], BF16))
        ONESF = es.enter_context(nc.sbuf_tensor("onesf_sb", [128, 128], F32))
        psum = [es.enter_context(nc.psum_tensor("ps%d" % i, [128, 512], F32)) for i in range(8)]
        NS = 5
        sems = {e: es.enter_context(nc.semaphore("sem_" + e)) for e in ENGS}
        bsems = {e: es.enter_context(nc.semaphore("bsem_" + e)) for e in ENGS}
        dsems = {e: [es.enter_context(nc.semaphore("dsem_%s_%d" % (e, i))) for i in range(NDMASEM)]
                 for e in (SP, POOL)}
        block = es.enter_context(nc.Block())

        def sc(off, n, dt):
            assert off % 4 == 0
            if dt == BF16:
                return scr[:, off // 2: off // 2 + n]
            return scr[:, off // 2: off // 2 + 2 * n].bitcast(F32)

        def ps_bf(b):
            return psum[b][:, :].bitcast(BF16)

        EPSC = GAINS[:, 21:22]
        ident = CONST[:, 0:128]
        ones = CONST[:, 128:256]
        CcB = CONST[:, 256:384]
        ScB = CONST[:, 384:512]
        WQN4 = WQN[:, :].rearrange("p (c h n) -> p c h n", c=3, h=NH)
        WQRR4 = WQRR[:, :].rearrange("p (c h n) -> p c h n", c=3, h=NH)
        w_q_d4 = w_q_d.rearrange("p c (h n) -> p c h n", h=NH)
        WKV3 = WKV[:, :].rearrange("p (c n) -> p c n", c=2)
        WIO3 = WIO[:, :].rearrange("p (c n) -> p c n", c=8)
        CAT3 = CAT[:, :].rearrange("p (c n) -> p c n", c=8)

        s.op(SP, lambda e: e.dma_start(out=GAINS[:, :], in_=gains_d), writes=["GAINS"], dma=True)
        s.op(SP, lambda e: e.dma_start(out=GFIN[:, :], in_=gfin_d), writes=["GFIN"], dma=True)
        s.op(POOL, lambda e: e.dma_start(out=CONST[:, :].rearrange("p (c n) -> p c n", c=4), in_=cst_d),
             writes=["CONST"], dma=True)
        s.op(POOL, lambda e: e.dma_start(out=WQN4, in_=w_q_d4[:, :, :, 0:128]), writes=["WQ"], dma=True)
        s.op(POOL, lambda e: e.dma_start(out=WQRR4[:, :, :, 0:64], in_=w_q_d4[:, :, :, 128:192]), writes=["WQRR"], dma=True)
        s.op(POOL, lambda e: e.dma_start(out=WKV3, in_=w_kv_d), writes=["WKV"], dma=True)
        for c in range(3):
            s.op(DVE, lambda e, c=c: e.tensor_scalar(
                out=WQRR4[:, c, :, 64:96], in0=WQRR4[:, c, :, 32:64], scalar1=-1.0, scalar2=None, op0=ALU.mult),
                reads=["WQRR"], writes=["WQRR2"])
            s.op(DVE, lambda e, c=c: e.tensor_copy(out=WQRR4[:, c, :, 96:128], in_=WQRR4[:, c, :, 0:32]),
                 reads=["WQRR"], writes=["WQRR2"])

        s.op(DVE, lambda e: e.memset(ONESF[:, :], 1.0), writes=["ONESF"])
        psrr = [0]
        xtc = [0]

        def load_w_in():
            s.op(POOL, lambda e: e.dma_start(out=WIO3[:, :, 0:256], in_=w_in_d[:, :, 0:256]), writes=["WIOF"], dma=True)
            s.op(POOL, lambda e: e.dma_start(out=WIO3[:, :, 256:960], in_=w_in_d[:, :, 256:960]), writes=["WIO"], dma=True)
            s.op(DVE, lambda e: e.tensor_scalar(out=WIO3[:, :, 960:992], in0=WIO3[:, :, 928:960],
                                                scalar1=-1.0, scalar2=None, op0=ALU.mult),
                 reads=["WIO"], writes=["WIO"])
            s.op(DVE, lambda e: e.tensor_copy(out=WIO3[:, :, 992:1024], in_=WIO3[:, :, 896:928]),
                 reads=["WIO"], writes=["WIO"])

        def load_w_out():
            s.op(POOL, lambda e: e.dma_start(out=WIO3, in_=w_out_d), writes=["WIO", "WIOF"], dma=True)

        gctr = [0]

        def norm_front(srcs):
            gi = gctr[0]
            gctr[0] += 1
            blk = gi % 2
            SSb = STAT[:, 8 * blk: 8 * blk + 4]
            RSb = STAT[:, 8 * blk + 4: 8 * blk + 8]
            kss = [("ss", blk, i) for i in range(4)]
            krs = ("rs", blk)
            XSs = []
            for i, (ap, key) in enumerate(srcs):
                xsb = (gi * 4 + i) % NXS
                XS = sc(lay["XS"][xsb], 1024, BF16)
                s.op(ACT, lambda e, XS=XS, ap=ap, i=i: e.activation(out=XS, in_=ap, func=AF.Square,
                                                                  accum_out=SSb[:, i:i + 1]),
                     reads=[key, kss[i]], writes=[kss[i], ("XS", xsb)])
                XSs.append((XS, xsb))
            s.op(ACT, lambda e: e.activation(out=RSb, in_=SSb, func=AF.Ln, scale=1.0 / D, bias=EPSC),
                 reads=kss + ["GAINS"], writes=[krs])
            s.op(ACT, lambda e: e.activation(out=RSb, in_=RSb, func=AF.Exp, scale=-0.5),
                 reads=[krs], writes=[krs])
            for i, (ap, key) in enumerate(srcs):
                XS, xsb = XSs[i]
                s.op(ACT, lambda e, XS=XS, ap=ap, i=i: e.activation(
                    out=XS, in_=ap, func=AF.Copy, scale=RSb[:, i:i + 1]),
                    reads=[key, krs], writes=[("XS", xsb)])
            return (gi, XSs)

        def norm_back(h, g_off):
            gi, XSs = h
            for i in range(4):
                XS, xsb = XSs[i]
                for c in range(8):
                    b = c // 2
                    o0 = (c % 2) * 512 + i * 128
                    s.op(PE, lambda e, c=c, b=b, o0=o0, XS=XS: e.transpose(
                        out=ps_bf(b)[:, o0:o0 + 128], in_=XS[:, c * 128:(c + 1) * 128], identity=ident),
                        reads=[("XS", xsb), "CONST"], writes=[("ps", b)])
            xnt_buf = gi % 2
            XNT = sc(lay["XNT"][xnt_buf], 4096, BF16).rearrange("p (c n) -> p c n", c=8)
            for c in range(8):
                b = c // 2
                o0 = (c % 2) * 512
                s.op(DVE, lambda e, c=c, b=b, o0=o0: e.tensor_scalar(
                    out=XNT[:, c, :], in0=ps_bf(b)[:, o0:o0 + 512],
                    scalar1=GAINS[:, g_off + c: g_off + c + 1], scalar2=None, op0=ALU.mult),
                    reads=[("ps", b), "GAINS"], writes=[("XNT", xnt_buf, c)])
            return XNT, xnt_buf

        def norm_group(srcs, g_off):
            return norm_back(norm_front(srcs), g_off)

        def ps_next(lo, hi):
            psrr[0] += 1
            return lo + psrr[0] % (hi - lo)

        def do_job(ji, jb):
            S, Q = jb["S"], jb["Q"]
            N2 = S // 128
            QP = Q // N2
            K2O = Q // 128
            TR = 2 * N2
            R = 128 // N2
            NG = N2 // 4
            NQC = Q // 512
            xd = dram[jb["x"]]
            tb = jb["tab"]
            etab = dram["etab_" + tb]
            f2c_d = dram["f2c_" + tb]
            ropec = dram["ropec_" + tb]
            ropeq = dram["ropeq_" + tb]
            yd = dram[jb["y"]]
            xbase = jb["xbase"]
            segs = jb["perm_segs"]
            own_off = jb["own_off"]

            def load_ctx_tile(t, buf):
                XT = sc(lay["XT"][buf], 1024, F32)
                for (dp, sp_, n) in segs:
                    src = xd[xbase + sp_ * N2 + t: xbase + (sp_ + n - 1) * N2 + t + 1: N2, :]
                    s.op(SP, lambda e, src=src, dp=dp, n=n: e.dma_start(out=XT[dp:dp + n, :], in_=src),
                         writes=[("XT", buf)], dma=True)
                return XT

            s.barrier()
            load_w_in()
            if "F" not in phases:
                return
            s.op(POOL, lambda e: e.dma_start(out=F2C[0:TR, 0:2 * K2O], in_=f2c_d), writes=["F2C"], dma=True)
            Bv = sc(lay["B"], N2 * 512, BF16)
            def ctx_front(g):
                srcs = []
                for i in range(4):
                    t = g * 4 + i
                    xtb = xtc[0] % NXT
                    xtc[0] += 1
                    srcs.append((load_ctx_tile(t, xtb), ("XT", xtb)))
                return norm_front(srcs)

            hcur = ctx_front(0)
            for g in range(NG):
                XNT, xb = norm_back(hcur, 0)
                hcur = ctx_front(g + 1) if g + 1 < NG else None

                def u_part(i, g=g, XNT=XNT, xb=xb):
                    t = g * 4 + i
                    ub = t % 2
                    eb = t % 3
                    U = sc(lay["U"][ub], 256, BF16)
                    ET = sc(lay["ET"][eb], 256, BF16)
                    s.op(POOL, lambda e, t=t, ET=ET: e.dma_start(out=ET, in_=etab[t * 128:(t + 1) * 128, :]),
                         writes=[("ET", eb)], dma=True)
                    pb = 4 if t % 2 == 0 else 7
                    for c in range(8):
                        s.op(PE, lambda e, c=c, i=i, pb=pb, XNT=XNT: e.matmul(
                            out=psum[pb][:, 0:256], lhsT=XNT[:, c, i * 128:(i + 1) * 128],
                            rhs=WIO3[:, c, 0:256], start=(c == 0), stop=(c == 7)),
                            reads=[("XNT", xb, c), "WIOF"], writes=[("ps", pb)])
                    s.op(DVE, lambda e, pb=pb, U=U: e.tensor_copy(out=U, in_=psum[pb][:, 0:256]),
                         reads=[("ps", pb)], writes=[("U", ub)])

                def s1_part(i, g=g):
                    t = g * 4 + i
                    ub = t % 2
                    eb = t % 3
                    U = sc(lay["U"][ub], 256, BF16)
                    ET = sc(lay["ET"][eb], 256, BF16)
                    sb = 5 + t % 2
                    for ri in range(2):
                        s.op(PE, lambda e, ri=ri, sb=sb, U=U, ET=ET: e.matmul(
                            out=psum[sb][:, ri * 256:(ri + 1) * 256], lhsT=ET[:, ri * 128:(ri + 1) * 128],
                            rhs=U, start=True, stop=True),
                            reads=[("U", ub), ("ET", eb)], writes=[("ps", sb)])
                    s.op(DVE, lambda e, t=t, sb=sb: e.tensor_scalar(
                        out=Bv[:, t * 512:(t + 1) * 512], in0=psum[sb][:, :], scalar1=1.0, scalar2=None, op0=ALU.mult),
                        reads=[("ps", sb)], writes=[("B", t)])

                u_part(0)
                for i in range(4):
                    if i + 1 < 4:
                        u_part(i + 1)
                    s1_part(i)
            s.barrier()
            if "f" in phases:
                return
            BT = sc(lay["BT"], 128 * 128, BF16)
            BT3 = BT.rearrange("p (f k) -> p f k", f=128)
            YC = sc(lay["YC"], 2 * Q, BF16)
            for fc in range(2):
                for fb in range(16):
                    b = fb % 2
                    for i in range(8):
                        f = fc * 128 + fb * 8 + i
                        s.op(PE, lambda e, f=f, b=b, i=i: e.transpose(
                            out=ps_bf(b)[0:TR, i * 128:(i + 1) * 128], in_=Bv[:, f:f + 256 * (TR - 1) + 1:256],
                            identity=ident), reads=["CONST"], writes=[("ps", b)])
                    if fb % 2:
                        s.op(ACT, lambda e, fb=fb, b=b: e.activation(
                            out=BT[0:TR, fb * 1024:(fb + 1) * 1024], in_=ps_bf(b)[0:TR, :], func=AF.Copy),
                            reads=[("ps", b)], writes=[("BT", fb)])
                    else:
                        s.op(DVE, lambda e, fb=fb, b=b: e.tensor_copy(
                            out=BT[0:TR, fb * 1024:(fb + 1) * 1024], in_=ps_bf(b)[0:TR, :]),
                            reads=[("ps", b)], writes=[("BT", fb)])
                for kb in range(8):
                    b = 2 + kb % 2
                    for i in range(16):
                        k1 = kb * 16 + i
                        s.op(PE, lambda e, k1=k1, b=b, i=i: e.matmul(
                            out=psum[b][:, i * 2 * K2O:(i + 1) * 2 * K2O], lhsT=BT3[0:TR, :, k1],
                            rhs=F2C[0:TR, 0:2 * K2O], start=True, stop=True),
                            reads=[("BT", x) for x in range(16)] + ["F2C"], writes=[("ps", b)])
                    k10 = kb * 16
                    t0 = k10 % N2
                    rr = k10 // N2
                    YCv = YC.rearrange("p (r t k q) -> p r t k q", r=2, t=N2, k=K2O, q=R)
                    for ri in range(2):
                        src_ap = psum[b][:, 0:16 * 2 * K2O].rearrange("p (i r k) -> p i r k", i=16, r=2)[:, :, ri, :]
                        dst_ap = YCv[:, ri, t0:t0 + 16, :, rr]
                        s.op(DVE, lambda e, src_ap=src_ap, dst_ap=dst_ap: e.tensor_copy(out=dst_ap, in_=src_ap),
                             reads=[("ps", b)], writes=[("YC", kb)])
                for qc in range(NQC):
                    b = 4 + qc % 2
                    s.op(PE, lambda e, b=b, qc=qc: e.matmul(out=psum[b][:, :], lhsT=CcB,
                                                          rhs=YC[:, qc * 512:(qc + 1) * 512], start=True, stop=False),
                         reads=[("YC", x) for x in range(8)] + ["CONST"], writes=[("ps", b)])
                    s.op(PE, lambda e, b=b, qc=qc: e.matmul(out=psum[b][:, :], lhsT=ScB,
                                                          rhs=YC[:, Q + qc * 512: Q + (qc + 1) * 512],
                                                          start=False, stop=True),
                         reads=[("YC", x) for x in range(8)] + ["CONST"], writes=[("ps", b)])
                    s.op(ACT, lambda e, b=b, qc=qc, fc=fc: e.activation(
                        out=CAT3[:, fc, qc * 512:(qc + 1) * 512], in_=psum[b][:, :], func=AF.Copy,
                        scale=1.0 / math.sqrt(64.0 * S)), reads=[("ps", b)], writes=[("CAT", fc, qc)])

            s.barrier()
            if "A" not in phases:
                return
            CKV = sc(lay["CKV"], 2 * S, BF16).rearrange("p (c n) -> p c n", c=2)
            KR = sc(lay["KR"], S, BF16)
            CQ = sc(lay["CQ"], 3 * Q, BF16).rearrange("p (c n) -> p c n", c=3)
            SQ = sc(lay["SQ"], 5 * 512, BF16).rearrange("p (c n) -> p c n", c=5)
            CST = sc(lay["CST"], 5 * 512, F32).rearrange("p (c n) -> p c n", c=5)
            nq = 4 * QP
            for k_, (dst0, src0) in enumerate(((0, 896), (64, 896), (128, 960), (192, 960))):
                s.op(DVE, lambda e, dst0=dst0, src0=src0: e.tensor_copy(
                    out=WIO3[:, :, dst0:dst0 + 64], in_=WIO3[:, :, src0:src0 + 64]),
                    reads=["WIO"], writes=["WIOK"])
            hcur = ctx_front(0)
            for g in range(NG):
                XNT, xb = norm_back(hcur, 0)
                hcur = ctx_front(g + 1) if g + 1 < NG else None
                rb = 0
                ROPE = sc(lay["ROPE"][rb], 1024, F32).rearrange("p (c n) -> p c n", c=2)
                s.op(SP, lambda e, g=g, ROPE=ROPE: e.dma_start(
                    out=ROPE[:, :, :], in_=ropec[:, :, g * 512:(g + 1) * 512].rearrange("c p n -> p c n")),
                    writes=[("ROPE", rb)], dma=True)
                def rhs_q(c, XNT=XNT):
                    if QP == 128:
                        return XNT[:, c, :]
                    return XNT[:, c, :].rearrange("p (i q) -> p i q", i=4)[:, :, 0:QP]
                outs = []
                for m in range(5):
                    b = 4 + m % 4 if m < 4 else 4
                    b = [4, 5, 6, 7, 4][m]
                    n = nq if m < 3 else 512
                    col0 = 256 + m * 128
                    for c in range(8):
                        if m < 3 and QP != 128:
                            o_ap = psum[b][:, 0:n].rearrange("p (i q) -> p i q", i=4)
                        else:
                            o_ap = psum[b][:, 0:n]
                        r_ap = rhs_q(c) if m < 3 else XNT[:, c, :]
                        s.op(PE, lambda e, c=c, m=m, o_ap=o_ap, col0=col0, r_ap=r_ap: e.matmul(
                            out=o_ap, lhsT=WIO3[:, c, col0:col0 + 128],
                            rhs=r_ap, start=(c == 0), stop=(c == 7)),
                            reads=[("XNT", xb, c), "WIO"], writes=[("ps", b)])
                    s.op(DVE, lambda e, b=b, m=m, n=n: e.tensor_copy(out=CST[:, m, 0:n], in_=psum[b][:, 0:n]),
                         reads=[("ps", b)], writes=[("CST", m)])
                    s.op(ACT, lambda e, b=b, m=m, n=n: e.activation(out=SQ[:, m, 0:n], in_=CST[:, m, 0:n], func=AF.Square),
                         reads=[("CST", m)], writes=[("SQ", m)])
                for (ms, n, dim, goff, which) in (((0, 1, 2), nq, 384.0, 16, 0), ((3, 4), 512, 256.0, 19, 1)):
                    b = 5 + which
                    for k, m in enumerate(ms):
                        s.op(PE, lambda e, b=b, m=m, n=n, k=k, ms=ms: e.matmul(
                            out=psum[b][:, 0:n], lhsT=ones, rhs=SQ[:, m, 0:n], start=(k == 0), stop=(k == len(ms) - 1)),
                            reads=[("SQ", m), "CONST"], writes=[("ps", b)])
                    RS = sc(lay["RSTD"][which], 512, F32)
                    s.op(ACT, lambda e, b=b, n=n, RS=RS, dim=dim: e.activation(
                        out=RS[:, 0:n], in_=psum[b][:, 0:n], func=AF.Ln, scale=1.0 / dim, bias=EPSC),
                        reads=[("ps", b), "GAINS"], writes=[("RSTD", which)])
                    s.op(ACT, lambda e, n=n, RS=RS: e.activation(out=RS[:, 0:n], in_=RS[:, 0:n], func=AF.Exp, scale=-0.5),
                        reads=[("RSTD", which)], writes=[("RSTD", which)])
                    for k, m in enumerate(ms):
                        if which == 0:
                            dst = CQ[:, k, g * nq:(g + 1) * nq]
                            wk = ("CQ", g)
                        else:
                            dst = CKV[:, k, g * 512:(g + 1) * 512]
                            wk = ("CKV", g)
                        s.op(DVE, lambda e, m=m, n=n, RS=RS, dst=dst, goff=goff, k=k: e.scalar_tensor_tensor(
                            out=dst, in0=CST[:, m, 0:n], scalar=GAINS[:, goff + k: goff + k + 1], in1=RS[:, 0:n],
                            op0=ALU.mult, op1=ALU.mult), reads=[("CST", m), ("RSTD", which), "GAINS"], writes=[wk])
                for k, (col0, b) in enumerate(((0, 7), (128, 4))):
                    for c in range(8):
                        s.op(PE, lambda e, c=c, b=b, col0=col0, XNT=XNT: e.matmul(
                            out=psum[b][:, :], lhsT=WIO3[:, c, col0:col0 + 128], rhs=XNT[:, c, :],
                            start=(c == 0), stop=(c == 7)), reads=[("XNT", xb, c), "WIOK"], writes=[("ps", b)])
                    T12 = sc(lay["T12"][k], 512, F32)
                    s.op(DVE, lambda e, b=b, k=k, T12=T12, ROPE=ROPE: e.tensor_tensor(
                        out=T12, in0=psum[b][:, :], in1=ROPE[:, k, :], op=ALU.mult),
                        reads=[("ps", b), ("ROPE", rb)], writes=[("T12", k)])
                Ta = sc(lay["T12"][0], 512, F32)
                Tb = sc(lay["T12"][1], 512, F32)
                s.op(DVE, lambda e, g=g, Ta=Ta, Tb=Tb: e.tensor_tensor(
                    out=KR[:, g * 512:(g + 1) * 512], in0=Ta, in1=Tb, op=ALU.add),
                    reads=[("T12", 0), ("T12", 1)], writes=[("KR", g)])

            s.barrier()
            load_w_out()
            if "B" not in phases:
                return
            KH = sc(lay["KH"], S, BF16)
            VH = sc(lay["VH"], S, BF16).rearrange("p (t d) -> p t d", d=128)
            QN = sc(lay["QN"], Q, BF16)
            QRp = sc(lay["QR"], Q, BF16)
            RD = sc(lay["RD"], 512, F32)
            if ji == 0:
                for j in range(NJ):
                    s.op(POOL, lambda e, j=j: e.dma_start(out=wg16[j], in_=w_g_d[j]), writes=["W16"], dma=True)
                    s.op(POOL, lambda e, j=j: e.dma_start(out=wu16[j], in_=w_u_d[j]), writes=["W16"], dma=True)
                for hf in range(2):
                    for j in range(NJ):
                        s.op(POOL, lambda e, j=j, hf=hf: e.dma_start(out=wd16[hf, j], in_=w_d_d[hf, j]), writes=["W16"], dma=True)
            allCKV = [("CKV", g) for g in range(NG)]
            allKR = [("KR", g) for g in range(NG)]
            allCQ = [("CQ", g) for g in range(NG)]
            for h in range(NH):
                for g in range(NG):
                    bk = (0, 2)[g % 2]
                    bv = (3, 4)[g % 2]
                    for c in range(2):
                        s.op(PE, lambda e, bk=bk, c=c, g=g, h=h: e.matmul(
                            out=psum[bk][:, :], lhsT=WKV3[:, c, h * 256:h * 256 + 128],
                            rhs=CKV[:, c, g * 512:(g + 1) * 512], start=(c == 0), stop=(c == 1)),
                            reads=[("CKV", g), "WKV"], writes=[("ps", bk)])
                    s.op(ACT, lambda e, bk=bk, g=g: e.activation(out=KH[:, g * 512:(g + 1) * 512], in_=psum[bk][:, :], func=AF.Copy),
                         reads=[("ps", bk)], writes=[("KH", g)])
                    for i in range(4):
                        for c in range(2):
                            s.op(PE, lambda e, bv=bv, c=c, g=g, i=i, h=h: e.matmul(
                                out=psum[bv][:, i * 128:(i + 1) * 128],
                                lhsT=CKV[:, c, (g * 4 + i) * 128:(g * 4 + i + 1) * 128],
                                rhs=WKV3[:, c, h * 256 + 128:h * 256 + 256], start=(c == 0), stop=(c == 1)),
                                reads=[("CKV", g), "WKV"], writes=[("ps", bv)])
                    s.op(DVE, lambda e, bv=bv, g=g: e.tensor_scalar(
                        out=VH[:, g * 4:(g + 1) * 4, :], in0=psum[bv][:, :].rearrange("p (t d) -> p t d", d=128),
                        scalar1=1.0, scalar2=None, op0=ALU.mult),
                        reads=[("ps", bv)], writes=[("VH", g)])
                for qc in range(NQC):
                    b = (0, 2)[qc % 2]
                    for c in range(3):
                        s.op(PE, lambda e, b=b, c=c, qc=qc, h=h: e.matmul(
                            out=psum[b][:, :], lhsT=WQN4[:, c, h, :],
                            rhs=CQ[:, c, qc * 512:(qc + 1) * 512], start=(c == 0), stop=(c == 2)),
                            reads=allCQ + ["WQ"], writes=[("ps", b)])
                    s.op(ACT, lambda e, b=b, qc=qc: e.activation(out=QN[:, qc * 512:(qc + 1) * 512], in_=psum[b][:, :], func=AF.Copy),
                         reads=[("ps", b)], writes=[("QN", qc)])
                    rb = qc % 2
                    RQ = sc(lay["ROPEB"][rb], 512, F32)
                    s.op(SP, lambda e, qc=qc, RQ=RQ: e.dma_start(out=RQ, in_=ropeq[:, qc * 512:(qc + 1) * 512]),
                         writes=[("ROPEB", rb)], dma=True)
                    b2 = 3 + qc % 2
                    for c in range(3):
                        s.op(PE, lambda e, b2=b2, c=c, qc=qc, h=h: e.matmul(
                            out=psum[b2][:, :], lhsT=WQRR4[:, c, h, :], rhs=CQ[:, c, qc * 512:(qc + 1) * 512],
                            start=(c == 0), stop=(c == 2)),
                            reads=allCQ + ["WQRR", "WQRR2"], writes=[("ps", b2)])
                    s.op(DVE, lambda e, b2=b2, qc=qc, RQ=RQ: e.tensor_tensor(
                        out=QRp[:, qc * 512:(qc + 1) * 512], in0=psum[b2][:, :], in1=RQ, op=ALU.mult),
                        reads=[("ps", b2), ("ROPEB", rb)], writes=[("QR", qc)])
                allKH = [("KH", g) for g in range(NG)]
                allVH = [("VH", g) for g in range(NG)]
                for qp in range(NQC // 2):
                    qcs = (2 * qp, 2 * qp + 1)
                    obs = (6, 7)
                    db = 1

                    def qk(kt, u):
                        qc = qcs[u]
                        qs = slice(qc * 512, (qc + 1) * 512)
                        bsc = 2 + 2 * u + kt % 2
                        s.op(PE, lambda e, bsc=bsc, kt=kt, qs=qs: e.matmul(
                            out=psum[bsc][:, :], lhsT=KH[:, kt * 128:(kt + 1) * 128], rhs=QN[:, qs],
                            start=True, stop=False),
                            reads=[("KH", kt // 4), ("QN", qc)], writes=[("ps", bsc)])
                        s.op(PE, lambda e, bsc=bsc, kt=kt, qs=qs: e.matmul(
                            out=psum[bsc][:, :], lhsT=KR[:, kt * 128:(kt + 1) * 128], rhs=QRp[:, qs],
                            start=False, stop=True),
                            reads=[("KR", kt // 4), ("QR", qc)], writes=[("ps", bsc)])

                    qk(0, 0)
                    qk(0, 1)
                    for kt in range(N2):
                        if kt + 1 < N2:
                            qk(kt + 1, 0)
                            qk(kt + 1, 1)
                        for u in range(2):
                            bsc = 2 + 2 * u + kt % 2
                            pb = 2 * u + kt % 2
                            PT = sc(lay["PT"][pb], 512, BF16)
                            s.op(ACT, lambda e, bsc=bsc, PT=PT: e.activation(out=PT, in_=psum[bsc][:, :], func=AF.Exp, scale=SM_SCALE),
                                 reads=[("ps", bsc)], writes=[("PT", pb)])
                            s.op(PE, lambda e, kt=kt, PT=PT, u=u: e.matmul(
                                out=psum[obs[u]][:, :], lhsT=VH[:, kt, :], rhs=PT, start=(kt == 0), stop=(kt == N2 - 1)),
                                reads=[("VH", kt // 4), ("PT", pb)], writes=[("ps", obs[u])])
                            ab = 2 * u + kt % 2
                            ACC = sc(lay["ACC"][ab], 512, F32)
                            if kt < 2:
                                s.op(DVE, lambda e, PT=PT, ACC=ACC: e.tensor_copy(out=ACC, in_=PT),
                                     reads=[("PT", pb)], writes=[("ACC", ab)])
                            else:
                                s.op(DVE, lambda e, PT=PT, ACC=ACC: e.tensor_tensor(out=ACC, in0=ACC, in1=PT, op=ALU.add),
                                     reads=[("PT", pb), ("ACC", ab)], writes=[("ACC", ab)])
                    OSBs = []
                    for u in range(2):
                        OSB = sc(lay["OSB"][u], 512, F32)
                        s.op(DVE, lambda e, u=u, OSB=OSB: e.tensor_copy(out=OSB, in_=psum[obs[u]][:, :]),
                             reads=[("ps", obs[u])], writes=[("OSB", u)])
                        OSBs.append(OSB)
                    for u in range(2):
                        qc = qcs[u]
                        qs = slice(qc * 512, (qc + 1) * 512)
                        ACC0 = sc(lay["ACC"][2 * u], 512, F32)
                        ACC1 = sc(lay["ACC"][2 * u + 1], 512, F32)
                        ACS = sc(lay["ACC"][4], 512, F32)
                        if N2 > 1:
                            s.op(DVE, lambda e, ACC0=ACC0, ACC1=ACC1, ACS=ACS: e.tensor_tensor(out=ACS, in0=ACC0, in1=ACC1, op=ALU.add),
                                 reads=[("ACC", 2 * u), ("ACC", 2 * u + 1)], writes=[("ACC", 4)])
                        else:
                            s.op(DVE, lambda e, ACC0=ACC0, ACS=ACS: e.tensor_copy(out=ACS, in_=ACC0),
                                 reads=[("ACC", 2 * u)], writes=[("ACC", 4)])
                        s.op(PE, lambda e, ACS=ACS: e.matmul(out=psum[db][:, :], lhsT=ONESF[:, :], rhs=ACS, start=True, stop=True),
                             reads=[("ACC", 4), "ONESF"], writes=[("ps", db)])
                        s.op(ACT, lambda e: e.activation(out=RD, in_=psum[db][:, :], func=AF.Ln),
                             reads=[("ps", db)], writes=["RD"])
                        s.op(ACT, lambda e: e.activation(out=RD, in_=RD, func=AF.Exp, scale=-1.0),
                             reads=["RD"], writes=["RD"])
                        s.op(DVE, lambda e, u=u, h=h, qs=qs, OSB=OSBs[u]: e.tensor_tensor(
                            out=CAT3[:, 2 + h, qs], in0=OSB, in1=RD, op=ALU.mult),
                            reads=[("OSB", u), "RD"], writes=[("CAT", 2 + h, qc)])

            s.barrier()
            if "C" not in phases:
                return
            HT = sc(lay["HT"], NJ * TC, BF16).rearrange("p (j n) -> p j n", j=NJ)
            NSUB = TC // 512
            for tg in range(Q // TC):
                XNTs = []
                for sub in range(NSUB):
                    srcs = []
                    for i in range(4):
                        ti = sub * 4 + i
                        qt = tg * (TC // 128) + ti
                        xtb = xtc[0] % NXT
                        xtc[0] += 1
                        XT = sc(lay["XT"][xtb], 1024, F32)
                        npt = 128 // QP
                        for a_ in range(npt):
                            t = qt * npt + a_
                            src_ = xd[xbase + own_off * N2 + t: xbase + (own_off + QP - 1) * N2 + t + 1: N2, :]
                            s.op(SP, lambda e, src_=src_, a_=a_, XT=XT: e.dma_start(out=XT[a_ * QP:(a_ + 1) * QP, :], in_=src_),
                                 writes=[("XT", xtb)], dma=True)
                        X1 = sc(lay["X1"][ti], 1024, F32)
                        for hf in range(2):
                            b = 4 + hf + 2 * (i % 2)
                            for c in range(8):
                                s.op(PE, lambda e, b=b, c=c, qt=qt, hf=hf: e.matmul(
                                    out=psum[b][:, :], lhsT=CAT3[:, c, qt * 128:(qt + 1) * 128],
                                    rhs=WIO3[:, c, hf * 512:(hf + 1) * 512], start=(c == 0), stop=(c == 7)),
                                    reads=[("CAT", c, qt // 4), "WIO"], writes=[("ps", b)])
                            s.op(DVE, lambda e, b=b, hf=hf, X1=X1, XT=XT: e.tensor_tensor(
                                out=X1[:, hf * 512:(hf + 1) * 512], in0=psum[b][:, :], in1=XT[:, hf * 512:(hf + 1) * 512],
                                op=ALU.add), reads=[("ps", b), ("XT", xtb)], writes=[("X1", ti)])
                        srcs.append((X1, ("X1", ti)))
                    XNTs.append(norm_front(srcs))
                XNTs = [norm_back(h_, 8) for h_ in XNTs]
                for j in range(NJ):
                    wb = j % 2
                    WG = sc(lay["WG"][wb], 1024, BF16).rearrange("p (c n) -> p c n", c=8)
                    WU = sc(lay["WU"][wb], 1024, BF16).rearrange("p (c n) -> p c n", c=8)
                    s.op(SP, lambda e, j=j, WG=WG: e.dma_start(out=WG, in_=wg16[j]), reads=["W16"], writes=[("WG", wb)], dma=True)
                    s.op(SP, lambda e, j=j, WU=WU: e.dma_start(out=WU, in_=wu16[j]), reads=["W16"], writes=[("WU", wb)], dma=True)
                    for sub in range(NSUB):
                        XNT, xb = XNTs[sub]
                        bg = 4 * (j % 2) + sub
                        bu = 4 * (j % 2) + 2 + sub
                        for c in range(8):
                            s.op(PE, lambda e, c=c, bg=bg, WG=WG, XNT=XNT: e.matmul(
                                out=psum[bg][:, :], lhsT=WG[:, c, :], rhs=XNT[:, c, :], start=(c == 0), stop=(c == 7)),
                                reads=[("WG", wb), ("XNT", xb, c)], writes=[("ps", bg)])
                        for c in range(8):
                            s.op(PE, lambda e, c=c, bu=bu, WU=WU, XNT=XNT: e.matmul(
                                out=psum[bu][:, :], lhsT=WU[:, c, :], rhs=XNT[:, c, :], start=(c == 0), stop=(c == 7)),
                                reads=[("WU", wb), ("XNT", xb, c)], writes=[("ps", bu)])
                        sgb = 0
                        SG = sc(lay["SG"][sgb], 512, F32)
                        s.op(ACT, lambda e, bg=bg, SG=SG: e.activation(out=SG, in_=psum[bg][:, :], func=AF.Silu),
                             reads=[("ps", bg)], writes=[("SG", sgb)])
                        s.op(DVE, lambda e, bu=bu, SG=SG, j=j, sub=sub: e.tensor_tensor(
                            out=HT[:, j, sub * 512:(sub + 1) * 512], in0=psum[bu][:, :], in1=SG, op=ALU.mult),
                            reads=[("ps", bu), ("SG", sgb)], writes=[("HT", j, sub)])
                NT8 = TC // 128
                for hf in range(2):
                    for j in range(NJ):
                        wb = (hf * NJ + j) % 2
                        WD = sc(lay["WD"][wb], 512, BF16)
                        s.op(SP, lambda e, j=j, hf=hf, WD=WD: e.dma_start(out=WD, in_=wd16[hf, j]),
                             reads=["W16"], writes=[("WD", wb)], dma=True)
                        for i in range(NT8):
                            s.op(PE, lambda e, i=i, j=j, WD=WD: e.matmul(
                                out=psum[i][:, :], lhsT=HT[:, j, i * 128:(i + 1) * 128], rhs=WD,
                                start=(j == 0), stop=(j == NJ - 1)),
                                reads=[("HT", j, i // 4), ("WD", wb)], writes=[("ps", i)])
                    for i in range(NT8):
                        X1 = sc(lay["X1"][i], 1024, F32)
                        s.op(DVE, lambda e, i=i, hf=hf, X1=X1: e.tensor_tensor(
                            out=X1[:, hf * 512:(hf + 1) * 512], in0=psum[i][:, :], in1=X1[:, hf * 512:(hf + 1) * 512],
                            op=ALU.add), reads=[("ps", i), ("X1", i)], writes=[("X1", i)])
                for sub in range(NSUB):
                    SSf = STAT[:, 16 + 8 * sub:20 + 8 * sub]
                    RSf = STAT[:, 20 + 8 * sub:24 + 8 * sub]
                    kssf = [("ssf", sub, i) for i in range(4)]
                    krsf = ("rsf", sub)
                    for i in range(4):
                        ti = sub * 4 + i
                        X1 = sc(lay["X1"][ti], 1024, F32)
                        YOj = sc(lay["XS"][i % NXS], 1024, BF16)
                        s.op(ACT, lambda e, X1=X1, SSf=SSf, YOj=YOj, i=i: e.activation(
                            out=YOj, in_=X1, func=AF.Square, accum_out=SSf[:, i:i + 1]),
                            reads=[("X1", ti), kssf[i]], writes=[kssf[i], ("XS", i % NXS)])
                    s.op(ACT, lambda e, SSf=SSf, RSf=RSf: e.activation(out=RSf, in_=SSf, func=AF.Ln, scale=1.0 / D, bias=EPSC),
                         reads=kssf + ["GAINS"], writes=[krsf])
                    s.op(ACT, lambda e, RSf=RSf: e.activation(out=RSf, in_=RSf, func=AF.Exp, scale=-0.5), reads=[krsf], writes=[krsf])
                    for i in range(4):
                        ti = sub * 4 + i
                        qt = tg * (TC // 128) + ti
                        X1 = sc(lay["X1"][ti], 1024, F32)
                        s.op(DVE, lambda e, X1=X1, RSf=RSf, i=i: e.scalar_tensor_tensor(
                            out=X1, in0=X1, scalar=RSf[:, i:i + 1], in1=GFIN[:, :], op0=ALU.mult, op1=ALU.mult),
                            reads=[("X1", ti), krsf, "GFIN"], writes=[("X1", ti)])
                        s.op(SP, lambda e, qt=qt, X1=X1: e.dma_start(out=yd[qt * 128:(qt + 1) * 128, :], in_=X1),
                             reads=[("X1", ti)], dma=True)
        for ji, jb in enumerate(jobs):
            do_job(ji, jb)
        s.barrier()

        s.prepare()

        @block.sync
        def _(e):
            s.emit(SP, e, sems, dsems, bsems)

        @block.gpsimd
        def _(e):
            s.emit(POOL, e, sems, dsems, bsems)

        @block.scalar
        def _(e):
            s.emit(ACT, e, sems, dsems, bsems)

        @block.vector
        def _(e):
            s.emit(DVE, e, sems, dsems, bsems)

        @block.tensor
        def _(e):
            s.emit(PE, e, sems, dsems, bsems)

    return nc


def KH_ok(lay, smax):
    return True


def _tables(S, Q, s1_of_p, own_k2_0):
    N2 = S // 128
    K2O = Q // 128
    t = np.arange(N2)[:, None]
    tok = (N2 * np.asarray(s1_of_p)[None, :] + t).astype(np.int64)
    k1 = np.arange(128, dtype=np.int64)
    ph = (tok[:, :, None] * k1[None, None, :]) % S
    ang = 2.0 * np.pi * ph.astype(np.float64) / S
    etab = np.concatenate([np.cos(ang), -np.sin(ang)], axis=2).reshape(S, 256).astype(np.float32)
    k2 = own_k2_0 + np.arange(K2O, dtype=np.int64)
    phi = 2.0 * np.pi * ((np.arange(N2, dtype=np.int64)[:, None] * k2[None, :]) % N2) / N2
    f2c = np.zeros((N2, 2, 2, K2O), np.float64)
    f2c[:, 0, 0, :] = np.cos(phi)
    f2c[:, 1, 0, :] = np.sin(phi)
    f2c[:, 0, 1, :] = -np.sin(phi)
    f2c[:, 1, 1, :] = np.cos(phi)
    f2c = f2c.reshape(2 * N2, 2 * K2O).astype(np.float32)
    inv_freq = (1.0 / (10000.0 ** (np.arange(0, 64, 2, dtype=np.float32) / 64.0))).astype(np.float32)
    pos = tok.reshape(-1).astype(np.float32)
    angr = pos[None, :] * inv_freq[:, None]
    cosr = np.cos(angr).astype(np.float32)
    sinr = np.sin(angr).astype(np.float32)
    ropec = np.stack([np.concatenate([cosr, cosr, cosr, cosr], 0),
                      np.concatenate([sinr, sinr, sinr, sinr], 0)], 0)
    return etab, f2c, np.ascontiguousarray(ropec)


def _rope_q(ropec, S, Q):
    N2 = S // 128
    QP = Q // N2
    r = ropec.reshape(2, 128, N2, 128)[:, 0:64, :, :QP].reshape(2, 64, Q)
    return np.ascontiguousarray(np.concatenate([r[0], r[1]], 0))


def _consts():
    ident = np.eye(128, dtype=np.float32)
    ones = np.ones((128, 128), np.float32)
    c = np.arange(64)
    psi = 2.0 * np.pi * ((c[:, None] * c[None, :]) % 64) / 64.0
    Cc = np.zeros((128, 128), np.float64)
    Sc = np.zeros((128, 128), np.float64)
    for g in range(2):
        Cc[g * 64:(g + 1) * 64, g * 64:(g + 1) * 64] = np.cos(psi)
        Sc[g * 64:(g + 1) * 64, g * 64:(g + 1) * 64] = np.sin(psi)
    return np.ascontiguousarray(np.stack([ident, ones, Cc.astype(np.float32), Sc.astype(np.float32)], 1))


def _weight_maps(inp):
    f = lambda a: np.ascontiguousarray(np.asarray(a, dtype=np.float32))
    w_in = f(inp["w_in"])[0]
    m = {}
    m["w_in_r"] = f(w_in.reshape(8, 128, 960).transpose(1, 0, 2))
    m["w_q_r"] = f(f(inp["w_q_up"])[0].reshape(3, 128, 1152).transpose(1, 0, 2))
    m["w_kv_r"] = f(f(inp["w_kv_up"])[0].reshape(2, 128, 1536).transpose(1, 0, 2))
    m["w_out_r"] = f(f(inp["w_out"])[0].reshape(8, 128, 1024).transpose(1, 0, 2))
    m["w_gate_r"] = f(f(inp["w_gate"])[0].reshape(8, 128, NJ, 128).transpose(2, 1, 0, 3))
    m["w_up_r"] = f(f(inp["w_up"])[0].reshape(8, 128, NJ, 128).transpose(2, 1, 0, 3))
    m["w_down_r"] = f(f(inp["w_down"])[0].reshape(NJ, 128, 2, 512).transpose(2, 0, 1, 3))
    g = np.zeros((128, 24), np.float32)
    g[:, 0:8] = f(inp["norm_mix_g"])[0].reshape(8, 128).T
    g[:, 8:16] = f(inp["norm_ffn_g"])[0].reshape(8, 128).T
    g[:, 16:19] = f(inp["q_norm_g"])[0].reshape(3, 128).T
    g[:, 19:21] = f(inp["kv_norm_g"])[0].reshape(2, 128).T
    g[:, 21] = EPS
    m["gains"] = g
    m["gfin_bc"] = f(np.broadcast_to(f(inp["final_norm_g"])[None, :], (128, 1024)))
    m["consts"] = _consts()
    return m


SP_, QQ = 8192, 2048
SS_ = 2048


def _job_list():
    jobs = []
    QPp = QQ // (SP_ // 128)
    jobs.append(dict(S=SP_, Q=QQ, x="xp", xrows=SP_, xbase=0, perm_segs=[(0, 0, 128)], own_off=0, tab="p0", y="y0"))
    jobs.append(dict(S=SP_, Q=QQ, x="xp", xrows=SP_, xbase=0,
                     perm_segs=[(0, QPp, QPp), (QPp, 0, QPp), (2 * QPp, 2 * QPp, 128 - 2 * QPp)],
                     own_off=QPp, tab="p1", y="y1"))
    for k in range(4):
        jobs.append(dict(S=SS_, Q=QQ, x="xs", xrows=4 * SS_, xbase=k * SS_, perm_segs=[(0, 0, 128)], own_off=0,
                         tab="s", y="y%d" % (2 + k)))
    return jobs


def kernel(**inputs):
    xp = np.asarray(inputs["x_prompt"], dtype=np.float32)
    xs = np.asarray(inputs["x_sample"], dtype=np.float32)
    wm = _weight_maps(inputs)
    jobs = _job_list()
    nc = build_program(jobs)
    N2p = SP_ // 128
    QPp = QQ // N2p
    in_maps = []
    sigmas = []
    tabs_s = _tables(SS_, QQ, np.arange(128), 0)
    for core in range(8):
        b, half = core // 2, core % 2
        q0, q1 = 2 * half, 2 * half + 1
        own0 = np.arange(q0 * QPp, (q0 + 1) * QPp)
        own1 = np.arange(q1 * QPp, (q1 + 1) * QPp)
        rest = np.array([v for v in range(128) if v // QPp not in (q0, q1)])
        sigma = np.concatenate([own0, own1, rest])
        sigmas.append(sigma)
        m = dict(wm)
        xb = xp[b].reshape(128, N2p, D)[sigma].reshape(SP_, D)
        m["xp"] = np.ascontiguousarray(xb)
        m["xs"] = np.ascontiguousarray(xs[4 * core:4 * core + 4].reshape(4 * SS_, D))
        s1p0 = sigma
        s1p1 = np.concatenate([sigma[QPp:2 * QPp], sigma[0:QPp], sigma[2 * QPp:]])
        for tbn, s1p, qr in (("p0", s1p0, q0), ("p1", s1p1, q1)):
            et, f2, rc = _tables(SP_, QQ, s1p, qr * (QQ // 128))
            m["etab_" + tbn] = et
            m["f2c_" + tbn] = f2
            m["ropec_" + tbn] = rc
            m["ropeq_" + tbn] = _rope_q(rc, SP_, QQ)
        m["etab_s"], m["f2c_s"], m["ropec_s"] = tabs_s
        m["ropeq_s"] = _rope_q(tabs_s[2], SS_, QQ)
        in_maps.append(m)
    res = run_bass_kernel_spmd(nc, in_maps, core_ids=list(range(8)))
    y_prompt = np.zeros((4, SP_, D), np.float32)
    y_sample = np.zeros((32, SS_, D), np.float32)
    for core in range(8):
        r = res.results[core]
        b, half = core // 2, core % 2
        for jj in range(2):
            qr = 2 * half + jj
            y = np.asarray(r["y%d" % jj]).reshape(N2p, QPp, D)
            y_prompt[b].reshape(128, N2p, D)[qr * QPp:(qr + 1) * QPp] = y.transpose(1, 0, 2)
        for k in range(4):
            y = np.asarray(r["y%d" % (2 + k)]).reshape(SS_ // 128, 128, D)
            y_sample[4 * core + k] = y.transpose(1, 0, 2).reshape(SS_, D)
    return (y_prompt, y_sample)
```

```python
import math
import numpy as np
import concourse.bass as bass
import concourse.mybir as mybir
from concourse.bass_utils import run_bass_kernel_spmd

F32 = mybir.dt.float32
BF16 = mybir.dt.bfloat16
AF = mybir.ActivationFunctionType
ALU = mybir.AluOpType

D = 1024
NH = 6
DFF = 2816
NJ = DFF // 128
EPS = 1e-6
SM_SCALE = 1.0 / math.sqrt(192.0)
T = 512
TC = 1024

PE, ACT, DVE, POOL, SP = "tensor", "scalar", "vector", "gpsimd", "sync"
ENGS = [PE, ACT, DVE, POOL, SP]
NDMASEM = 24
NXT = 4
NXS = 8


class Op:
    __slots__ = ("eng", "fn", "deps", "signal", "seq", "dma", "dsem", "dval", "didx", "barrier", "raw")

    def __init__(self, eng, fn, dma):
        self.eng = eng
        self.fn = fn
        self.deps = []
        self.signal = False
        self.seq = 0
        self.dma = dma
        self.dsem = None
        self.dval = None
        self.didx = None
        self.barrier = 0
        self.raw = ()


class Sched:
    def __init__(self, same_engine_sync=True):
        self.ops = {e: [] for e in ENGS}
        self.last_writer = {}
        self.readers = {}
        self.same_engine_sync = same_engine_sync
        self.nbar = 0

    def op(self, eng, fn, reads=(), writes=(), dma=False):
        o = Op(eng, fn, dma)
        deps = {}
        raw = set()
        for k in reads:
            w = self.last_writer.get(k)
            if w is not None:
                deps[id(w)] = w
                raw.add(id(w))
        for k in writes:
            w = self.last_writer.get(k)
            if w is not None:
                deps[id(w)] = w
            for r in self.readers.get(k, ()):
                deps[id(r)] = r
        o.raw = raw
        for k in reads:
            lst = self.readers.setdefault(k, [])
            if not dma:
                lst[:] = [r for r in lst if r.dma or r.eng != eng]
            lst.append(o)
        for k in writes:
            self.last_writer[k] = o
            self.readers[k] = []
        o.deps = [d for d in deps.values() if d is not o]
        self.ops[eng].append(o)
        return o

    def barrier(self):
        self.nbar += 1
        for e in ENGS:
            o = Op(e, None, False)
            o.barrier = self.nbar
            self.ops[e].append(o)
        self.last_writer = {}
        self.readers = {}

    def plan(self):
        for e in ENGS:
            for o in self.ops[e]:
                for d in o.deps:
                    if d.dma:
                        continue
                    if d.eng == o.eng and not o.dma:
                        if d.eng == PE or not self.same_engine_sync or id(d) not in o.raw:
                            continue
                    d.signal = True
        for e in ENGS:
            c = 0
            i = 0
            for o in self.ops[e]:
                if o.barrier:
                    continue
                if o.dma:
                    o.didx = i
                    o.dsem = i % NDMASEM
                    o.dval = 16 * (i // NDMASEM + 1)
                    i += 1
                elif o.signal:
                    c += 1
                    o.seq = c

    def emit(self, eng, e, sems, dsems, bsems):
        waited = {f: 0 for f in ENGS}
        waited_dma = set()
        my_dmas = []
        for o in self.ops[eng]:
            if o.barrier:
                for d in my_dmas[-NDMASEM:]:
                    if id(d) not in waited_dma:
                        e.wait_ge(dsems[eng][d.dsem], d.dval)
                        waited_dma.add(id(d))
                e.drain().then_inc(bsems[eng], 1)
                for f in ENGS:
                    if f != eng:
                        e.wait_ge(bsems[f], o.barrier)
                for f in ENGS:
                    waited[f] = max(waited[f], self.bar_seq[f][o.barrier])
                continue
            for d in o.deps:
                if d.dma:
                    if id(d) not in waited_dma:
                        e.wait_ge(dsems[d.eng][d.dsem], d.dval)
                        waited_dma.add(id(d))
                else:
                    if d.eng == eng and not o.dma and (eng == PE or not self.same_engine_sync
                                                       or id(d) not in o.raw):
                        continue
                    if d.seq > waited[d.eng]:
                        e.wait_ge(sems[d.eng], d.seq)
                        waited[d.eng] = d.seq
            if o.dma:
                if o.didx >= NDMASEM:
                    p = my_dmas[o.didx - NDMASEM]
                    if id(p) not in waited_dma:
                        e.wait_ge(dsems[eng][p.dsem], p.dval)
                        waited_dma.add(id(p))
                my_dmas.append(o)
            inst = o.fn(e)
            if o.dma:
                inst.then_inc(dsems[eng][o.dsem], 16)
            elif o.signal:
                inst.then_inc(sems[eng], 1)

    def prepare(self):
        self.plan()
        self.bar_seq = {e: {} for e in ENGS}
        for e in ENGS:
            c = 0
            for o in self.ops[e]:
                if o.barrier:
                    self.bar_seq[e][o.barrier] = c
                elif (not o.dma) and o.signal:
                    c = o.seq


def build_program(jobs, phases="FABC", debug_out=False, smax_min=0):
    nc = bass.Bass("TRN2", target_bir_lowering=False)
    s = Sched()
    dram = {}

    def din(name, shape):
        if name not in dram:
            dram[name] = nc.dram_tensor(name, list(shape), F32, kind="ExternalInput").ap()
        return dram[name]

    def dout(name, shape):
        dram[name] = nc.dram_tensor(name, list(shape), F32, kind="ExternalOutput").ap()
        return dram[name]

    w_in_d = din("w_in_r", [128, 8, 960])
    w_q_d = din("w_q_r", [128, 3, 1152])
    w_kv_d = din("w_kv_r", [128, 2, 1536])
    w_out_d = din("w_out_r", [128, 8, 1024])
    w_g_d = din("w_gate_r", [NJ, 128, 8, 128])
    w_u_d = din("w_up_r", [NJ, 128, 8, 128])
    w_d_d = din("w_down_r", [2, NJ, 128, 512])
    gains_d = din("gains", [128, 24])
    gfin_d = din("gfin_bc", [128, 1024])
    cst_d = din("consts", [128, 4, 128])

    for jb in jobs:
        S, Q = jb["S"], jb["Q"]
        N2 = S // 128
        tb = jb["tab"]
        din(jb["x"], [jb["xrows"], D])
        din("etab_" + tb, [S, 256])
        din("f2c_" + tb, [2 * N2, 2 * (Q // 128)])
        din("ropec_" + tb, [2, 128, S])
        din("ropeq_" + tb, [128, Q])
        dout(jb["y"], [Q, D])

    wg16 = nc.dram_tensor("wg16", [NJ, 128, 8, 128], BF16, kind="Internal").ap()
    wu16 = nc.dram_tensor("wu16", [NJ, 128, 8, 128], BF16, kind="Internal").ap()
    wd16 = nc.dram_tensor("wd16", [2, NJ, 128, 512], BF16, kind="Internal").ap()
    SMAX = max(max(jb["S"] for jb in jobs), smax_min)
    QMAX = max(jb["Q"] for jb in jobs)
    N2MAX = SMAX // 128

    class Carver:
        def __init__(self):
            self.off = 0
            self.maxoff = 0

        def take(self, nbytes):
            o = self.off
            self.off += (nbytes + 31) // 32 * 32
            self.maxoff = max(self.maxoff, self.off)
            return o

    lay = {}
    cv = Carver()
    XPIPE0 = cv.take(0)
    lay["XT"] = [cv.take(4096) for _ in range(NXT)]
    lay["XS"] = [cv.take(2048) for _ in range(NXS)]
    lay["XNT"] = [cv.take(8192) for _ in range(2)]
    XPIPE1 = cv.off
    lay["BT"] = XPIPE0
    if XPIPE1 - XPIPE0 < 32768:
        cv.off = XPIPE0 + 32768
        cv.maxoff = max(cv.maxoff, cv.off)
    COMMON_END = cv.off
    lay["U"] = [cv.take(512) for _ in range(2)]
    lay["ET"] = [cv.take(512) for _ in range(3)]
    lay["B"] = cv.take(N2MAX * 1024)
    lay["YC"] = cv.take(2 * QMAX * 2)
    F0_END = cv.off
    cv.off = COMMON_END
    lay["CKV"] = cv.take(2 * SMAX * 2)
    lay["KR"] = cv.take(SMAX * 2)
    lay["CQ"] = cv.take(3 * QMAX * 2)
    AB0 = cv.off
    lay["ROPE"] = [cv.take(4096) for _ in range(1)]
    lay["SQ"] = cv.take(5 * 512 * 2)
    lay["RSTD"] = [cv.take(2048) for _ in range(2)]
    lay["CST"] = cv.take(5 * 512 * 4)
    lay["T12"] = [cv.take(2048) for _ in range(2)]
    A_END = cv.off
    cv.off = XPIPE0
    lay["KH"] = cv.take(SMAX * 2)
    lay["VH"] = cv.take(SMAX * 2)
    assert cv.off <= COMMON_END + 0 or True
    B_X_END = cv.off
    lay["ACC"] = [cv.take(2048) for _ in range(5)]
    assert cv.off <= COMMON_END, (cv.off, COMMON_END)
    cv.off = AB0
    lay["ROPEB"] = [cv.take(4096) for _ in range(2)]
    lay["T12B"] = [cv.take(2048) for _ in range(2)]
    lay["QN"] = cv.take(QMAX * 2)
    lay["QR"] = cv.take(QMAX * 2)
    lay["PT"] = [cv.take(1024) for _ in range(4)]
    lay["RD"] = cv.take(2048)
    lay["OSB"] = [cv.take(2048) for _ in range(2)]
    B_END = cv.off
    cv.off = COMMON_END
    lay["X1"] = [cv.take(4096) for _ in range(TC // 128)]
    lay["HT"] = cv.take(NJ * TC * 2)
    lay["SG"] = [cv.take(2048) for _ in range(1)]
    lay["WG"] = [cv.take(2048) for _ in range(2)]
    lay["WU"] = [cv.take(2048) for _ in range(2)]
    lay["WD"] = [cv.take(1024) for _ in range(2)]
    C_END = cv.off
    SCR_BYTES = cv.maxoff
    print('SBUF layout: F0_END', F0_END, 'A_END', A_END, 'B_END', B_END, 'C_END', C_END, 'SCR', SCR_BYTES)
    assert KH_ok(lay, SMAX) if False else True
    if B_X_END > COMMON_END:
        raise AssertionError("KH/VH overflow common region: %d > %d" % (B_X_END, COMMON_END))

    from contextlib import ExitStack
    es = ExitStack()
    with es:
        scr = es.enter_context(nc.sbuf_tensor("scr", [128, SCR_BYTES // 2], BF16))
        WQN = es.enter_context(nc.sbuf_tensor("wqn", [128, 3 * 768], BF16))
        WQRR = es.enter_context(nc.sbuf_tensor("wqrr", [128, 3 * 768], BF16))
        WKV = es.enter_context(nc.sbuf_tensor("wkv", [128, 2 * 1536], BF16))
        WIO = es.enter_context(nc.sbuf_tensor("wio", [128, 8 * 1024], BF16))
        CAT = es.enter_context(nc.sbuf_tensor("cat", [128, 8 * QMAX], BF16))
        GAINS = es.enter_context(nc.sbuf_tensor("gains_sb", [128, 24], F32))
        GFIN = es.enter_context(nc.sbuf_tensor("gfin_sb", [128, 1024], F32))
        CONST = es.enter_context(nc.sbuf_tensor("const_sb", [128, 4 * 128], BF16))
        STAT = es.enter_context(nc.sbuf_tensor("stat_sb", [128, 48], F32))
        F2C = es.enter_context(nc.sbuf_tensor("f2c_sb", [128, 64], BF16))
        ONESF = es.enter_context(nc.sbuf_tensor("onesf_sb", [128, 128], F32))
        psum = [es.enter_context(nc.psum_tensor("ps%d" % i, [128, 512], F32)) for i in range(8)]
        NS = 5
        sems = {e: es.enter_context(nc.semaphore("sem_" + e)) for e in ENGS}
        bsems = {e: es.enter_context(nc.semaphore("bsem_" + e)) for e in ENGS}
        dsems = {e: [es.enter_context(nc.semaphore("dsem_%s_%d" % (e, i))) for i in range(NDMASEM)]
                 for e in (SP, POOL)}
        block = es.enter_context(nc.Block())

        def sc(off, n, dt):
            assert off % 4 == 0
            if dt == BF16:
                return scr[:, off // 2: off // 2 + n]
            return scr[:, off // 2: off // 2 + 2 * n].bitcast(F32)

        def ps_bf(b):
            return psum[b][:, :].bitcast(BF16)

        EPSC = GAINS[:, 21:22]
        ident = CONST[:, 0:128]
        ones = CONST[:, 128:256]
        CcB = CONST[:, 256:384]
        ScB = CONST[:, 384:512]
        WQN4 = WQN[:, :].rearrange("p (c h n) -> p c h n", c=3, h=NH)
        WQRR4 = WQRR[:, :].rearrange("p (c h n) -> p c h n", c=3, h=NH)
        w_q_d4 = w_q_d.rearrange("p c (h n) -> p c h n", h=NH)
        WKV3 = WKV[:, :].rearrange("p (c n) -> p c n", c=2)
        WIO3 = WIO[:, :].rearrange("p (c n) -> p c n", c=8)
        CAT3 = CAT[:, :].rearrange("p (c n) -> p c n", c=8)

        s.op(SP, lambda e: e.dma_start(out=GAINS[:, :], in_=gains_d), writes=["GAINS"], dma=True)
        s.op(SP, lambda e: e.dma_start(out=GFIN[:, :], in_=gfin_d), writes=["GFIN"], dma=True)
        s.op(POOL, lambda e: e.dma_start(out=CONST[:, :].rearrange("p (c n) -> p c n", c=4), in_=cst_d),
             writes=["CONST"], dma=True)
        s.op(POOL, lambda e: e.dma_start(out=WQN4, in_=w_q_d4[:, :, :, 0:128]), writes=["WQ"], dma=True)
        s.op(POOL, lambda e: e.dma_start(out=WQRR4[:, :, :, 0:64], in_=w_q_d4[:, :, :, 128:192]), writes=["WQRR"], dma=True)
        s.op(POOL, lambda e: e.dma_start(out=WKV3, in_=w_kv_d), writes=["WKV"], dma=True)
        for c in range(3):
            s.op(DVE, lambda e, c=c: e.tensor_scalar(
                out=WQRR4[:, c, :, 64:96], in0=WQRR4[:, c, :, 32:64], scalar1=-1.0, scalar2=None, op0=ALU.mult),
                reads=["WQRR"], writes=["WQRR2"])
            s.op(DVE, lambda e, c=c: e.tensor_copy(out=WQRR4[:, c, :, 96:128], in_=WQRR4[:, c, :, 0:32]),
                 reads=["WQRR"], writes=["WQRR2"])

        s.op(DVE, lambda e: e.memset(ONESF[:, :], 1.0), writes=["ONESF"])
        psrr = [0]
        xtc = [0]

        def load_w_in():
            s.op(POOL, lambda e: e.dma_start(out=WIO3[:, :, 0:256], in_=w_in_d[:, :, 0:256]), writes=["WIOF"], dma=True)
            s.op(POOL, lambda e: e.dma_start(out=WIO3[:, :, 256:960], in_=w_in_d[:, :, 256:960]), writes=["WIO"], dma=True)
            s.op(DVE, lambda e: e.tensor_scalar(out=WIO3[:, :, 960:992], in0=WIO3[:, :, 928:960],
                                                scalar1=-1.0, scalar2=None, op0=ALU.mult),
                 reads=["WIO"], writes=["WIO"])
            s.op(DVE, lambda e: e.tensor_copy(out=WIO3[:, :, 992:1024], in_=WIO3[:, :, 896:928]),
                 reads=["WIO"], writes=["WIO"])

        def load_w_out():
            s.op(POOL, lambda e: e.dma_start(out=WIO3, in_=w_out_d), writes=["WIO", "WIOF"], dma=True)

        gctr = [0]

        def norm_front(srcs):
            gi = gctr[0]
            gctr[0] += 1
            blk = gi % 2
            SSb = STAT[:, 8 * blk: 8 * blk + 4]
            RSb = STAT[:, 8 * blk + 4: 8 * blk + 8]
            kss = [("ss", blk, i) for i in range(4)]
            krs = ("rs", blk)
            XSs = []
            for i, (ap, key) in enumerate(srcs):
                xsb = (gi * 4 + i) % NXS
                XS = sc(lay["XS"][xsb], 1024, BF16)
                s.op(ACT, lambda e, XS=XS, ap=ap, i=i: e.activation(out=XS, in_=ap, func=AF.Square,
                                                                  accum_out=SSb[:, i:i + 1]),
                     reads=[key, kss[i]], writes=[kss[i], ("XS", xsb)])
                XSs.append((XS, xsb))
            s.op(ACT, lambda e: e.activation(out=RSb, in_=SSb, func=AF.Ln, scale=1.0 / D, bias=EPSC),
                 reads=kss + ["GAINS"], writes=[krs])
            s.op(ACT, lambda e: e.activation(out=RSb, in_=RSb, func=AF.Exp, scale=-0.5),
                 reads=[krs], writes=[krs])
            for i, (ap, key) in enumerate(srcs):
                XS, xsb = XSs[i]
                s.op(ACT, lambda e, XS=XS, ap=ap, i=i: e.activation(
                    out=XS, in_=ap, func=AF.Copy, scale=RSb[:, i:i + 1]),
                    reads=[key, krs], writes=[("XS", xsb)])
            return (gi, XSs)

        def norm_back(h, g_off):
            gi, XSs = h
            for i in range(4):
                XS, xsb = XSs[i]
                for c in range(8):
                    b = c // 2
                    o0 = (c % 2) * 512 + i * 128
                    s.op(PE, lambda e, c=c, b=b, o0=o0, XS=XS: e.transpose(
                        out=ps_bf(b)[:, o0:o0 + 128], in_=XS[:, c * 128:(c + 1) * 128], identity=ident),
                        reads=[("XS", xsb), "CONST"], writes=[("ps", b)])
            xnt_buf = gi % 2
            XNT = sc(lay["XNT"][xnt_buf], 4096, BF16).rearrange("p (c n) -> p c n", c=8)
            for c in range(8):
                b = c // 2
                o0 = (c % 2) * 512
                s.op(DVE, lambda e, c=c, b=b, o0=o0: e.tensor_scalar(
                    out=XNT[:, c, :], in0=ps_bf(b)[:, o0:o0 + 512],
                    scalar1=GAINS[:, g_off + c: g_off + c + 1], scalar2=None, op0=ALU.mult),
                    reads=[("ps", b), "GAINS"], writes=[("XNT", xnt_buf, c)])
            return XNT, xnt_buf

        def norm_group(srcs, g_off):
            return norm_back(norm_front(srcs), g_off)

        def ps_next(lo, hi):
            psrr[0] += 1
            return lo + psrr[0] % (hi - lo)

        def do_job(ji, jb):
            S, Q = jb["S"], jb["Q"]
            N2 = S // 128
            QP = Q // N2
            K2O = Q // 128
            TR = 2 * N2
            R = 128 // N2
            NG = N2 // 4
            NQC = Q // 512
            xd = dram[jb["x"]]
            tb = jb["tab"]
            etab = dram["etab_" + tb]
            f2c_d = dram["f2c_" + tb]
            ropec = dram["ropec_" + tb]
            ropeq = dram["ropeq_" + tb]
            yd = dram[jb["y"]]
            xbase = jb["xbase"]
            segs = jb["perm_segs"]
            own_off = jb["own_off"]

            def load_ctx_tile(t, buf):
                XT = sc(lay["XT"][buf], 1024, F32)
                for (dp, sp_, n) in segs:
                    src = xd[xbase + sp_ * N2 + t: xbase + (sp_ + n - 1) * N2 + t + 1: N2, :]
                    s.op(SP, lambda e, src=src, dp=dp, n=n: e.dma_start(out=XT[dp:dp + n, :], in_=src),
                         writes=[("XT", buf)], dma=True)
                return XT

            s.barrier()
            load_w_in()
            if "F" not in phases:
                return
            s.op(POOL, lambda e: e.dma_start(out=F2C[0:TR, 0:2 * K2O], in_=f2c_d), writes=["F2C"], dma=True)
            Bv = sc(lay["B"], N2 * 512, BF16)
            def ctx_front(g):
                srcs = []
                for i in range(4):
                    t = g * 4 + i
                    xtb = xtc[0] % NXT
                    xtc[0] += 1
                    srcs.append((load_ctx_tile(t, xtb), ("XT", xtb)))
                return norm_front(srcs)

            hcur = ctx_front(0)
            for g in range(NG):
                XNT, xb = norm_back(hcur, 0)
                hcur = ctx_front(g + 1) if g + 1 < NG else None

                def u_part(i, g=g, XNT=XNT, xb=xb):
                    t = g * 4 + i
                    ub = t % 2
                    eb = t % 3
                    U = sc(lay["U"][ub], 256, BF16)
                    ET = sc(lay["ET"][eb], 256, BF16)
                    s.op(POOL, lambda e, t=t, ET=ET: e.dma_start(out=ET, in_=etab[t * 128:(t + 1) * 128, :]),
                         writes=[("ET", eb)], dma=True)
                    pb = 4 if t % 2 == 0 else 7
                    for c in range(8):
                        s.op(PE, lambda e, c=c, i=i, pb=pb, XNT=XNT: e.matmul(
                            out=psum[pb][:, 0:256], lhsT=XNT[:, c, i * 128:(i + 1) * 128],
                            rhs=WIO3[:, c, 0:256], start=(c == 0), stop=(c == 7)),
                            reads=[("XNT", xb, c), "WIOF"], writes=[("ps", pb)])
                    s.op(DVE, lambda e, pb=pb, U=U: e.tensor_copy(out=U, in_=psum[pb][:, 0:256]),
                         reads=[("ps", pb)], writes=[("U", ub)])

                def s1_part(i, g=g):
                    t = g * 4 + i
                    ub = t % 2
                    eb = t % 3
                    U = sc(lay["U"][ub], 256, BF16)
                    ET = sc(lay["ET"][eb], 256, BF16)
                    sb = 5 + t % 2
                    for ri in range(2):
                        s.op(PE, lambda e, ri=ri, sb=sb, U=U, ET=ET: e.matmul(
                            out=psum[sb][:, ri * 256:(ri + 1) * 256], lhsT=ET[:, ri * 128:(ri + 1) * 128],
                            rhs=U, start=True, stop=True),
                            reads=[("U", ub), ("ET", eb)], writes=[("ps", sb)])
                    s.op(DVE, lambda e, t=t, sb=sb: e.tensor_scalar(
                        out=Bv[:, t * 512:(t + 1) * 512], in0=psum[sb][:, :], scalar1=1.0, scalar2=None, op0=ALU.mult),
                        reads=[("ps", sb)], writes=[("B", t)])

                u_part(0)
                for i in range(4):
                    if i + 1 < 4:
                        u_part(i + 1)
                    s1_part(i)
            s.barrier()
            if "f" in phases:
                return
            BT = sc(lay["BT"], 128 * 128, BF16)
            BT3 = BT.rearrange("p (f k) -> p f k", f=128)
            YC = sc(lay["YC"], 2 * Q, BF16)
            for fc in range(2):
                for fb in range(16):
                    b = fb % 2
                    for i in range(8):
                        f = fc * 128 + fb * 8 + i
                        s.op(PE, lambda e, f=f, b=b, i=i: e.transpose(
                            out=ps_bf(b)[0:TR, i * 128:(i + 1) * 128], in_=Bv[:, f:f + 256 * (TR - 1) + 1:256],
                            identity=ident), reads=["CONST"], writes=[("ps", b)])
                    if fb % 2:
                        s.op(ACT, lambda e, fb=fb, b=b: e.activation(
                            out=BT[0:TR, fb * 1024:(fb + 1) * 1024], in_=ps_bf(b)[0:TR, :], func=AF.Copy),
                            reads=[("ps", b)], writes=[("BT", fb)])
                    else:
                        s.op(DVE, lambda e, fb=fb, b=b: e.tensor_copy(
                            out=BT[0:TR, fb * 1024:(fb + 1) * 1024], in_=ps_bf(b)[0:TR, :]),
                            reads=[("ps", b)], writes=[("BT", fb)])
                for kb in range(8):
                    b = 2 + kb % 2
                    for i in range(16):
                        k1 = kb * 16 + i
                        s.op(PE, lambda e, k1=k1, b=b, i=i: e.matmul(
                            out=psum[b][:, i * 2 * K2O:(i + 1) * 2 * K2O], lhsT=BT3[0:TR, :, k1],
                            rhs=F2C[0:TR, 0:2 * K2O], start=True, stop=True),
                            reads=[("BT", x) for x in range(16)] + ["F2C"], writes=[("ps", b)])
                    k10 = kb * 16
                    t0 = k10 % N2
                    rr = k10 // N2
                    YCv = YC.rearrange("p (r t k q) -> p r t k q", r=2, t=N2, k=K2O, q=R)
                    for ri in range(2):
                        src_ap = psum[b][:, 0:16 * 2 * K2O].rearrange("p (i r k) -> p i r k", i=16, r=2)[:, :, ri, :]
                        dst_ap = YCv[:, ri, t0:t0 + 16, :, rr]
                        s.op(DVE, lambda e, src_ap=src_ap, dst_ap=dst_ap: e.tensor_copy(out=dst_ap, in_=src_ap),
                             reads=[("ps", b)], writes=[("YC", kb)])
                for qc in range(NQC):
                    b = 4 + qc % 2
                    s.op(PE, lambda e, b=b, qc=qc: e.matmul(out=psum[b][:, :], lhsT=CcB,
                                                          rhs=YC[:, qc * 512:(qc + 1) * 512], start=True, stop=False),
                         reads=[("YC", x) for x in range(8)] + ["CONST"], writes=[("ps", b)])
                    s.op(PE, lambda e, b=b, qc=qc: e.matmul(out=psum[b][:, :], lhsT=ScB,
                                                          rhs=YC[:, Q + qc * 512: Q + (qc + 1) * 512],
                                                          start=False, stop=True),
                         reads=[("YC", x) for x in range(8)] + ["CONST"], writes=[("ps", b)])
                    s.op(ACT, lambda e, b=b, qc=qc, fc=fc: e.activation(
                        out=CAT3[:, fc, qc * 512:(qc + 1) * 512], in_=psum[b][:, :], func=AF.Copy,
                        scale=1.0 / math.sqrt(64.0 * S)), reads=[("ps", b)], writes=[("CAT", fc, qc)])

            s.barrier()
            if "A" not in phases:
                return
            CKV = sc(lay["CKV"], 2 * S, BF16).rearrange("p (c n) -> p c n", c=2)
            KR = sc(lay["KR"], S, BF16)
            CQ = sc(lay["CQ"], 3 * Q, BF16).rearrange("p (c n) -> p c n", c=3)
            SQ = sc(lay["SQ"], 5 * 512, BF16).rearrange("p (c n) -> p c n", c=5)
            CST = sc(lay["CST"], 5 * 512, F32).rearrange("p (c n) -> p c n", c=5)
            nq = 4 * QP
            for k_, (dst0, src0) in enumerate(((0, 896), (64, 896), (128, 960), (192, 960))):
                s.op(DVE, lambda e, dst0=dst0, src0=src0: e.tensor_copy(
                    out=WIO3[:, :, dst0:dst0 + 64], in_=WIO3[:, :, src0:src0 + 64]),
                    reads=["WIO"], writes=["WIOK"])
            hcur = ctx_front(0)
            for g in range(NG):
                XNT, xb = norm_back(hcur, 0)
                hcur = ctx_front(g + 1) if g + 1 < NG else None
                rb = 0
                ROPE = sc(lay["ROPE"][rb], 1024, F32).rearrange("p (c n) -> p c n", c=2)
                s.op(SP, lambda e, g=g, ROPE=ROPE: e.dma_start(
                    out=ROPE[:, :, :], in_=ropec[:, :, g * 512:(g + 1) * 512].rearrange("c p n -> p c n")),
                    writes=[("ROPE", rb)], dma=True)
                def rhs_q(c, XNT=XNT):
                    if QP == 128:
                        return XNT[:, c, :]
                    return XNT[:, c, :].rearrange("p (i q) -> p i q", i=4)[:, :, 0:QP]
                outs = []
                for m in range(5):
                    b = 4 + m % 4 if m < 4 else 4
                    b = [4, 5, 6, 7, 4][m]
                    n = nq if m < 3 else 512
                    col0 = 256 + m * 128
                    for c in range(8):
                        if m < 3 and QP != 128:
                            o_ap = psum[b][:, 0:n].rearrange("p (i q) -> p i q", i=4)
                        else:
                            o_ap = psum[b][:, 0:n]
                        r_ap = rhs_q(c) if m < 3 else XNT[:, c, :]
                        s.op(PE, lambda e, c=c, m=m, o_ap=o_ap, col0=col0, r_ap=r_ap: e.matmul(
                            out=o_ap, lhsT=WIO3[:, c, col0:col0 + 128],
                            rhs=r_ap, start=(c == 0), stop=(c == 7)),
                            reads=[("XNT", xb, c), "WIO"], writes=[("ps", b)])
                    s.op(DVE, lambda e, b=b, m=m, n=n: e.tensor_copy(out=CST[:, m, 0:n], in_=psum[b][:, 0:n]),
                         reads=[("ps", b)], writes=[("CST", m)])
                    s.op(ACT, lambda e, b=b, m=m, n=n: e.activation(out=SQ[:, m, 0:n], in_=CST[:, m, 0:n], func=AF.Square),
                         reads=[("CST", m)], writes=[("SQ", m)])
                for (ms, n, dim, goff, which) in (((0, 1, 2), nq, 384.0, 16, 0), ((3, 4), 512, 256.0, 19, 1)):
                    b = 5 + which
                    for k, m in enumerate(ms):
                        s.op(PE, lambda e, b=b, m=m, n=n, k=k, ms=ms: e.matmul(
                            out=psum[b][:, 0:n], lhsT=ones, rhs=SQ[:, m, 0:n], start=(k == 0), stop=(k == len(ms) - 1)),
                            reads=[("SQ", m), "CONST"], writes=[("ps", b)])
                    RS = sc(lay["RSTD"][which], 512, F32)
                    s.op(ACT, lambda e, b=b, n=n, RS=RS, dim=dim: e.activation(
                        out=RS[:, 0:n], in_=psum[b][:, 0:n], func=AF.Ln, scale=1.0 / dim, bias=EPSC),
                        reads=[("ps", b), "GAINS"], writes=[("RSTD", which)])
                    s.op(ACT, lambda e, n=n, RS=RS: e.activation(out=RS[:, 0:n], in_=RS[:, 0:n], func=AF.Exp, scale=-0.5),
                        reads=[("RSTD", which)], writes=[("RSTD", which)])
                    for k, m in enumerate(ms):
                        if which == 0:
                            dst = CQ[:, k, g * nq:(g + 1) * nq]
                            wk = ("CQ", g)
                        else:
                            dst = CKV[:, k, g * 512:(g + 1) * 512]
                            wk = ("CKV", g)
                        s.op(DVE, lambda e, m=m, n=n, RS=RS, dst=dst, goff=goff, k=k: e.scalar_tensor_tensor(
                            out=dst, in0=CST[:, m, 0:n], scalar=GAINS[:, goff + k: goff + k + 1], in1=RS[:, 0:n],
                            op0=ALU.mult, op1=ALU.mult), reads=[("CST", m), ("RSTD", which), "GAINS"], writes=[wk])
                for k, (col0, b) in enumerate(((0, 7), (128, 4))):
                    for c in range(8):
                        s.op(PE, lambda e, c=c, b=b, col0=col0, XNT=XNT: e.matmul(
                            out=psum[b][:, :], lhsT=WIO3[:, c, col0:col0 + 128], rhs=XNT[:, c, :],
                            start=(c == 0), stop=(c == 7)), reads=[("XNT", xb, c), "WIOK"], writes=[("ps", b)])
                    T12 = sc(lay["T12"][k], 512, F32)
                    s.op(DVE, lambda e, b=b, k=k, T12=T12, ROPE=ROPE: e.tensor_tensor(
                        out=T12, in0=psum[b][:, :], in1=ROPE[:, k, :], op=ALU.mult),
                        reads=[("ps", b), ("ROPE", rb)], writes=[("T12", k)])
                Ta = sc(lay["T12"][0], 512, F32)
                Tb = sc(lay["T12"][1], 512, F32)
                s.op(DVE, lambda e, g=g, Ta=Ta, Tb=Tb: e.tensor_tensor(
                    out=KR[:, g * 512:(g + 1) * 512], in0=Ta, in1=Tb, op=ALU.add),
                    reads=[("T12", 0), ("T12", 1)], writes=[("KR", g)])

            s.barrier()
            load_w_out()
            if "B" not in phases:
                return
            KH = sc(lay["KH"], S, BF16)
            VH = sc(lay["VH"], S, BF16).rearrange("p (t d) -> p t d", d=128)
            QN = sc(lay["QN"], Q, BF16)
            QRp = sc(lay["QR"], Q, BF16)
            RD = sc(lay["RD"], 512, F32)
            if ji == 0:
                for j in range(NJ):
                    s.op(POOL, lambda e, j=j: e.dma_start(out=wg16[j], in_=w_g_d[j]), writes=["W16"], dma=True)
                    s.op(POOL, lambda e, j=j: e.dma_start(out=wu16[j], in_=w_u_d[j]), writes=["W16"], dma=True)
                for hf in range(2):
                    for j in range(NJ):
                        s.op(POOL, lambda e, j=j, hf=hf: e.dma_start(out=wd16[hf, j], in_=w_d_d[hf, j]), writes=["W16"], dma=True)
            allCKV = [("CKV", g) for g in range(NG)]
            allKR = [("KR", g) for g in range(NG)]
            allCQ = [("CQ", g) for g in range(NG)]
            dbl = False
            if dbl:
                KHb = [sc(lay["KH"] + p_ * S * 2, S, BF16) for p_ in range(2)]
                VHb = [sc(lay["VH"] + p_ * S * 2, S, BF16).rearrange("p (t d) -> p t d", d=128) for p_ in range(2)]
                QNb = [QN, sc(lay["VH"] + 2 * S * 2, Q, BF16)]
                QRb = [QRp, sc(lay["VH"] + 2 * S * 2 + Q * 2, Q, BF16)]
            else:
                KHb, VHb, QNb, QRb = [KH, KH], [VH, VH], [QN, QN], [QRp, QRp]

            def gen_units(h):
                par = h % 2 if dbl else 0
                KH_, VH_, QN_, QR_ = KHb[par], VHb[par], QNb[par], QRb[par]
                units = []
                for g in range(NG):
                    bk = 0 if dbl else (0, 2)[g % 2]
                    bv = 1 if dbl else (3, 4)[g % 2]

                    def uk(g=g, bk=bk):
                        for c in range(2):
                            s.op(PE, lambda e, c=c: e.matmul(
                                out=psum[bk][:, :], lhsT=WKV3[:, c, h * 256:h * 256 + 128],
                                rhs=CKV[:, c, g * 512:(g + 1) * 512], start=(c == 0), stop=(c == 1)),
                                reads=[("CKV", g), "WKV"], writes=[("ps", bk)])
                        s.op(ACT, lambda e: e.activation(out=KH_[:, g * 512:(g + 1) * 512], in_=psum[bk][:, :], func=AF.Copy),
                             reads=[("ps", bk)], writes=[("KH", par, g)])

                    def uv(g=g, bv=bv):
                        for i in range(4):
                            for c in range(2):
                                s.op(PE, lambda e, c=c, i=i: e.matmul(
                                    out=psum[bv][:, i * 128:(i + 1) * 128],
                                    lhsT=CKV[:, c, (g * 4 + i) * 128:(g * 4 + i + 1) * 128],
                                    rhs=WKV3[:, c, h * 256 + 128:h * 256 + 256], start=(c == 0), stop=(c == 1)),
                                    reads=[("CKV", g), "WKV"], writes=[("ps", bv)])
                        s.op(DVE, lambda e: e.tensor_scalar(
                            out=VH_[:, g * 4:(g + 1) * 4, :], in0=psum[bv][:, :].rearrange("p (t d) -> p t d", d=128),
                            scalar1=1.0, scalar2=None, op0=ALU.mult),
                            reads=[("ps", bv)], writes=[("VH", par, g)])
                    units.append(uk)
                    units.append(uv)
                for qc in range(NQC):
                    b = 0 if dbl else (0, 2)[qc % 2]
                    b2 = 1 if dbl else 3 + qc % 2

                    def uqn(qc=qc, b=b):
                        for c in range(3):
                            s.op(PE, lambda e, c=c: e.matmul(
                                out=psum[b][:, :], lhsT=WQN4[:, c, h, :],
                                rhs=CQ[:, c, qc * 512:(qc + 1) * 512], start=(c == 0), stop=(c == 2)),
                                reads=allCQ + ["WQ"], writes=[("ps", b)])
                        s.op(ACT, lambda e: e.activation(out=QN_[:, qc * 512:(qc + 1) * 512], in_=psum[b][:, :], func=AF.Copy),
                             reads=[("ps", b)], writes=[("QN", par, qc)])

                    def uqr(qc=qc, b2=b2):
                        rb = qc % 2
                        RQ = sc(lay["ROPEB"][rb], 512, F32)
                        s.op(SP, lambda e: e.dma_start(out=RQ, in_=ropeq[:, qc * 512:(qc + 1) * 512]),
                             writes=[("ROPEB", rb)], dma=True)
                        for c in range(3):
                            s.op(PE, lambda e, c=c: e.matmul(
                                out=psum[b2][:, :], lhsT=WQRR4[:, c, h, :], rhs=CQ[:, c, qc * 512:(qc + 1) * 512],
                                start=(c == 0), stop=(c == 2)),
                                reads=allCQ + ["WQRR", "WQRR2"], writes=[("ps", b2)])
                        s.op(DVE, lambda e: e.tensor_tensor(
                            out=QR_[:, qc * 512:(qc + 1) * 512], in0=psum[b2][:, :], in1=RQ, op=ALU.mult),
                            reads=[("ps", b2), ("ROPEB", rb)], writes=[("QR", par, qc)])
                    units.append(uqn)
                    units.append(uqr)
                return units

            def attention(h, pending):
                par = h % 2 if dbl else 0
                KH_, VH_, QN_, QR_ = KHb[par], VHb[par], QNb[par], QRb[par]
                for qp in range(NQC // 2):
                    qcs = (2 * qp, 2 * qp + 1)
                    obs = (6, 7)
                    db = 1

                    def qk(kt, u):
                        qc = qcs[u]
                        qs = slice(qc * 512, (qc + 1) * 512)
                        bsc = 2 + 2 * u + kt % 2
                        s.op(PE, lambda e, bsc=bsc, kt=kt, qs=qs: e.matmul(
                            out=psum[bsc][:, :], lhsT=KH_[:, kt * 128:(kt + 1) * 128], rhs=QN_[:, qs],
                            start=True, stop=False),
                            reads=[("KH", par, kt // 4), ("QN", par, qc)], writes=[("ps", bsc)])
                        s.op(PE, lambda e, bsc=bsc, kt=kt, qs=qs: e.matmul(
                            out=psum[bsc][:, :], lhsT=KR[:, kt * 128:(kt + 1) * 128], rhs=QR_[:, qs],
                            start=False, stop=True),
                            reads=[("KR", kt // 4), ("QR", par, qc)], writes=[("ps", bsc)])

                    qk(0, 0)
                    qk(0, 1)
                    for kt in range(N2):
                        if kt + 1 < N2:
                            qk(kt + 1, 0)
                            qk(kt + 1, 1)
                        for u in range(2):
                            bsc = 2 + 2 * u + kt % 2
                            pb = 2 * u + kt % 2
                            PT = sc(lay["PT"][pb], 512, BF16)
                            s.op(ACT, lambda e, bsc=bsc, PT=PT: e.activation(out=PT, in_=psum[bsc][:, :], func=AF.Exp, scale=SM_SCALE),
                                 reads=[("ps", bsc)], writes=[("PT", pb)])
                            s.op(PE, lambda e, kt=kt, PT=PT, u=u: e.matmul(
                                out=psum[obs[u]][:, :], lhsT=VH_[:, kt, :], rhs=PT, start=(kt == 0), stop=(kt == N2 - 1)),
                                reads=[("VH", par, kt // 4), ("PT", pb)], writes=[("ps", obs[u])])
                            ab = 2 * u + kt % 2
                            ACC = sc(lay["ACC"][ab], 512, F32)
                            if kt < 2:
                                s.op(DVE, lambda e, PT=PT, ACC=ACC: e.tensor_copy(out=ACC, in_=PT),
                                     reads=[("PT", pb)], writes=[("ACC", ab)])
                            else:
                                s.op(DVE, lambda e, PT=PT, ACC=ACC: e.tensor_tensor(out=ACC, in0=ACC, in1=PT, op=ALU.add),
                                     reads=[("PT", pb), ("ACC", ab)], writes=[("ACC", ab)])
                        if pending and kt % 2 == 1:
                            pending.pop(0)()
                    OSBs = []
                    for u in range(2):
                        OSB = sc(lay["OSB"][u], 512, F32)
                        s.op(DVE, lambda e, u=u, OSB=OSB: e.tensor_copy(out=OSB, in_=psum[obs[u]][:, :]),
                             reads=[("ps", obs[u])], writes=[("OSB", u)])
                        OSBs.append(OSB)
                    ACSs = [sc(lay["ACC"][4], 512, F32), sc(lay["T12B"][0], 512, F32)]
                    for u in range(2):
                        ACC0 = sc(lay["ACC"][2 * u], 512, F32)
                        ACC1 = sc(lay["ACC"][2 * u + 1], 512, F32)
                        ACS = ACSs[u]
                        if N2 > 1:
                            s.op(DVE, lambda e, ACC0=ACC0, ACC1=ACC1, ACS=ACS: e.tensor_tensor(out=ACS, in0=ACC0, in1=ACC1, op=ALU.add),
                                 reads=[("ACC", 2 * u), ("ACC", 2 * u + 1)], writes=[("ACS", u)])
                        else:
                            s.op(DVE, lambda e, ACC0=ACC0, ACS=ACS: e.tensor_copy(out=ACS, in_=ACC0),
                                 reads=[("ACC", 2 * u)], writes=[("ACS", u)])

                    def tail_rest(qcs=qcs, OSBs=OSBs, ACSs=ACSs, db=db, h=h):
                        for u in range(2):
                            qc = qcs[u]
                            qs = slice(qc * 512, (qc + 1) * 512)
                            s.op(PE, lambda e, u=u: e.matmul(out=psum[db][:, :], lhsT=ONESF[:, :], rhs=ACSs[u], start=True, stop=True),
                                 reads=[("ACS", u), "ONESF"], writes=[("ps", db)])
                            s.op(ACT, lambda e: e.activation(out=RD, in_=psum[db][:, :], func=AF.Ln),
                                 reads=[("ps", db)], writes=["RD"])
                            s.op(ACT, lambda e: e.activation(out=RD, in_=RD, func=AF.Exp, scale=-1.0),
                                 reads=["RD"], writes=["RD"])
                            s.op(DVE, lambda e, u=u, qs=qs: e.tensor_tensor(
                                out=CAT3[:, 2 + h, qs], in0=OSBs[u], in1=RD, op=ALU.mult),
                                reads=[("OSB", u), "RD"], writes=[("CAT", 2 + h, qc)])
                    pending.insert(0, tail_rest)
                return pending

            for u_ in gen_units(0):
                u_()
            left = []
            for h in range(NH):
                left = attention(h, left)
                if h + 1 < NH:
                    for u_ in gen_units(h + 1):
                        u_()
            while left:
                left.pop(0)()

            s.barrier()
            if "C" not in phases:
                return
            HT = sc(lay["HT"], NJ * TC, BF16).rearrange("p (j n) -> p j n", j=NJ)
            NSUB = TC // 512
            for tg in range(Q // TC):
                XNTs = []
                for sub in range(NSUB):
                    srcs = []
                    for i in range(4):
                        ti = sub * 4 + i
                        qt = tg * (TC // 128) + ti
                        xtb = xtc[0] % NXT
                        xtc[0] += 1
                        XT = sc(lay["XT"][xtb], 1024, F32)
                        npt = 128 // QP
                        for a_ in range(npt):
                            t = qt * npt + a_
                            src_ = xd[xbase + own_off * N2 + t: xbase + (own_off + QP - 1) * N2 + t + 1: N2, :]
                            s.op(SP, lambda e, src_=src_, a_=a_, XT=XT: e.dma_start(out=XT[a_ * QP:(a_ + 1) * QP, :], in_=src_),
                                 writes=[("XT", xtb)], dma=True)
                        X1 = sc(lay["X1"][ti], 1024, F32)
                        for hf in range(2):
                            b = 4 + hf + 2 * (i % 2)
                            for c in range(8):
                                s.op(PE, lambda e, b=b, c=c, qt=qt, hf=hf: e.matmul(
                                    out=psum[b][:, :], lhsT=CAT3[:, c, qt * 128:(qt + 1) * 128],
                                    rhs=WIO3[:, c, hf * 512:(hf + 1) * 512], start=(c == 0), stop=(c == 7)),
                                    reads=[("CAT", c, qt // 4), "WIO"], writes=[("ps", b)])
                            s.op(DVE, lambda e, b=b, hf=hf, X1=X1, XT=XT: e.tensor_tensor(
                                out=X1[:, hf * 512:(hf + 1) * 512], in0=psum[b][:, :], in1=XT[:, hf * 512:(hf + 1) * 512],
                                op=ALU.add), reads=[("ps", b), ("XT", xtb)], writes=[("X1", ti)])
                        srcs.append((X1, ("X1", ti)))
                    XNTs.append(norm_front(srcs))
                XNTs = [norm_back(h_, 8) for h_ in XNTs]
                for j in range(NJ):
                    wb = j % 2
                    WG = sc(lay["WG"][wb], 1024, BF16).rearrange("p (c n) -> p c n", c=8)
                    WU = sc(lay["WU"][wb], 1024, BF16).rearrange("p (c n) -> p c n", c=8)
                    s.op(SP, lambda e, j=j, WG=WG: e.dma_start(out=WG, in_=wg16[j]), reads=["W16"], writes=[("WG", wb)], dma=True)
                    s.op(SP, lambda e, j=j, WU=WU: e.dma_start(out=WU, in_=wu16[j]), reads=["W16"], writes=[("WU", wb)], dma=True)
                    for sub in range(NSUB):
                        XNT, xb = XNTs[sub]
                        bg = 4 * (j % 2) + sub
                        bu = 4 * (j % 2) + 2 + sub
                        for c in range(8):
                            s.op(PE, lambda e, c=c, bg=bg, WG=WG, XNT=XNT: e.matmul(
                                out=psum[bg][:, :], lhsT=WG[:, c, :], rhs=XNT[:, c, :], start=(c == 0), stop=(c == 7)),
                                reads=[("WG", wb), ("XNT", xb, c)], writes=[("ps", bg)])
                        for c in range(8):
                            s.op(PE, lambda e, c=c, bu=bu, WU=WU, XNT=XNT: e.matmul(
                                out=psum[bu][:, :], lhsT=WU[:, c, :], rhs=XNT[:, c, :], start=(c == 0), stop=(c == 7)),
                                reads=[("WU", wb), ("XNT", xb, c)], writes=[("ps", bu)])
                        sgb = 0
                        SG = sc(lay["SG"][sgb], 512, F32)
                        s.op(ACT, lambda e, bg=bg, SG=SG: e.activation(out=SG, in_=psum[bg][:, :], func=AF.Silu),
                             reads=[("ps", bg)], writes=[("SG", sgb)])
                        s.op(DVE, lambda e, bu=bu, SG=SG, j=j, sub=sub: e.tensor_tensor(
                            out=HT[:, j, sub * 512:(sub + 1) * 512], in0=psum[bu][:, :], in1=SG, op=ALU.mult),
                            reads=[("ps", bu), ("SG", sgb)], writes=[("HT", j, sub)])
                NT8 = TC // 128
                for hf in range(2):
                    for j in range(NJ):
                        wb = (hf * NJ + j) % 2
                        WD = sc(lay["WD"][wb], 512, BF16)
                        s.op(SP, lambda e, j=j, hf=hf, WD=WD: e.dma_start(out=WD, in_=wd16[hf, j]),
                             reads=["W16"], writes=[("WD", wb)], dma=True)
                        for i in range(NT8):
                            s.op(PE, lambda e, i=i, j=j, WD=WD: e.matmul(
                                out=psum[i][:, :], lhsT=HT[:, j, i * 128:(i + 1) * 128], rhs=WD,
                                start=(j == 0), stop=(j == NJ - 1)),
                                reads=[("HT", j, i // 4), ("WD", wb)], writes=[("ps", i)])
                    for i in range(NT8):
                        X1 = sc(lay["X1"][i], 1024, F32)
                        s.op(DVE, lambda e, i=i, hf=hf, X1=X1: e.tensor_tensor(
                            out=X1[:, hf * 512:(hf + 1) * 512], in0=psum[i][:, :], in1=X1[:, hf * 512:(hf + 1) * 512],
                            op=ALU.add), reads=[("ps", i), ("X1", i)], writes=[("X1", i)])
                for sub in range(NSUB):
                    SSf = STAT[:, 16 + 8 * sub:20 + 8 * sub]
                    RSf = STAT[:, 20 + 8 * sub:24 + 8 * sub]
                    kssf = [("ssf", sub, i) for i in range(4)]
                    krsf = ("rsf", sub)
                    for i in range(4):
                        ti = sub * 4 + i
                        X1 = sc(lay["X1"][ti], 1024, F32)
                        YOj = sc(lay["XS"][i % NXS], 1024, BF16)
                        s.op(ACT, lambda e, X1=X1, SSf=SSf, YOj=YOj, i=i: e.activation(
                            out=YOj, in_=X1, func=AF.Square, accum_out=SSf[:, i:i + 1]),
                            reads=[("X1", ti), kssf[i]], writes=[kssf[i], ("XS", i % NXS)])
                    s.op(ACT, lambda e, SSf=SSf, RSf=RSf: e.activation(out=RSf, in_=SSf, func=AF.Ln, scale=1.0 / D, bias=EPSC),
                         reads=kssf + ["GAINS"], writes=[krsf])
                    s.op(ACT, lambda e, RSf=RSf: e.activation(out=RSf, in_=RSf, func=AF.Exp, scale=-0.5), reads=[krsf], writes=[krsf])
                    for i in range(4):
                        ti = sub * 4 + i
                        qt = tg * (TC // 128) + ti
                        X1 = sc(lay["X1"][ti], 1024, F32)
                        s.op(DVE, lambda e, X1=X1, RSf=RSf, i=i: e.scalar_tensor_tensor(
                            out=X1, in0=X1, scalar=RSf[:, i:i + 1], in1=GFIN[:, :], op0=ALU.mult, op1=ALU.mult),
                            reads=[("X1", ti), krsf, "GFIN"], writes=[("X1", ti)])
                        s.op(SP, lambda e, qt=qt, X1=X1: e.dma_start(out=yd[qt * 128:(qt + 1) * 128, :], in_=X1),
                             reads=[("X1", ti)], dma=True)
        for ji, jb in enumerate(jobs):
            do_job(ji, jb)
        s.barrier()

        s.prepare()

        @block.sync
        def _(e):
            s.emit(SP, e, sems, dsems, bsems)

        @block.gpsimd
        def _(e):
            s.emit(POOL, e, sems, dsems, bsems)

        @block.scalar
        def _(e):
            s.emit(ACT, e, sems, dsems, bsems)

        @block.vector
        def _(e):
            s.emit(DVE, e, sems, dsems, bsems)

        @block.tensor
        def _(e):
            s.emit(PE, e, sems, dsems, bsems)

    return nc


def KH_ok(lay, smax):
    return True


def _tables(S, Q, s1_of_p, own_k2_0):
    N2 = S // 128
    K2O = Q // 128
    t = np.arange(N2)[:, None]
    tok = (N2 * np.asarray(s1_of_p)[None, :] + t).astype(np.int64)
    k1 = np.arange(128, dtype=np.int64)
    ph = (tok[:, :, None] * k1[None, None, :]) % S
    ang = 2.0 * np.pi * ph.astype(np.float64) / S
    etab = np.concatenate([np.cos(ang), -np.sin(ang)], axis=2).reshape(S, 256).astype(np.float32)
    k2 = own_k2_0 + np.arange(K2O, dtype=np.int64)
    phi = 2.0 * np.pi * ((np.arange(N2, dtype=np.int64)[:, None] * k2[None, :]) % N2) / N2
    f2c = np.zeros((N2, 2, 2, K2O), np.float64)
    f2c[:, 0, 0, :] = np.cos(phi)
    f2c[:, 1, 0, :] = np.sin(phi)
    f2c[:, 0, 1, :] = -np.sin(phi)
    f2c[:, 1, 1, :] = np.cos(phi)
    f2c = f2c.reshape(2 * N2, 2 * K2O).astype(np.float32)
    inv_freq = (1.0 / (10000.0 ** (np.arange(0, 64, 2, dtype=np.float32) / 64.0))).astype(np.float32)
    pos = tok.reshape(-1).astype(np.float32)
    angr = pos[None, :] * inv_freq[:, None]
    cosr = np.cos(angr).astype(np.float32)
    sinr = np.sin(angr).astype(np.float32)
    ropec = np.stack([np.concatenate([cosr, cosr, cosr, cosr], 0),
                      np.concatenate([sinr, sinr, sinr, sinr], 0)], 0)
    return etab, f2c, np.ascontiguousarray(ropec)


def _rope_q(ropec, S, Q):
    N2 = S // 128
    QP = Q // N2
    r = ropec.reshape(2, 128, N2, 128)[:, 0:64, :, :QP].reshape(2, 64, Q)
    return np.ascontiguousarray(np.concatenate([r[0], r[1]], 0))


def _consts():
    ident = np.eye(128, dtype=np.float32)
    ones = np.ones((128, 128), np.float32)
    c = np.arange(64)
    psi = 2.0 * np.pi * ((c[:, None] * c[None, :]) % 64) / 64.0
    Cc = np.zeros((128, 128), np.float64)
    Sc = np.zeros((128, 128), np.float64)
    for g in range(2):
        Cc[g * 64:(g + 1) * 64, g * 64:(g + 1) * 64] = np.cos(psi)
        Sc[g * 64:(g + 1) * 64, g * 64:(g + 1) * 64] = np.sin(psi)
    return np.ascontiguousarray(np.stack([ident, ones, Cc.astype(np.float32), Sc.astype(np.float32)], 1))


def _weight_maps(inp):
    f = lambda a: np.ascontiguousarray(np.asarray(a, dtype=np.float32))
    w_in = f(inp["w_in"])[0]
    m = {}
    m["w_in_r"] = f(w_in.reshape(8, 128, 960).transpose(1, 0, 2))
    m["w_q_r"] = f(f(inp["w_q_up"])[0].reshape(3, 128, 1152).transpose(1, 0, 2))
    m["w_kv_r"] = f(f(inp["w_kv_up"])[0].reshape(2, 128, 1536).transpose(1, 0, 2))
    m["w_out_r"] = f(f(inp["w_out"])[0].reshape(8, 128, 1024).transpose(1, 0, 2))
    m["w_gate_r"] = f(f(inp["w_gate"])[0].reshape(8, 128, NJ, 128).transpose(2, 1, 0, 3))
    m["w_up_r"] = f(f(inp["w_up"])[0].reshape(8, 128, NJ, 128).transpose(2, 1, 0, 3))
    m["w_down_r"] = f(f(inp["w_down"])[0].reshape(NJ, 128, 2, 512).transpose(2, 0, 1, 3))
    g = np.zeros((128, 24), np.float32)
    g[:, 0:8] = f(inp["norm_mix_g"])[0].reshape(8, 128).T
    g[:, 8:16] = f(inp["norm_ffn_g"])[0].reshape(8, 128).T
    g[:, 16:19] = f(inp["q_norm_g"])[0].reshape(3, 128).T
    g[:, 19:21] = f(inp["kv_norm_g"])[0].reshape(2, 128).T
    g[:, 21] = EPS
    m["gains"] = g
    m["gfin_bc"] = f(np.broadcast_to(f(inp["final_norm_g"])[None, :], (128, 1024)))
    m["consts"] = _consts()
    return m


SP_, QQ = 8192, 2048
SS_ = 2048


def _job_list():
    jobs = []
    QPp = QQ // (SP_ // 128)
    jobs.append(dict(S=SP_, Q=QQ, x="xp", xrows=SP_, xbase=0, perm_segs=[(0, 0, 128)], own_off=0, tab="p0", y="y0"))
    jobs.append(dict(S=SP_, Q=QQ, x="xp", xrows=SP_, xbase=0,
                     perm_segs=[(0, QPp, QPp), (QPp, 0, QPp), (2 * QPp, 2 * QPp, 128 - 2 * QPp)],
                     own_off=QPp, tab="p1", y="y1"))
    for k in range(4):
        jobs.append(dict(S=SS_, Q=QQ, x="xs", xrows=4 * SS_, xbase=k * SS_, perm_segs=[(0, 0, 128)], own_off=0,
                         tab="s", y="y%d" % (2 + k)))
    return jobs


def kernel(**inputs):
    xp = np.asarray(inputs["x_prompt"], dtype=np.float32)
    xs = np.asarray(inputs["x_sample"], dtype=np.float32)
    wm = _weight_maps(inputs)
    jobs = _job_list()
    nc = build_program(jobs)
    N2p = SP_ // 128
    QPp = QQ // N2p
    in_maps = []
    sigmas = []
    tabs_s = _tables(SS_, QQ, np.arange(128), 0)
    for core in range(8):
        b, half = core // 2, core % 2
        q0, q1 = 2 * half, 2 * half + 1
        own0 = np.arange(q0 * QPp, (q0 + 1) * QPp)
        own1 = np.arange(q1 * QPp, (q1 + 1) * QPp)
        rest = np.array([v for v in range(128) if v // QPp not in (q0, q1)])
        sigma = np.concatenate([own0, own1, rest])
        sigmas.append(sigma)
        m = dict(wm)
        xb = xp[b].reshape(128, N2p, D)[sigma].reshape(SP_, D)
        m["xp"] = np.ascontiguousarray(xb)
        m["xs"] = np.ascontiguousarray(xs[4 * core:4 * core + 4].reshape(4 * SS_, D))
        s1p0 = sigma
        s1p1 = np.concatenate([sigma[QPp:2 * QPp], sigma[0:QPp], sigma[2 * QPp:]])
        for tbn, s1p, qr in (("p0", s1p0, q0), ("p1", s1p1, q1)):
            et, f2, rc = _tables(SP_, QQ, s1p, qr * (QQ // 128))
            m["etab_" + tbn] = et
            m["f2c_" + tbn] = f2
            m["ropec_" + tbn] = rc
            m["ropeq_" + tbn] = _rope_q(rc, SP_, QQ)
        m["etab_s"], m["f2c_s"], m["ropec_s"] = tabs_s
        m["ropeq_s"] = _rope_q(tabs_s[2], SS_, QQ)
        in_maps.append(m)
    res = run_bass_kernel_spmd(nc, in_maps, core_ids=list(range(8)))
    y_prompt = np.zeros((4, SP_, D), np.float32)
    y_sample = np.zeros((32, SS_, D), np.float32)
    for core in range(8):
        r = res.results[core]
        b, half = core // 2, core % 2
        for jj in range(2):
            qr = 2 * half + jj
            y = np.asarray(r["y%d" % jj]).reshape(N2p, QPp, D)
            y_prompt[b].reshape(128, N2p, D)[qr * QPp:(qr + 1) * QPp] = y.transpose(1, 0, 2)
        for k in range(4):
            y = np.asarray(r["y%d" % (2 + k)]).reshape(SS_ // 128, 128, D)
            y_sample[4 * core + k] = y.transpose(1, 0, 2).reshape(SS_, D)
    return (y_prompt, y_sample)
```
